# Optimizing a Trainium2 kernel written in Bass

```python
import math
import jax, jax.numpy as jnp
from jax import lax
import numpy as np

D_MODEL = 1024
BATCH = 8
SEQ = 8192
DEPTH = 2

GRID_W = 64
CTX_LEN = 256
MIX_WIDTH = D_MODEL
F_WIDTH = D_MODEL // 4
F_GROUPS = 4
F_GROUP_DIM = F_WIDTH // F_GROUPS
ATT_WIDTH = D_MODEL // 2
ATT_HEADS = 4
V_DIM = ATT_WIDTH // ATT_HEADS
QK_DIM = V_DIM // 2
ATT_SCALE = QK_DIM ** -0.5
Q_BLOCK = 128
ROPE_HALF = QK_DIM // 2
ROPE_FREQS = ROPE_HALF // 2
ROPE_BASE = 10000.0
LRU_WIDTH = D_MODEL // 4
LRU_BLOCKS = 4
LRU_BLOCK_DIM = LRU_WIDTH // LRU_BLOCKS
LRU_C = 8.0
CONV_W = 4
CONV_LEFT = (CONV_W - 1) // 2
Q_OFF = F_WIDTH
K_OFF = Q_OFF + ATT_HEADS * 2 * QK_DIM
V_OFF = K_OFF + ATT_HEADS * 2 * QK_DIM
Y_OFF = V_OFF + ATT_WIDTH
R_OFF = Y_OFF + LRU_WIDTH
IN_WIDTH = R_OFF + LRU_WIDTH
N_GROUPS = 4
EXPERTS_PER_GROUP = 8
N_EXPERTS = N_GROUPS * EXPERTS_PER_GROUP
TOP_K = 2
D_EXPERT = D_MODEL // 2
MOE_BLOCK = 256
N_MOD = 6
EPS = 1e-6

kernel_name = 'hybrid_fourier_diffattn_rglru_hmoe_dit'


def _rms(x, g):
    xf = x.astype(jnp.float32)
    y = xf * lax.rsqrt(jnp.mean(xf * xf, axis=-1, keepdims=True) + EPS)
    return (y * g.astype(jnp.float32)).astype(x.dtype)


def _modulation(cond, w, b):
    m = jax.nn.silu(cond) @ w + b
    return [m[:, None, i * D_MODEL:(i + 1) * D_MODEL] for i in range(N_MOD)]


def _adaln(x, g, shift, scale):
    return _rms(x, g) * (1 + scale) + shift


def _fourier_mix(u):
    B, T, _ = u.shape
    z = u.astype(jnp.float32).reshape(B, T, F_GROUPS, F_GROUP_DIM)
    y = jnp.fft.fft2(z, axes=(1, 3), norm='ortho').real
    return y.reshape(B, T, F_WIDTH).astype(u.dtype)


def _rope_axis(x, ang):
    cos = jnp.cos(ang)[None, :, None, None, :]
    sin = jnp.sin(ang)[None, :, None, None, :]
    x1, x2 = x[..., :ROPE_FREQS], x[..., ROPE_FREQS:]
    return jnp.concatenate([x1 * cos - x2 * sin, x2 * cos + x1 * sin], axis=-1)


def _rope2d(x, ang_row, ang_col):
    y = jnp.concatenate([_rope_axis(x[..., :ROPE_HALF], ang_row),
                         _rope_axis(x[..., ROPE_HALF:], ang_col)], axis=-1)
    return y.astype(x.dtype)


def _diff_attend(q, k, v, lam):
    s = jnp.einsum('bqhmd,bkhmd->bhmqk', q, k).astype(jnp.float32) * ATT_SCALE
    p = jax.nn.softmax(s, axis=-1)
    a = p[:, :, 0] - lam * p[:, :, 1]
    return jnp.einsum('bhqk,bkhd->bqhd', a.astype(v.dtype), v)


def _diff_head_out(o, subln_g, lam_init):
    B, T = o.shape[0], o.shape[1]
    return (_rms(o, subln_g) * (1 - lam_init)).reshape(B, T, ATT_WIDTH)


def _dwconv_centred(x, w, b):
    T = x.shape[1]
    xp = jnp.pad(x, ((0, 0), (CONV_LEFT, CONV_W - 1 - CONV_LEFT), (0, 0)))
    out = b
    for k in range(CONV_W):
        out = out + xp[:, k:k + T] * w[k]
    return out


def _rglru_coeffs(xr, w_a, b_a, w_x, b_x, lam):
    B, T, _ = xr.shape
    xg = xr.reshape(B, T, LRU_BLOCKS, LRU_BLOCK_DIM)
    r = jax.nn.sigmoid(jnp.einsum('btgi,gio->btgo', xg, w_a).reshape(B, T, LRU_WIDTH) + b_a)
    i = jax.nn.sigmoid(jnp.einsum('btgi,gio->btgo', xg, w_x).reshape(B, T, LRU_WIDTH) + b_x)
    log_a = -LRU_C * r.astype(jnp.float32) * jax.nn.softplus(-lam.astype(jnp.float32))
    a = jnp.exp(log_a)
    bt = jnp.sqrt(-jnp.expm1(2.0 * log_a)) * (i * xr).astype(jnp.float32)
    return a, bt


def _scan_combine(left, right):
    a_l, b_l = left
    a_r, b_r = right
    return a_l * a_r, a_r * b_l + b_r


def _linear_scan(a, b, h0, reverse):
    if h0 is not None:
        idx = -1 if reverse else 0
        b = b.at[:, idx].add(a[:, idx] * h0)
    _, h = lax.associative_scan(_scan_combine, (a, b), reverse=reverse, axis=1)
    return h


def _mixer(hx, hc, p, lam_init, need_ctx):
    B, S, _ = hx.shape
    ux = hx @ p['w_in']
    uc = hc @ p['w_in']

    def split_qkv(u):
        T = u.shape[1]
        q = _rms(u[..., Q_OFF:K_OFF].reshape(B, T, ATT_HEADS, 2, QK_DIM), p['q_norm_g'])
        k = _rms(u[..., K_OFF:V_OFF].reshape(B, T, ATT_HEADS, 2, QK_DIM), p['k_norm_g'])
        v = u[..., V_OFF:Y_OFF].reshape(B, T, ATT_HEADS, V_DIM)
        return q, k, v

    qx, kx, vx = split_qkv(ux)
    qc, kc, vc = split_qkv(uc)
    rows_n = S // GRID_W
    row = jnp.repeat(jnp.arange(rows_n, dtype=jnp.float32), GRID_W)
    col = jnp.tile(jnp.arange(GRID_W, dtype=jnp.float32), rows_n)
    freqs = ROPE_BASE ** (-jnp.arange(ROPE_FREQS, dtype=jnp.float32) / ROPE_FREQS)
    ang_r = row[:, None] * freqs
    ang_c = col[:, None] * freqs
    qx = _rope2d(qx, ang_r, ang_c)
    kx = _rope2d(kx, ang_r, ang_c)
    lam = (jnp.exp(jnp.sum(p['lq1'] * p['lk1']).astype(jnp.float32))
           - jnp.exp(jnp.sum(p['lq2'] * p['lk2']).astype(jnp.float32)) + lam_init)
    k_all = jnp.concatenate([kc, kx], axis=1)
    v_all = jnp.concatenate([vc, vx], axis=1)
    nb = S // Q_BLOCK
    qb = qx.reshape(B, nb, Q_BLOCK, ATT_HEADS, 2, QK_DIM).transpose(1, 0, 2, 3, 4, 5)
    ob = lax.map(lambda qq: _diff_attend(qq, k_all, v_all, lam), qb)
    att_x = _diff_head_out(ob.transpose(1, 0, 2, 3, 4).reshape(B, S, ATT_HEADS, V_DIM),
                           p['subln_g'], lam_init)

    def coeffs(xr, d):
        return _rglru_coeffs(xr, p['gate_a_w'][d], p['gate_a_b'][d], p['gate_x_w'][d],
                             p['gate_x_b'][d], p['lru_lambda'][d])

    xr_c = _dwconv_centred(uc[..., R_OFF:], p['conv_w'], p['conv_b'])
    xr_x = _dwconv_centred(ux[..., R_OFF:], p['conv_w'], p['conv_b'])
    a, bt = coeffs(xr_c, 0)
    hc_f = _linear_scan(a, bt, None, False)
    a, bt = coeffs(xr_c, 1)
    hc_b = _linear_scan(a, bt, None, True)
    a, bt = coeffs(xr_x, 0)
    hx_f = _linear_scan(a, bt, hc_f[:, -1], False)
    a, bt = coeffs(xr_x, 1)
    hx_b = _linear_scan(a, bt, hc_b[:, 0], True)
    rec_x = (jax.nn.gelu(ux[..., Y_OFF:R_OFF]).astype(jnp.float32) * (hx_f + hx_b)).astype(hx.dtype)

    y_x = jnp.concatenate([_fourier_mix(ux[..., :F_WIDTH]), att_x, rec_x], axis=-1) @ p['w_out']
    if not need_ctx:
        return y_x, None
    att_c = _diff_head_out(_diff_attend(qc, kc, vc, lam), p['subln_g'], lam_init)
    rec_c = (jax.nn.gelu(uc[..., Y_OFF:R_OFF]).astype(jnp.float32) * (hc_f + hc_b)).astype(hc.dtype)
    y_c = jnp.concatenate([_fourier_mix(uc[..., :F_WIDTH]), att_c, rec_c], axis=-1) @ p['w_out']
    return y_x, y_c


def _hier_moe(xt, w_group, b_group, w_router, b_router, w1, w3, w2):
    N, D = xt.shape
    gp = jax.nn.softmax((xt @ w_group).astype(jnp.float32) + b_group, axis=-1)
    g_idx = jnp.argmax(gp, axis=-1).astype(jnp.int32)
    p_g = jnp.take_along_axis(gp, g_idx[:, None], axis=1)
    el = (xt @ w_router).astype(jnp.float32) + b_router
    cols = g_idx[:, None] * EXPERTS_PER_GROUP + jnp.arange(EXPERTS_PER_GROUP, dtype=jnp.int32)[None]
    el_g = jnp.take_along_axis(el, cols, axis=1)
    top_v, top_i = lax.top_k(el_g, TOP_K)
    wts = jax.nn.softmax(top_v, axis=-1) * p_g
    eid = g_idx[:, None] * EXPERTS_PER_GROUP + top_i.astype(jnp.int32)

    A = N * TOP_K
    eid_f = eid.reshape(A)
    tok_f = jnp.repeat(jnp.arange(N, dtype=jnp.int32), TOP_K)
    w_f = wts.reshape(A)
    order = jnp.argsort(eid_f)
    se, st, sw = eid_f[order], tok_f[order], w_f[order]
    counts = jnp.zeros((N_EXPERTS,), jnp.int32).at[eid_f].add(1)
    starts = jnp.cumsum(counts) - counts
    padded = (counts + MOE_BLOCK - 1) // MOE_BLOCK * MOE_BLOCK
    pends = jnp.cumsum(padded)
    pstarts = pends - padded
    dest = pstarts[se] + jnp.arange(A, dtype=jnp.int32) - starts[se]
    n_blocks = -(-A // MOE_BLOCK) + N_EXPERTS
    P = n_blocks * MOE_BLOCK
    slot_tok = jnp.full((P,), N, jnp.int32).at[dest].set(st)
    slot_w = jnp.zeros((P,), jnp.float32).at[dest].set(sw)
    block_exp = jnp.minimum(jnp.searchsorted(pends, jnp.arange(n_blocks, dtype=jnp.int32) * MOE_BLOCK,
                                             side='right'), N_EXPERTS - 1).astype(jnp.int32)
    xpad = jnp.concatenate([xt, jnp.zeros((1, D), xt.dtype)], axis=0)

    def body(acc, blk):
        e, toks, ws = blk
        xb = xpad[toks]
        hb = jax.nn.silu(xb @ w1[e]) * (xb @ w3[e])
        yb = (hb @ w2[e]) * ws[:, None].astype(xb.dtype)
        return acc.at[toks].add(yb), None

    acc, _ = lax.scan(body, jnp.zeros((N + 1, D), xt.dtype),
                      (block_exp, slot_tok.reshape(n_blocks, MOE_BLOCK), slot_w.reshape(n_blocks, MOE_BLOCK)))
    return acc[:N]


def setup_inputs(seed: int = 0) -> dict:
    key = jax.random.key(seed)
    ks = jax.random.split(key, 32)
    L, D = DEPTH, D_MODEL

    def nrm(k, shape, s):
        return jax.random.normal(k, shape, jnp.float32) * s

    u = jax.random.uniform(ks[22], (L, 2, LRU_WIDTH), jnp.float32, minval=0.9, maxval=0.999)
    s_lru = u ** (1.0 / LRU_C)
    return {
        'x': nrm(ks[0], (BATCH, SEQ, D), 1.0),
        'c': nrm(ks[1], (BATCH, D), 1.0),
        'ctx': nrm(ks[2], (BATCH, CTX_LEN, D), 1.0),
        'c_ctx': nrm(ks[3], (D,), 1.0),
        'w_mod': nrm(ks[4], (L, D, N_MOD * D), 0.5 * D ** -0.5),
        'b_mod': nrm(ks[5], (L, N_MOD * D), 0.01),
        'norm1_g': 1.0 + nrm(ks[6], (L, D), 0.02),
        'norm2_g': 1.0 + nrm(ks[7], (L, D), 0.02),
        'w_in': nrm(ks[8], (L, D, IN_WIDTH), D ** -0.5),
        'q_norm_g': 1.0 + nrm(ks[9], (L, QK_DIM), 0.02),
        'k_norm_g': 1.0 + nrm(ks[10], (L, QK_DIM), 0.02),
        'lambda_q1': nrm(ks[11], (L, QK_DIM), 0.1),
        'lambda_k1': nrm(ks[12], (L, QK_DIM), 0.1),
        'lambda_q2': nrm(ks[13], (L, QK_DIM), 0.1),
        'lambda_k2': nrm(ks[14], (L, QK_DIM), 0.1),
        'subln_g': 1.0 + nrm(ks[15], (L, V_DIM), 0.02),
        'conv_w': nrm(ks[16], (L, CONV_W, LRU_WIDTH), CONV_W ** -0.5),
        'conv_b': nrm(ks[17], (L, LRU_WIDTH), 0.01),
        'gate_a_w': nrm(ks[18], (L, 2, LRU_BLOCKS, LRU_BLOCK_DIM, LRU_BLOCK_DIM), LRU_BLOCK_DIM ** -0.5),
        'gate_a_b': nrm(ks[19], (L, 2, LRU_WIDTH), 0.01),
        'gate_x_w': nrm(ks[20], (L, 2, LRU_BLOCKS, LRU_BLOCK_DIM, LRU_BLOCK_DIM), LRU_BLOCK_DIM ** -0.5),
        'gate_x_b': nrm(ks[21], (L, 2, LRU_WIDTH), 0.01),
        'lru_lambda': jnp.log(s_lru) - jnp.log1p(-s_lru),
        'w_out': nrm(ks[23], (L, MIX_WIDTH, D), MIX_WIDTH ** -0.5),
        'w_group': nrm(ks[24], (L, D, N_GROUPS), D ** -0.5),
        'b_group': nrm(ks[25], (L, N_GROUPS), 0.01),
        'w_router': nrm(ks[26], (L, D, N_EXPERTS), D ** -0.5),
        'b_router': nrm(ks[27], (L, N_EXPERTS), 0.01),
        'w1': nrm(ks[28], (L, N_EXPERTS, D, D_EXPERT), D ** -0.5),
        'w3': nrm(ks[29], (L, N_EXPERTS, D, D_EXPERT), D ** -0.5),
        'w2': nrm(ks[30], (L, N_EXPERTS, D_EXPERT, D), D_EXPERT ** -0.5),
    }


def reference(x, c, ctx, c_ctx, w_mod, b_mod, norm1_g, norm2_g, w_in, q_norm_g, k_norm_g,
              lambda_q1, lambda_k1, lambda_q2, lambda_k2, subln_g, conv_w, conv_b,
              gate_a_w, gate_a_b, gate_x_w, gate_x_b, lru_lambda, w_out,
              w_group, b_group, w_router, b_router, w1, w3, w2):
    B, S, D = x.shape
    xc = ctx
    for l in range(DEPTH):
        last = l == DEPTH - 1
        lam_init = 0.8 - 0.6 * math.exp(-0.3 * l)
        sh1, sc1, g1, sh2, sc2, g2 = _modulation(c, w_mod[l], b_mod[l])
        csh1, csc1, cg1, csh2, csc2, cg2 = _modulation(c_ctx[None], w_mod[l], b_mod[l])
        p = {
            'w_in': w_in[l], 'q_norm_g': q_norm_g[l], 'k_norm_g': k_norm_g[l],
            'lq1': lambda_q1[l], 'lk1': lambda_k1[l], 'lq2': lambda_q2[l], 'lk2': lambda_k2[l],
            'subln_g': subln_g[l], 'conv_w': conv_w[l], 'conv_b': conv_b[l],
            'gate_a_w': gate_a_w[l], 'gate_a_b': gate_a_b[l], 'gate_x_w': gate_x_w[l],
            'gate_x_b': gate_x_b[l], 'lru_lambda': lru_lambda[l], 'w_out': w_out[l],
        }
        y_x, y_c = _mixer(_adaln(x, norm1_g[l], sh1, sc1), _adaln(xc, norm1_g[l], csh1, csc1),
                          p, lam_init, not last)
        x = x + g1 * y_x
        hx2 = _adaln(x, norm2_g[l], sh2, sc2).reshape(B * S, D)
        if last:
            y = _hier_moe(hx2, w_group[l], b_group[l], w_router[l], b_router[l], w1[l], w3[l], w2[l])
            x = x + g2 * y.reshape(B, S, D)
        else:
            xc = xc + cg1 * y_c
            hc2 = _adaln(xc, norm2_g[l], csh2, csc2).reshape(-1, D)
            nc = hc2.shape[0]
            y = _hier_moe(jnp.concatenate([hc2, hx2], axis=0), w_group[l], b_group[l],
                          w_router[l], b_router[l], w1[l], w3[l], w2[l])
            xc = xc + cg2 * y[:nc].reshape(xc.shape)
            x = x + g2 * y[nc:].reshape(B, S, D)
    return x
```

```python
import math
import contextlib
import numpy as np
import concourse.bass as bass
import concourse.mybir as mybir
from concourse.bass_utils import run_bass_kernel_spmd
from concourse.alu_op_type import AluOpType as ALU

F32 = mybir.dt.float32
BF16 = mybir.dt.bfloat16
I32 = mybir.dt.int32
U32 = mybir.dt.uint32
AF = mybir.ActivationFunctionType
AX = mybir.AxisListType

D = 1024
S = 8192
C = 256
T = S + C
NT = T // 128
DEPTH = 2
INW = 2304
EPS = 1e-6
NE = 32
DE = 512
BLK = 512
NB = (2 * T + BLK - 1) // BLK + NE
NSLOT = NB * BLK


class Eng:
    def __init__(self, fw, name, h):
        self.fw, self.name, self.h = fw, name, h
        self.sem = fw.new_sem("e_" + name)
        self.n = 0
        self.seen = {}


class Stream:
    def __init__(self, fw, name):
        self.sem = fw.new_sem("s_" + name)
        self.n = 0
        self.name = name


class Buf:
    __slots__ = ("name", "w", "r")

    def __init__(self, name):
        self.name = name
        self.w = None
        self.r = {}


class FW:
    def __init__(self, nc):
        self.nc = nc
        self.es = contextlib.ExitStack()
        self.nsem = 0
        self.pe = Eng(self, "pe", nc.tensor)
        self.act = Eng(self, "act", nc.scalar)
        self.dve = Eng(self, "dve", nc.vector)
        self.pool = Eng(self, "pool", nc.gpsimd)
        self.sp = Eng(self, "sp", nc.sync)
        self.engs = [self.pe, self.act, self.dve, self.pool, self.sp]
        self.streams = []

    def new_sem(self, name):
        self.nsem += 1
        return self.es.enter_context(self.nc.semaphore(name))

    def stream(self, name):
        s = Stream(self, name)
        self.streams.append(s)
        return s

    def _need(self, E, dep):
        if dep is None:
            return
        if dep[0] == "E":
            _, Fe, idx = dep
            if Fe is E and E is self.pe:
                return
            if E.seen.get(Fe, 0) >= idx:
                return
            E.h.wait_ge(Fe.sem, idx)
            E.seen[Fe] = idx
        else:
            _, St, idx = dep
            if E.seen.get(St, 0) >= idx:
                return
            E.h.wait_ge(St.sem, 16 * St.n)
            E.seen[St] = St.n

    def _deps(self, E, reads, writes):
        for b in reads:
            self._need(E, b.w)
        for b in writes:
            self._need(E, b.w)
            for d in b.r.values():
                self._need(E, d)

    def op(self, E, fn, reads=(), writes=()):
        self._deps(E, reads, writes)
        ins = fn()
        E.n += 1
        ins.then_inc(E.sem, 1)
        tok = ("E", E, E.n)
        for b in reads:
            b.r[E] = tok
        for b in writes:
            b.w = tok
            b.r = {}
        return ins

    def dma(self, Q, St, fn, reads=(), writes=()):
        self._deps(Q, reads, writes)
        ins = fn()
        St.n += 1
        ins.then_inc(St.sem, 16)
        tok = ("D", St, St.n)
        for b in reads:
            b.r[St] = tok
        for b in writes:
            b.w = tok
            b.r = {}
        return ins

    def barrier(self):
        for E in self.engs:
            for Fe in self.engs:
                if Fe is not E and Fe.n > 0:
                    self._need(E, ("E", Fe, Fe.n))
            for St in self.streams:
                if St.n > 0:
                    self._need(E, ("D", St, St.n))


def bc(ap, shape):
    return ap.broadcast_to(list(shape))


class K:
    pass


def build(debug=()):
    nc = bass.Bass("TRN2", target_bir_lowering=False)
    fw = FW(nc)
    k = K()
    k.nc, k.fw, k.debug = nc, fw, set(debug)
    pe, act, dve, pool, sp = fw.pe, fw.act, fw.dve, fw.pool, fw.sp

    def din(name, shape, dt=F32):
        return nc.dram_tensor(name, list(shape), dt, kind="ExternalInput").ap()

    def dscr(name, shape, dt=F32):
        kind = "ExternalOutput" if name in k.debug else "Internal"
        return nc.dram_tensor(name, list(shape), dt, kind=kind).ap()

    I = {}
    I["xin"] = din("xin", [T, D])
    I["cvec"] = din("cvec", [2, D])
    I["w_mod"] = din("w_mod", [DEPTH, D, 6 * D])
    I["b_mod"] = din("b_mod", [DEPTH, 6 * D])
    I["norm1_g"] = din("norm1_g", [DEPTH, D])
    I["norm2_g"] = din("norm2_g", [DEPTH, D])
    I["w_in"] = din("w_in", [DEPTH, D, INW])
    for n_ in ("q_norm_g", "k_norm_g", "lambda_q1", "lambda_k1", "lambda_q2", "lambda_k2"):
        I[n_] = din(n_, [DEPTH, 64])
    I["subln_g"] = din("subln_g", [DEPTH, 128])
    I["conv_w"] = din("conv_w", [DEPTH, 4, 256])
    I["conv_b"] = din("conv_b", [DEPTH, 256])
    I["gate_a_w"] = din("gate_a_w", [DEPTH, 2, 4, 64, 64])
    I["gate_a_b"] = din("gate_a_b", [DEPTH, 2, 256])
    I["gate_x_w"] = din("gate_x_w", [DEPTH, 2, 4, 64, 64])
    I["gate_x_b"] = din("gate_x_b", [DEPTH, 2, 256])
    I["lru_lambda"] = din("lru_lambda", [DEPTH, 2, 256])
    I["w_out"] = din("w_out", [DEPTH, D, D])
    I["w_group"] = din("w_group", [DEPTH, D, 4])
    I["b_group"] = din("b_group", [DEPTH, 4])
    I["w_router"] = din("w_router", [DEPTH, D, NE])
    I["b_router"] = din("b_router", [DEPTH, NE])
    I["w1"] = din("w1", [DEPTH, NE, D, DE])
    I["w3"] = din("w3", [DEPTH, NE, D, DE])
    I["w2"] = din("w2", [DEPTH, NE, DE, D])
    I["ident"] = din("ident", [128, 128])
    I["rope_cos"] = din("rope_cos", [S, 64])
    I["rope_sin"] = din("rope_sin", [S, 64])
    I["f_cs64"] = din("f_cs64", [64, 128])
    I["f_c128"] = din("f_c128", [128, 64, 128])
    I["f_s128"] = din("f_s128", [128, 64, 128])
    I["f_n128"] = din("f_n128", [128, 64, 128])
    I["f_c256"] = din("f_c256", [256, 256])
    I["f_s256"] = din("f_s256", [256, 256])
    I["f_bdc"] = din("f_bdc", [128, 128])
    I["f_bds"] = din("f_bds", [128, 128])
    k.I = I
    out = nc.dram_tensor("out", [S, D], F32, kind="ExternalOutput").ap()
    k.out = out

    Sc = {}
    Sc["XR"] = dscr("XR", [T, D])
    Sc["MOD"] = dscr("MOD", [DEPTH, 2, 6 * D])
    Sc["UF"] = dscr("UF", [T, 256], BF16)
    Sc["V"] = dscr("V", [T, 512], BF16)
    Sc["QT"] = dscr("QT", [4, 128, T], BF16)
    Sc["KT"] = dscr("KT", [4, 128, T], BF16)
    Sc["YRT"] = dscr("YRT", [512, T])
    Sc["MIXT"] = dscr("MIXT", [1280, T], BF16)
    Sc["HFD"] = dscr("HFD", [256, T])
    Sc["H2T"] = dscr("H2T", [D, T], BF16)
    k.B_MIXT = Buf("MIXT")
    k.Sc = Sc

    es = fw.es
    ident_f = es.enter_context(nc.sbuf_tensor("ident_f", [128, 128], F32))
    ident_b = es.enter_context(nc.sbuf_tensor("ident_b", [128, 128], BF16))
    ones_b = es.enter_context(nc.sbuf_tensor("ones_b", [128, 128], BF16))
    ones_f = es.enter_context(nc.sbuf_tensor("ones_f", [128, 128], F32))
    k.ident_f, k.ident_b, k.ones_b, k.ones_f = ident_f, ident_b, ones_b, ones_f
    k.G_all = es.enter_context(nc.sbuf_tensor("G_all", [128, NT, NE], F32))
    k.B_G = Buf("G_all")
    k.B_OUT = Buf("OUT")
    k.B_const = Buf("const")
    st0 = fw.stream("const")
    k.st_const = st0
    fw.dma(sp, st0, lambda: nc.sync.dma_start(out=ident_f[:], in_=I["ident"][:, :]), writes=[k.B_const])
    fw.op(dve, lambda: nc.vector.tensor_copy(out=ident_b[:], in_=ident_f[:]), writes=[k.B_const])
    fw.op(dve, lambda: nc.vector.memset(ones_b[:], 1.0), writes=[k.B_const])
    fw.op(dve, lambda: nc.vector.memset(ones_f[:], 1.0), writes=[k.B_const])

    k.st = {n_: fw.stream(n_) for n_ in ("ldA0", "ldA1", "ldB0", "ldB1", "ldC0", "ldC1", "stA0", "stA1", "stB0", "stB1",
                                         "stC0", "stC1", "stD0", "stD1", "w0", "w1", "w2", "misc")}

    k.B_XR = [Buf(f"XR{i}") for i in range(NT)]
    for i in range(0, NT, 11):
        fw.dma(sp, k.st["misc"], lambda i=i: nc.sync.dma_start(out=Sc["XR"][i * 128:(i + 11) * 128, :],
                                                               in_=I["xin"][i * 128:(i + 11) * 128, :]),
               writes=k.B_XR[i:i + 11])

    for l in range(DEPTH):
        if "stop_before_mod" in k.debug:
            break
        phase_mod(k, l)
        if f"stop_mod{l}" in k.debug:
            break
        phase_proj(k, l)
        if f"stop_proj{l}" in k.debug:
            break
        phase_lru(k, l)
        if f"stop_lru{l}" in k.debug:
            break
        phase_fourier(k, l)
        if f"stop_fourier{l}" in k.debug:
            break
        phase_attn(k, l)
        if f"stop_attn{l}" in k.debug:
            break
        phase_wout(k, l)
        if f"stop_wout{l}" in k.debug:
            break
        phase_moe(k, l)
        if f"stop_moe{l}" in k.debug:
            break

    fw.barrier()
    fw.es.close()
    return nc


def phase_mod(k, l):
    nc, fw, I, Sc = k.nc, k.fw, k.I, k.Sc
    pe, act, dve, pool, sp = fw.pe, fw.act, fw.dve, fw.pool, fw.sp
    fw.barrier()
    with contextlib.ExitStack() as es:
        ccol = es.enter_context(nc.sbuf_tensor(f"ccol{l}", [128, 2, 8], F32))
        scol = es.enter_context(nc.sbuf_tensor(f"scol{l}", [128, 2, 8], F32))
        brow = es.enter_context(nc.sbuf_tensor(f"brow{l}", [1, 6 * D], F32))
        mrow = es.enter_context(nc.sbuf_tensor(f"mrow{l}", [1, 2, 6 * D], F32))
        wbuf = [es.enter_context(nc.sbuf_tensor(f"wmod{i}_{l}", [128, 8, 512], F32)) for i in range(2)]
        ps = [es.enter_context(nc.psum_tensor(f"psmod{i}_{l}", [1, 512], F32)) for i in range(2)]
        B_c, B_s, B_b, B_m = Buf("ccol"), Buf("scol"), Buf("brow"), Buf("mrow")
        B_w = [Buf("wmod0"), Buf("wmod1")]
        B_ps = [Buf("psmod0"), Buf("psmod1")]
        st = k.st
        for r in range(2):
            fw.dma(sp, st["misc"], lambda r=r: nc.sync.dma_start(
                out=ccol[:, r, :], in_=I["cvec"][r, :].rearrange("(k p) -> p k", p=128),
                allow_slow_non_contiguous=True), writes=[B_c])
        fw.dma(sp, st["misc"], lambda: nc.sync.dma_start(out=brow[:], in_=I["b_mod"][l:l + 1, :]), writes=[B_b])
        fw.op(act, lambda: nc.scalar.activation(out=scol[:], in_=ccol[:], func=AF.Silu), reads=[B_c], writes=[B_s])
        wv = I["w_mod"][l].rearrange("(k p) n -> p k n", p=128)
        for nb in range(12):
            sl = nb % 2
            fw.dma(sp, st[f"w{sl}"], lambda nb=nb, sl=sl: nc.sync.dma_start(
                out=wbuf[sl][:], in_=wv[:, :, nb * 512:(nb + 1) * 512]), writes=[B_w[sl]])
            for r in range(2):
                for kk in range(8):
                    fw.op(pe, lambda r=r, kk=kk, sl=sl: nc.tensor.matmul(
                        ps[r][:], lhsT=scol[:, r, kk:kk + 1], rhs=wbuf[sl][:, kk, :], start=(kk == 0), stop=(kk == 7)),
                        reads=[B_s, B_w[sl]], writes=[B_ps[r]])
                fw.op(dve, lambda r=r, nb=nb: nc.vector.tensor_tensor(
                    out=mrow[:, r, nb * 512:(nb + 1) * 512], in0=ps[r][:], in1=brow[:, nb * 512:(nb + 1) * 512], op=ALU.add),
                    reads=[B_ps[r], B_b], writes=[B_m])
        k.B_MOD = Buf("MOD")
        for r in range(2):
            fw.dma(sp, st["misc"], lambda r=r: nc.sync.dma_start(out=Sc["MOD"][l, r:r + 1, :], in_=mrow[:, r, :]),
                   reads=[B_m], writes=[k.B_MOD])
        fw.barrier()


def phase_proj(k, l):
    nc, fw, I, Sc = k.nc, k.fw, k.I, k.Sc
    pe, act, dve, pool, sp = fw.pe, fw.act, fw.dve, fw.pool, fw.sp
    st = k.st
    fw.barrier()
    with contextlib.ExitStack() as es:
        sb = lambda n_, shp, dt=F32: es.enter_context(nc.sbuf_tensor(f"{n_}_L{l}", list(shp), dt))
        pst = lambda n_, shp, dt=F32: es.enter_context(nc.psum_tensor(f"{n_}_L{l}", list(shp), dt))
        winb = sb("winb", [128, 8, INW], BF16)
        B_win = Buf("winb")
        wv = I["w_in"][l].rearrange("(k p) n -> p k n", p=128)
        for kk in range(8):
            fw.dma(pool, st["w0"], lambda kk=kk: nc.gpsimd.dma_start(out=winb[:, kk, :], in_=wv[:, kk, :],
                                                                      max_dma_last_dim=2048),
                   writes=[B_win])
        cols = sb("p1cols", [128, 2, 3, 8])
        A1 = sb("p1A", [128, 2, 8])
        B_cols, B_A1 = Buf("cols"), Buf("A1")
        for r in range(2):
            for j in range(2):
                fw.dma(sp, st["misc"], lambda r=r, j=j: nc.sync.dma_start(
                    out=cols[:, r, j, :], in_=Sc["MOD"][l, r, j * D:(j + 1) * D].rearrange("(k p) -> p k", p=128),
                    allow_slow_non_contiguous=True), reads=[k.B_MOD], writes=[B_cols])
            fw.dma(sp, st["misc"], lambda r=r: nc.sync.dma_start(
                out=cols[:, r, 2, :], in_=I["norm1_g"][l, :].rearrange("(k p) -> p k", p=128),
                allow_slow_non_contiguous=True), writes=[B_cols])
        for r in range(2):
            fw.op(dve, lambda r=r: nc.vector.scalar_tensor_tensor(
                out=A1[:, r, :], in0=cols[:, r, 1, :], scalar=1.0, in1=cols[:, r, 2, :], op0=ALU.add, op1=ALU.mult),
                reads=[B_cols], writes=[B_A1])
        gqk = sb("gqk", [128, 2, 64])
        B_gqk = Buf("gqk")
        fw.dma(sp, st["misc"], lambda: nc.sync.dma_start(out=gqk[:, 0, :], in_=I["q_norm_g"][l, :].partition_broadcast(128)),
               writes=[B_gqk])
        fw.dma(sp, st["misc"], lambda: nc.sync.dma_start(out=gqk[:, 1, :], in_=I["k_norm_g"][l, :].partition_broadcast(128)),
               writes=[B_gqk])

        NS = 2
        xt = [sb(f"xt{i}", [128, D]) for i in range(NS)]
        junk = sb("junk", [128, D], BF16)
        xn = [sb(f"xn{i}", [128, D], BF16) for i in range(NS)]
        ss = [sb(f"ss{i}", [128, 1]) for i in range(NS)]
        rstd = [sb(f"rstd{i}", [128, 1]) for i in range(NS)]
        hT = [sb(f"hT{i}", [128, 8, 512], BF16) for i in range(2)]
        ufs = [sb(f"ufs{i}", [128, 4, 256], BF16) for i in range(2)]
        vs = [sb(f"vs{i}", [128, 4, 512], BF16) for i in range(2)]
        yr = [sb(f"yr{i}", [128, 4, 512]) for i in range(2)]
        qk = [sb(f"qk{i}", [128, 16, 64]) for i in range(NS)]
        sq = sb("sq", [128, 16, 64])
        ss16 = sb("ss16", [128, 16])
        rs16 = sb("rs16", [128, 16])
        t1 = sb("t1", [128, 16, 64])
        t2 = sb("t2", [128, 16, 64])
        qr = [sb(f"qr{i}", [128, 16, 64], BF16) for i in range(NS)]
        qkT = [sb(f"qkT{i}", [128, 8, 512], BF16) for i in range(2)]
        cs = [sb(f"cs{i}", [128, 4, 2, 64]) for i in range(2)]
        tp = pst("tp", [128, 8, 128], BF16)
        tm = pst("tm", [128, 2048])
        fm = [pst(f"fm{i}", [128, 512]) for i in range(2)]
        tq = pst("tq", [128, 8, 128], BF16)
        B = lambda n_: Buf(n_)
        B_xt = [B("xt0"), B("xt1")]; B_junk = B("junk"); B_xn = [B("xn0"), B("xn1")]
        B_ss = [B("ss0"), B("ss1")]; B_rstd = [B("r0"), B("r1")]
        B_hT = [B("hT0"), B("hT1")]; B_ufs = [B("u0"), B("u1")]; B_vs = [B("v0"), B("v1")]; B_yr = [B("y0"), B("y1")]
        B_qk = [B("qk0"), B("qk1")]; B_sq = B("sq"); B_ss16 = B("ss16"); B_rs16 = B("rs16"); B_t1 = B("t1"); B_t2 = B("t2")
        B_qr = [B("qr0"), B("qr1")]; B_qkT = [B("qkT0"), B("qkT1")]; B_cs = [B("cs0"), B("cs1")]
        B_tp = B("tp"); B_tm = [B(f"tm{i}") for i in range(4)]; B_fm = [B("fm0"), B("fm1")]; B_tq = B("tq")
        k.B_UF = Buf("UF"); k.B_V = Buf("V"); k.B_QT = Buf("QT"); k.B_KT = Buf("KT"); k.B_YRT = Buf("YRT")

        groups = [(0, 2)] + [(2 + 4 * g, 4) for g in range(16)]
        ti = 0
        for gi, (t0, ng) in enumerate(groups):
            gs = gi % 2
            r = 1 if gi == 0 else 0
            ntok = ng * 128
            tok0 = t0 * 128
            if gi > 0:
                xr0 = tok0 - C
                fw.dma(sp, st[f"ldB{gs}"], lambda xr0=xr0, gs=gs: nc.sync.dma_start(
                    out=cs[gs][:, :, 0, :], in_=I["rope_cos"][xr0:xr0 + 512, :].rearrange("(t p) d -> p t d", p=128)),
                    writes=[B_cs[gs]])
                fw.dma(sp, st[f"ldB{gs}"], lambda xr0=xr0, gs=gs: nc.sync.dma_start(
                    out=cs[gs][:, :, 1, :], in_=I["rope_sin"][xr0:xr0 + 512, :].rearrange("(t p) d -> p t d", p=128)),
                    writes=[B_cs[gs]])
            for t in range(ng):
                s_ = ti % NS
                ti += 1
                gt = t0 + t
                fw.dma(sp, st[f"ldA{s_}"], lambda s_=s_, gt=gt: nc.sync.dma_start(
                    out=xt[s_][:], in_=Sc["XR"][gt * 128:(gt + 1) * 128, :]), reads=[k.B_XR[gt]], writes=[B_xt[s_]])
                fw.op(act, lambda s_=s_: nc.scalar.activation(out=junk[:], in_=xt[s_][:], func=AF.Square, accum_out=ss[s_][:]),
                      reads=[B_xt[s_]], writes=[B_junk, B_ss[s_]])
                fw.op(act, lambda s_=s_: nc.scalar.activation(out=rstd[s_][:], in_=ss[s_][:], func=AF.Sqrt, scale=1.0 / D, bias=EPS),
                      reads=[B_ss[s_]], writes=[B_rstd[s_]])
                fw.op(dve, lambda s_=s_: nc.vector.reciprocal(out=rstd[s_][:], in_=rstd[s_][:]), writes=[B_rstd[s_]])
                fw.op(dve, lambda s_=s_: nc.vector.tensor_scalar(out=xn[s_][:], in0=xt[s_][:], scalar1=rstd[s_][:, 0:1],
                                                                  scalar2=None, op0=ALU.mult),
                      reads=[B_xt[s_], B_rstd[s_]], writes=[B_xn[s_]])
                for kk in range(8):
                    fw.op(pe, lambda s_=s_, kk=kk: nc.tensor.transpose(tp[:, kk, :], xn[s_][:, kk * 128:(kk + 1) * 128], ident_b_ap(k)),
                          reads=[B_xn[s_], k.B_const], writes=[B_tp])
                for kk in range(8):
                    fw.op(act, lambda kk=kk, t=t, gs=gs, r=r: nc.scalar.activation(
                        out=hT[gs][:, kk, t * 128:(t + 1) * 128], in_=tp[:, kk, :], func=AF.Identity,
                        scale=A1[:, r, kk:kk + 1], bias=cols[:, r, 0, kk:kk + 1]),
                        reads=[B_tp, B_A1, B_cols], writes=[B_hT[gs]])
                for nb in range(4):
                    c0, c1 = nb * 512, min(1792, (nb + 1) * 512)
                    for kk in range(8):
                        fw.op(pe, lambda kk=kk, t=t, gs=gs, c0=c0, c1=c1: nc.tensor.matmul(
                            tm[:, c0:c1], lhsT=hT[gs][:, kk, t * 128:(t + 1) * 128], rhs=winb[:, kk, c0:c1],
                            start=(kk == 0), stop=(kk == 7)), reads=[B_hT[gs], B_win], writes=[B_tm[nb]])
                fw.op(act, lambda t=t, gs=gs: nc.scalar.copy(out=ufs[gs][:, t, :], in_=tm[:, 0:256]),
                      reads=[B_tm[0]], writes=[B_ufs[gs]])
                fw.op(act, lambda s_=s_: nc.scalar.copy(out=qk[s_][:, 0:4, :], in_=tm[:, 256:512].rearrange("p (g d) -> p g d", d=64)),
                      reads=[B_tm[0]], writes=[B_qk[s_]])
                fw.op(dve, lambda s_=s_: nc.vector.tensor_copy(out=qk[s_][:, 4:12, :], in_=tm[:, 512:1024].rearrange("p (g d) -> p g d", d=64)),
                      reads=[B_tm[1]], writes=[B_qk[s_]])
                fw.op(act, lambda s_=s_: nc.scalar.copy(out=qk[s_][:, 12:16, :], in_=tm[:, 1024:1280].rearrange("p (g d) -> p g d", d=64)),
                      reads=[B_tm[2]], writes=[B_qk[s_]])
                fw.op(act, lambda t=t, gs=gs: nc.scalar.copy(out=vs[gs][:, t, 0:256], in_=tm[:, 1280:1536]),
                      reads=[B_tm[2]], writes=[B_vs[gs]])
                fw.op(dve, lambda t=t, gs=gs: nc.vector.tensor_copy(out=vs[gs][:, t, 256:512], in_=tm[:, 1536:1792]),
                      reads=[B_tm[3]], writes=[B_vs[gs]])
                fw.op(pool, lambda s_=s_: nc.gpsimd.tensor_tensor(out=sq[:], in0=qk[s_][:], in1=qk[s_][:], op=ALU.mult),
                      reads=[B_qk[s_]], writes=[B_sq])
                fw.op(dve, lambda: nc.vector.tensor_reduce(out=ss16[:], in_=sq[:], axis=AX.X, op=ALU.add),
                      reads=[B_sq], writes=[B_ss16])
                fw.op(act, lambda: nc.scalar.activation(out=rs16[:], in_=ss16[:], func=AF.Sqrt, scale=1.0 / 64, bias=EPS),
                      reads=[B_ss16], writes=[B_rs16])
                fw.op(dve, lambda: nc.vector.reciprocal(out=rs16[:], in_=rs16[:]), writes=[B_rs16])
                fw.op(dve, lambda s_=s_: nc.vector.tensor_tensor(out=t1[:], in0=qk[s_][:], in1=bc(rs16[:, :].unsqueeze(2), [128, 16, 64]),
                                                                  op=ALU.mult), reads=[B_qk[s_], B_rs16], writes=[B_t1])
                gq_b = lambda j: bc(gqk[:, j:j + 1, :], [128, 8, 64])
                if gi == 0:
                    for j in range(2):
                        fw.op(dve, lambda j=j, s_=s_: nc.vector.tensor_tensor(out=qr[s_][:, j * 8:(j + 1) * 8, :], in0=t1[:, j * 8:(j + 1) * 8, :],
                                                                             in1=gq_b(j), op=ALU.mult),
                              reads=[B_t1, B_gqk], writes=[B_qr[s_]])
                else:
                    for j in range(2):
                        fw.op(pool, lambda j=j: nc.gpsimd.tensor_tensor(out=t1[:, j * 8:(j + 1) * 8, :], in0=t1[:, j * 8:(j + 1) * 8, :],
                                                                       in1=gq_b(j), op=ALU.mult),
                              reads=[B_gqk], writes=[B_t1])
                    cosb = bc(cs[gs][:, t, 0:1, :], [128, 16, 64])
                    t1v = t1[:].rearrange("p g (a h f) -> p g a h f", a=2, h=2)
                    t2v = t2[:].rearrange("p g (a h f) -> p g a h f", a=2, h=2)
                    sinv = cs[gs][:, t, 1, :].rearrange("p (a h f) -> p a h f", a=2, h=2)
                    for g4 in range(2):
                        pass
                    for h in range(2):
                        for a in range(2):
                            fw.op(pool, lambda h=h, a=a: nc.gpsimd.tensor_tensor(
                                out=t2v[:, :, a, h, :], in0=t1v[:, :, a, 1 - h, :],
                                in1=bc(sinv[:, a, h, :].unsqueeze(1), [128, 16, 16]), op=ALU.mult),
                                reads=[B_t1, B_cs[gs]], writes=[B_t2])
                    fw.op(dve, lambda: nc.vector.tensor_tensor(out=t1[:], in0=t1[:], in1=cosb, op=ALU.mult),
                          reads=[B_cs[gs], B_t2], writes=[B_t1])
                    fw.op(dve, lambda s_=s_: nc.vector.tensor_tensor(out=qr[s_][:], in0=t1[:], in1=t2[:], op=ALU.add),
                          reads=[B_t1, B_t2], writes=[B_qr[s_]])
                for j in range(8):
                    fw.op(pe, lambda j=j, s_=s_: nc.tensor.transpose(tq[:, j, :], qr[s_][:, 2 * j:2 * j + 2, :].rearrange("p g d -> p (g d)"), ident_b_ap(k)),
                          reads=[B_qr[s_], k.B_const], writes=[B_tq])
                fw.op(act, lambda t=t, gs=gs: nc.scalar.copy(out=qkT[gs][:, :, t * 128:(t + 1) * 128], in_=tq[:]),
                      reads=[B_tq], writes=[B_qkT[gs]])
            for cc in range(4):
                fs = cc % 2
                for kk in range(8):
                    fw.op(pe, lambda kk=kk, cc=cc, fs=fs, gs=gs, ntok=ntok: nc.tensor.matmul(
                        fm[fs][:, 0:ntok], lhsT=winb[:, kk, 1792 + cc * 128:1792 + (cc + 1) * 128], rhs=hT[gs][:, kk, 0:ntok],
                        start=(kk == 0), stop=(kk == 7)), reads=[B_hT[gs], B_win], writes=[B_fm[fs]])
                fw.op(dve, lambda cc=cc, fs=fs, gs=gs, ntok=ntok: nc.vector.tensor_copy(out=yr[gs][:, cc, 0:ntok], in_=fm[fs][:, 0:ntok]),
                      reads=[B_fm[fs]], writes=[B_yr[gs]])
            fw.dma(pool, st[f"stA{gs}"], lambda gs=gs, tok0=tok0, ntok=ntok, ng=ng: nc.gpsimd.dma_start(
                out=Sc["UF"][tok0:tok0 + ntok, :].rearrange("(t p) d -> p t d", p=128), in_=ufs[gs][:, 0:ng, :]),
                reads=[B_ufs[gs]], writes=[k.B_UF])
            fw.dma(pool, st[f"stB{gs}"], lambda gs=gs, tok0=tok0, ntok=ntok, ng=ng: nc.gpsimd.dma_start(
                out=Sc["V"][tok0:tok0 + ntok, :].rearrange("(t p) d -> p t d", p=128), in_=vs[gs][:, 0:ng, :]),
                reads=[B_vs[gs]], writes=[k.B_V])
            fw.dma(pool, st[f"stC{gs}"], lambda gs=gs, tok0=tok0, ntok=ntok: nc.gpsimd.dma_start(
                out=Sc["YRT"][:, tok0:tok0 + ntok].rearrange("(c p) t -> p c t", p=128), in_=yr[gs][:, :, 0:ntok]),
                reads=[B_yr[gs]], writes=[k.B_YRT])
            fw.dma(pool, st[f"stD{gs}"], lambda gs=gs, tok0=tok0, ntok=ntok: nc.gpsimd.dma_start(
                out=Sc["QT"][:, :, tok0:tok0 + ntok].rearrange("h p t -> p h t"), in_=qkT[gs][:, 0:4, 0:ntok]),
                reads=[B_qkT[gs]], writes=[k.B_QT])
            fw.dma(pool, st[f"stD{gs}"], lambda gs=gs, tok0=tok0, ntok=ntok: nc.gpsimd.dma_start(
                out=Sc["KT"][:, :, tok0:tok0 + ntok].rearrange("h p t -> p h t"), in_=qkT[gs][:, 4:8, 0:ntok]),
                reads=[B_qkT[gs]], writes=[k.B_KT])
        fw.barrier()


def ident_b_ap(k):
    return k.ident_b[:]


def _tables():
    ident = np.eye(128, dtype=np.float32)
    t = np.arange(S)
    row = (t // 64).astype(np.float32)
    col = (t % 64).astype(np.float32)
    freqs = (10000.0 ** (-np.arange(16, dtype=np.float32) / 16)).astype(np.float32)
    ar = (row[:, None] * freqs[None, :]).astype(np.float32)
    ac = (col[:, None] * freqs[None, :]).astype(np.float32)
    cos = np.concatenate([np.cos(ar), np.cos(ar), np.cos(ac), np.cos(ac)], axis=1).astype(np.float32)
    sin = np.concatenate([-np.sin(ar), np.sin(ar), -np.sin(ac), np.sin(ac)], axis=1).astype(np.float32)
    tabs = {"ident": ident, "rope_cos": cos, "rope_sin": sin}
    j64 = np.arange(64, dtype=np.float64)
    a64 = 2 * np.pi * np.outer(j64, j64) / 64.0
    tabs["f_cs64"] = np.concatenate([np.cos(a64), np.sin(a64)], axis=1).astype(np.float32)
    t2 = np.arange(128, dtype=np.float64)[:, None, None]
    k1 = np.arange(64, dtype=np.float64)[None, :, None]
    k2 = np.arange(128, dtype=np.float64)[None, None, :]
    ang = 2 * np.pi * ((t2 * (k1 + 64 * k2)) % 8192) / 8192.0
    tabs["f_c128"] = np.cos(ang).astype(np.float32)
    tabs["f_s128"] = np.sin(ang).astype(np.float32)
    tabs["f_n128"] = (-np.sin(ang)).astype(np.float32)
    j256 = np.arange(256, dtype=np.float64)
    a256 = 2 * np.pi * (np.outer(j256, j256) % 256) / 256.0
    tabs["f_c256"] = np.cos(a256).astype(np.float32)
    tabs["f_s256"] = np.sin(a256).astype(np.float32)
    bdc = np.zeros((128, 128), np.float32); bds = np.zeros((128, 128), np.float32)
    for hb in range(2):
        bdc[hb * 64:(hb + 1) * 64, hb * 64:(hb + 1) * 64] = np.cos(a64)
        bds[hb * 64:(hb + 1) * 64, hb * 64:(hb + 1) * 64] = np.sin(a64)
    tabs["f_bdc"] = bdc; tabs["f_bds"] = bds
    return tabs


_PARAMS = ["w_mod", "b_mod", "norm1_g", "norm2_g", "w_in", "q_norm_g", "k_norm_g", "lambda_q1", "lambda_k1",
           "lambda_q2", "lambda_k2", "subln_g", "conv_w", "conv_b", "gate_a_w", "gate_a_b", "gate_x_w", "gate_x_b",
           "lru_lambda", "w_out", "w_group", "b_group", "w_router", "b_router", "w1", "w3", "w2"]


def make_in_maps(inputs, ncores=8):
    tabs = _tables()
    shared = {n_: np.ascontiguousarray(np.asarray(inputs[n_], dtype=np.float32)) for n_ in _PARAMS}
    maps = []
    for b in range(ncores):
        m = dict(shared)
        m.update(tabs)
        m["xin"] = np.ascontiguousarray(np.concatenate([inputs["ctx"][b], inputs["x"][b]], axis=0).astype(np.float32))
        m["cvec"] = np.ascontiguousarray(np.stack([inputs["c"][b], inputs["c_ctx"]], axis=0).astype(np.float32))
        maps.append(m)
    return maps


def kernel(**inputs):
    nc = build()
    maps = make_in_maps(inputs, 8)
    res = run_bass_kernel_spmd(nc, maps, core_ids=list(range(8)))
    return np.stack([np.asarray(r["out"]) for r in res.results], axis=0).astype(np.float32)


def phase_lru(k, l):
    nc, fw, I, Sc = k.nc, k.fw, k.I, k.Sc
    pe, act, dve, pool, sp = fw.pe, fw.act, fw.dve, fw.pool, fw.sp
    st = k.st
    last = (l == DEPTH - 1)
    fw.barrier()
    with contextlib.ExitStack() as es:
        sb = lambda n_, shp, dt=F32: es.enter_context(nc.sbuf_tensor(f"{n_}_L{l}", list(shp), dt))
        pst = lambda n_, shp, dt=F32: es.enter_context(nc.psum_tensor(f"{n_}_L{l}", list(shp), dt))
        NCH = 2048
        bdf = sb("bdf", [128, 8, 128])
        bdb = sb("bdb", [128, 8, 128], BF16)
        cw = sb("cw", [128, 2, 4])
        cb = sb("cb", [128, 2])
        gb = sb("gb", [128, 2, 2, 2])
        lam = sb("lam", [128, 2, 2])
        e_ = sb("lru_e", [128, 4]); ln_ = sb("lru_ln", [128, 4]); se_ = sb("lru_se", [128, 4]); mk_ = sb("lru_mk", [128, 4])
        m8 = sb("m8", [128, 4])
        B_w = Buf("lruw")
        fw.op(dve, lambda: nc.vector.memset(bdf[:], 0.0), writes=[B_w])
        for g_, nm in enumerate(("gate_a_w", "gate_x_w")):
            for d in range(2):
                for cc in range(2):
                    idx = g_ * 4 + d * 2 + cc
                    for hb in range(2):
                        fw.dma(sp, st["misc"], lambda idx=idx, nm=nm, d=d, cc=cc, hb=hb: nc.sync.dma_start(
                            out=bdf[hb * 64:(hb + 1) * 64, idx, hb * 64:(hb + 1) * 64], in_=I[nm][l, d, 2 * cc + hb, :, :]),
                            writes=[B_w])
        fw.op(dve, lambda: nc.vector.tensor_copy(out=bdb[:], in_=bdf[:]), writes=[B_w])
        col = lambda ap1: ap1.rearrange("(c p) -> p c", p=128)
        for kk in range(4):
            fw.dma(sp, st["misc"], lambda kk=kk: nc.sync.dma_start(out=cw[:, :, kk], in_=col(I["conv_w"][l, kk, :]),
                                                                  allow_slow_non_contiguous=True), writes=[B_w])
        fw.dma(sp, st["misc"], lambda: nc.sync.dma_start(out=cb[:, :], in_=col(I["conv_b"][l, :]), allow_slow_non_contiguous=True),
               writes=[B_w])
        for g_, nm in enumerate(("gate_a_b", "gate_x_b")):
            for d in range(2):
                fw.dma(sp, st["misc"], lambda g_=g_, nm=nm, d=d: nc.sync.dma_start(
                    out=gb[:, g_, d, :], in_=col(I[nm][l, d, :]), allow_slow_non_contiguous=True), writes=[B_w])
        for d in range(2):
            fw.dma(sp, st["misc"], lambda d=d: nc.sync.dma_start(out=lam[:, d, :], in_=col(I["lru_lambda"][l, d, :]),
                                                                allow_slow_non_contiguous=True), writes=[B_w])
        lamf = lam[:].rearrange("p d c -> p (d c)")
        fw.op(act, lambda: nc.scalar.activation(out=e_[:], in_=lamf, func=AF.Exp, scale=-1.0), writes=[B_w])
        fw.op(act, lambda: nc.scalar.activation(out=ln_[:], in_=e_[:], func=AF.Ln, bias=1.0, scale=1.0), writes=[B_w])
        fw.op(dve, lambda: nc.vector.tensor_scalar(out=se_[:], in0=e_[:], scalar1=1.0 / 3, scalar2=-0.5, op0=ALU.mult, op1=ALU.add), writes=[B_w])
        fw.op(dve, lambda: nc.vector.tensor_tensor(out=se_[:], in0=se_[:], in1=e_[:], op=ALU.mult), writes=[B_w])
        fw.op(dve, lambda: nc.vector.tensor_scalar(out=se_[:], in0=se_[:], scalar1=1.0, scalar2=None, op0=ALU.add), writes=[B_w])
        fw.op(dve, lambda: nc.vector.tensor_tensor(out=se_[:], in0=se_[:], in1=e_[:], op=ALU.mult), writes=[B_w])
        fw.op(dve, lambda: nc.vector.tensor_scalar(out=mk_[:], in0=e_[:], scalar1=0.02, scalar2=None, op0=ALU.is_lt), writes=[B_w])
        fw.op(dve, lambda: nc.vector.tensor_tensor(out=se_[:], in0=se_[:], in1=ln_[:], op=ALU.subtract), writes=[B_w])
        fw.op(dve, lambda: nc.vector.tensor_tensor(out=se_[:], in0=se_[:], in1=mk_[:], op=ALU.mult), writes=[B_w])
        fw.op(dve, lambda: nc.vector.tensor_tensor(out=se_[:], in0=se_[:], in1=ln_[:], op=ALU.add), writes=[B_w])
        fw.op(dve, lambda: nc.vector.tensor_scalar(out=m8[:], in0=se_[:], scalar1=-8.0, scalar2=None, op0=ALU.mult), writes=[B_w])

        raw = sb("raw", [128, NCH + 3]); xr = sb("xr", [128, NCH]); xrb = sb("xrb", [128, NCH], BF16)
        gr = sb("gr", [128, NCH]); gi = sb("gi", [128, NCH]); tmp = sb("ltmp", [128, NCH]); H = sb("H", [128, NCH])
        hf = sb("hf", [128, NCH]); yb = sb("yb", [128, NCH]); rec = sb("rec", [128, NCH], BF16)
        carry = sb("carry", [128, 1])
        psA = [pst(f"psA{i}", [128, 512]) for i in range(2)]
        psX = [pst(f"psX{i}", [128, 512]) for i in range(2)]
        Bn = lambda n_: Buf(n_)
        B_raw, B_xr, B_xrb, B_gr, B_gi, B_tmp, B_H, B_hf, B_yb, B_rec, B_carry = [Bn(n_) for n_ in
            ("raw", "xr", "xrb", "gr", "gi", "tmp", "H", "hf", "yb", "rec", "carry")]
        B_psA = [Bn("psA0"), Bn("psA1")]; B_psX = [Bn("psX0"), Bn("psX1")]
        HFD = Sc["HFD"]
        k.B_HFD = Buf("HFD")
        chunks = [(0, C, 0, C)] + [(C + j * NCH, C + (j + 1) * NCH, C, T) for j in range(4)]
        for cc in range(2):
            for d in range(2):
                order = chunks if d == 0 else [chunks[0]] + chunks[:0:-1]
                for ci, (lo, hi, slo, shi) in enumerate(order):
                    n = hi - lo
                    a0 = max(lo - 1, slo); a1 = min(hi + 2, shi)
                    fw.op(pool, lambda: nc.gpsimd.memset(raw[:], 0.0), writes=[B_raw])
                    fw.dma(sp, st["ldA0"], lambda a0=a0, a1=a1, lo=lo, cc=cc: nc.sync.dma_start(
                        out=raw[:, a0 - (lo - 1):a1 - (lo - 1)], in_=Sc["YRT"][256 + cc * 128:256 + (cc + 1) * 128, a0:a1]),
                        reads=[k.B_YRT], writes=[B_raw])
                    fw.op(dve, lambda n=n, cc=cc: nc.vector.tensor_scalar(out=xr[:, 0:n], in0=raw[:, 1:1 + n], scalar1=cw[:, cc, 1:2],
                                                                        scalar2=cb[:, cc:cc + 1], op0=ALU.mult, op1=ALU.add),
                          reads=[B_raw, B_w], writes=[B_xr])
                    for kk in (0, 2, 3):
                        fw.op(dve, lambda n=n, cc=cc, kk=kk: nc.vector.scalar_tensor_tensor(
                            out=xr[:, 0:n], in0=raw[:, kk:kk + n], scalar=cw[:, cc, kk:kk + 1], in1=xr[:, 0:n], op0=ALU.mult, op1=ALU.add),
                            reads=[B_raw, B_w], writes=[B_xr])
                    fw.op(act, lambda n=n: nc.scalar.copy(out=xrb[:, 0:n], in_=xr[:, 0:n]), reads=[B_xr], writes=[B_xrb])
                    nblk = (n + 511) // 512
                    for bi in range(nblk):
                        b0, b1 = bi * 512, min(n, (bi + 1) * 512)
                        s_ = bi % 2
                        fw.op(pe, lambda s_=s_, b0=b0, b1=b1, d=d, cc=cc: nc.tensor.matmul(
                            psA[s_][:, 0:b1 - b0], lhsT=bdb[:, 0 * 4 + d * 2 + cc, :], rhs=xrb[:, b0:b1], start=True, stop=True),
                            reads=[B_xrb, B_w], writes=[B_psA[s_]])
                        fw.op(pe, lambda s_=s_, b0=b0, b1=b1, d=d, cc=cc: nc.tensor.matmul(
                            psX[s_][:, 0:b1 - b0], lhsT=bdb[:, 1 * 4 + d * 2 + cc, :], rhs=xrb[:, b0:b1], start=True, stop=True),
                            reads=[B_xrb, B_w], writes=[B_psX[s_]])
                        fw.op(act, lambda s_=s_, b0=b0, b1=b1, d=d, cc=cc: nc.scalar.activation(
                            out=gr[:, b0:b1], in_=psA[s_][:, 0:b1 - b0], func=AF.Sigmoid, bias=gb[:, 0, d, cc:cc + 1], scale=1.0),
                            reads=[B_psA[s_], B_w], writes=[B_gr])
                        fw.op(act, lambda s_=s_, b0=b0, b1=b1, d=d, cc=cc: nc.scalar.activation(
                            out=gi[:, b0:b1], in_=psX[s_][:, 0:b1 - b0], func=AF.Sigmoid, bias=gb[:, 1, d, cc:cc + 1], scale=1.0),
                            reads=[B_psX[s_], B_w], writes=[B_gi])
                    fw.op(act, lambda n=n, d=d, cc=cc: nc.scalar.activation(out=gr[:, 0:n], in_=gr[:, 0:n], func=AF.Exp,
                                                                           scale=m8[:, d * 2 + cc:d * 2 + cc + 1]),
                          reads=[B_w], writes=[B_gr])
                    fw.op(pool, lambda n=n: nc.gpsimd.tensor_tensor(out=tmp[:, 0:n], in0=gr[:, 0:n], in1=gr[:, 0:n], op=ALU.mult),
                          reads=[B_gr], writes=[B_tmp])
                    fw.op(dve, lambda n=n: nc.vector.tensor_scalar(out=tmp[:, 0:n], in0=tmp[:, 0:n], scalar1=-1.0, scalar2=1.0,
                                                                   op0=ALU.mult, op1=ALU.add), writes=[B_tmp])
                    fw.op(act, lambda n=n: nc.scalar.activation(out=tmp[:, 0:n], in_=tmp[:, 0:n], func=AF.Sqrt), writes=[B_tmp])
                    fw.op(dve, lambda n=n: nc.vector.tensor_tensor(out=gi[:, 0:n], in0=gi[:, 0:n], in1=xr[:, 0:n], op=ALU.mult),
                          reads=[B_xr], writes=[B_gi])
                    fw.op(dve, lambda n=n: nc.vector.tensor_tensor(out=gi[:, 0:n], in0=gi[:, 0:n], in1=tmp[:, 0:n], op=ALU.mult),
                          reads=[B_tmp], writes=[B_gi])
                    init = 0.0 if ci == 0 else carry[:, 0:1]
                    if d == 0:
                        fw.op(dve, lambda n=n, init=init: nc.vector.tensor_tensor_scan(
                            out=H[:, 0:n], data0=gr[:, 0:n], data1=gi[:, 0:n], initial=init, op0=ALU.mult, op1=ALU.add),
                            reads=[B_gr, B_gi, B_carry], writes=[B_H])
                        fw.op(dve, lambda n=n: nc.vector.tensor_copy(out=carry[:], in_=H[:, n - 1:n]), reads=[B_H], writes=[B_carry])
                        fw.dma(sp, st["stA0"], lambda n=n, lo=lo, hi=hi, cc=cc: nc.sync.dma_start(
                            out=HFD[cc * 128:(cc + 1) * 128, lo:hi], in_=H[:, 0:n]), reads=[B_H], writes=[k.B_HFD])
                    else:
                        fw.op(dve, lambda n=n, init=init: nc.vector.tensor_tensor_scan(
                            out=H[:, 0:n][:, ::-1], data0=gr[:, 0:n][:, ::-1], data1=gi[:, 0:n][:, ::-1],
                            initial=init, op0=ALU.mult, op1=ALU.add),
                            reads=[B_gr, B_gi, B_carry], writes=[B_H])
                        fw.op(dve, lambda: nc.vector.tensor_copy(out=carry[:], in_=H[:, 0:1]), reads=[B_H], writes=[B_carry])
                        if last and lo == 0:
                            continue
                        fw.dma(sp, st["ldB0"], lambda n=n, lo=lo, hi=hi, cc=cc: nc.sync.dma_start(
                            out=hf[:, 0:n], in_=HFD[cc * 128:(cc + 1) * 128, lo:hi]), reads=[k.B_HFD], writes=[B_hf])
                        fw.dma(sp, st["ldC0"], lambda n=n, lo=lo, hi=hi, cc=cc: nc.sync.dma_start(
                            out=yb[:, 0:n], in_=Sc["YRT"][cc * 128:(cc + 1) * 128, lo:hi]), reads=[k.B_YRT], writes=[B_yb])
                        fw.op(pool, lambda n=n: nc.gpsimd.tensor_tensor(out=tmp[:, 0:n], in0=yb[:, 0:n], in1=yb[:, 0:n], op=ALU.mult),
                              reads=[B_yb], writes=[B_tmp])
                        fw.op(pool, lambda n=n: nc.gpsimd.tensor_scalar(out=tmp[:, 0:n], in0=tmp[:, 0:n], scalar1=0.044715, scalar2=1.0,
                                                                       op0=ALU.mult, op1=ALU.add), writes=[B_tmp])
                        fw.op(pool, lambda n=n: nc.gpsimd.tensor_tensor(out=tmp[:, 0:n], in0=tmp[:, 0:n], in1=yb[:, 0:n], op=ALU.mult),
                              reads=[B_yb], writes=[B_tmp])
                        fw.op(act, lambda n=n: nc.scalar.activation(out=tmp[:, 0:n], in_=tmp[:, 0:n], func=AF.Sigmoid,
                                                                    scale=2.0 * math.sqrt(2.0 / math.pi)), writes=[B_tmp])
                        fw.op(dve, lambda n=n: nc.vector.tensor_tensor(out=tmp[:, 0:n], in0=tmp[:, 0:n], in1=yb[:, 0:n], op=ALU.mult),
                              reads=[B_yb], writes=[B_tmp])
                        fw.op(dve, lambda n=n: nc.vector.tensor_tensor(out=hf[:, 0:n], in0=hf[:, 0:n], in1=H[:, 0:n], op=ALU.add),
                              reads=[B_H], writes=[B_hf])
                        fw.op(dve, lambda n=n: nc.vector.tensor_tensor(out=rec[:, 0:n], in0=tmp[:, 0:n], in1=hf[:, 0:n], op=ALU.mult),
                              reads=[B_tmp, B_hf], writes=[B_rec])
                        fw.dma(sp, st["stB0"], lambda n=n, lo=lo, hi=hi, cc=cc: nc.sync.dma_start(
                            out=Sc["MIXT"][1024 + cc * 128:1024 + (cc + 1) * 128, lo:hi], in_=rec[:, 0:n]),
                            reads=[B_rec], writes=[k.B_MIXT])
        fw.barrier()


def phase_fourier(k, l):
    nc, fw, I, Sc = k.nc, k.fw, k.I, k.Sc
    pe, act, dve, pool, sp = fw.pe, fw.act, fw.dve, fw.pool, fw.sp
    st = k.st
    last = (l == DEPTH - 1)
    fw.barrier()
    with contextlib.ExitStack() as es:
        sb = lambda n_, shp, dt=F32: es.enter_context(nc.sbuf_tensor(f"{n_}_L{l}", list(shp), dt))
        pst = lambda n_, shp, dt=F32: es.enter_context(nc.psum_tensor(f"{n_}_L{l}", list(shp), dt))
        cs64 = sb("cs64", [64, 128], BF16)
        tC = sb("tC", [128, 64, 128], BF16); tS = sb("tS", [128, 64, 128], BF16); tN = sb("tN", [128, 64, 128], BF16)
        B_tab = Buf("ftab")
        fw.dma(pool, st["w0"], lambda: nc.gpsimd.dma_start(out=cs64[:], in_=I["f_cs64"][:, :]), writes=[B_tab])
        for tt, nm in ((tC, "f_c128"), (tS, "f_s128"), (tN, "f_n128")):
            for q4 in range(4):
                fw.dma(pool, st["w0"], lambda tt=tt, nm=nm, q4=q4: nc.gpsimd.dma_start(
                    out=tt[:, q4 * 16:(q4 + 1) * 16, :], in_=I[nm][:, q4 * 16:(q4 + 1) * 16, :]), writes=[B_tab])
        zs = sb("zs", [64, 128, 128], BF16)
        As = sb("As", [128, 64, 2, 128], BF16)
        WT = [sb(f"WT{i}", [128, S], BF16) for i in range(2)]
        ps1 = [pst(f"ps1_{i}", [128, 4, 2, 64]) for i in range(2)]
        psr = [pst(f"psr{i}", [128, 4, 128]) for i in range(2)]
        psi = [pst(f"psi{i}", [128, 4, 128]) for i in range(2)]
        B_zs, B_As = Buf("zs"), Buf("As")
        B_WT = [Buf("WT0"), Buf("WT1")]
        B_ps1 = [Buf("ps1_0"), Buf("ps1_1")]; B_psr = [Buf("psr0"), Buf("psr1")]; B_psi = [Buf("psi0"), Buf("psi1")]
        sc_x = 1.0 / math.sqrt(S * 64.0)
        for cc in range(2):
            fw.dma(sp, st["ldA0"], lambda cc=cc: nc.sync.dma_start(
                out=zs[:], in_=Sc["UF"][C:T, cc * 128:(cc + 1) * 128].rearrange("(a b) c -> a b c", b=128)),
                reads=[k.B_UF], writes=[B_zs])
            for c4 in range(32):
                s_ = c4 % 2
                for j in range(4):
                    ch = c4 * 4 + j
                    fw.op(pe, lambda s_=s_, j=j, ch=ch: nc.tensor.matmul(
                        ps1[s_][:, j, :, :].rearrange("p r k -> p (r k)"), lhsT=zs[:, :, ch], rhs=cs64[:], start=True, stop=True),
                        reads=[B_zs, B_tab], writes=[B_ps1[s_]])
                dst = As[:, :, :, c4 * 4:(c4 + 1) * 4].rearrange("p k r c -> p c r k")
                if c4 % 2 == 0:
                    fw.op(act, lambda s_=s_, dst=dst: nc.scalar.copy(out=dst, in_=ps1[s_][:]), reads=[B_ps1[s_]], writes=[B_As])
                else:
                    fw.op(dve, lambda s_=s_, dst=dst: nc.vector.tensor_copy(out=dst, in_=ps1[s_][:]), reads=[B_ps1[s_]], writes=[B_As])
            for kg in range(16):
                s_ = kg % 2
                for j in range(4):
                    k1 = kg * 4 + j
                    fw.op(pe, lambda s_=s_, j=j, k1=k1: nc.tensor.matmul(psr[s_][:, j, :], lhsT=As[:, k1, 0, :], rhs=tC[:, k1, :], start=True, stop=False),
                          reads=[B_As, B_tab], writes=[B_psr[s_]])
                    fw.op(pe, lambda s_=s_, j=j, k1=k1: nc.tensor.matmul(psr[s_][:, j, :], lhsT=As[:, k1, 1, :], rhs=tN[:, k1, :], start=False, stop=True),
                          reads=[B_As, B_tab], writes=[B_psr[s_]])
                for j in range(4):
                    k1 = kg * 4 + j
                    fw.op(pe, lambda s_=s_, j=j, k1=k1: nc.tensor.matmul(psi[s_][:, j, :], lhsT=As[:, k1, 1, :], rhs=tC[:, k1, :], start=True, stop=False),
                          reads=[B_As, B_tab], writes=[B_psi[s_]])
                    fw.op(pe, lambda s_=s_, j=j, k1=k1: nc.tensor.matmul(psi[s_][:, j, :], lhsT=As[:, k1, 0, :], rhs=tS[:, k1, :], start=False, stop=True),
                          reads=[B_As, B_tab], writes=[B_psi[s_]])
                o_r = WT[0][:].rearrange("p (b a) -> p a b", a=64)[:, kg * 4:(kg + 1) * 4, :]
                o_i = WT[1][:].rearrange("p (b a) -> p a b", a=64)[:, kg * 4:(kg + 1) * 4, :]
                fw.op(act, lambda s_=s_, o_r=o_r: nc.scalar.mul(out=o_r, in_=psr[s_][:], mul=sc_x), reads=[B_psr[s_]], writes=[B_WT[0]])
                fw.op(dve, lambda s_=s_, o_i=o_i: nc.vector.tensor_scalar(out=o_i, in0=psi[s_][:], scalar1=-sc_x, scalar2=None, op0=ALU.mult),
                      reads=[B_psi[s_]], writes=[B_WT[1]])
            for ri in range(2):
                fw.dma(sp, st[f"stA{ri}"], lambda ri=ri, cc=cc: nc.sync.dma_start(
                    out=Sc["MIXT"][ri * 256 + cc * 128:ri * 256 + (cc + 1) * 128, C:T], in_=WT[ri][:]),
                    reads=[B_WT[ri]], writes=[k.B_MIXT])
        if not last:
            c256 = sb("c256", [128, 2, 256], BF16); s256 = sb("s256", [128, 2, 256], BF16)
            zc = sb("zc", [128, 2, 256], BF16)
            wc = sb("wc", [128, 2, 256], BF16)
            B_zc, B_wc = Buf("zc"), Buf("wc")
            fw.dma(pool, st["w0"], lambda: nc.gpsimd.dma_start(out=c256[:], in_=I["f_c256"].rearrange("(a p) n -> p a n", p=128)), writes=[B_tab])
            fw.dma(pool, st["w0"], lambda: nc.gpsimd.dma_start(out=s256[:], in_=I["f_s256"].rearrange("(a p) n -> p a n", p=128)), writes=[B_tab])
            fw.dma(sp, st["ldA0"], lambda: nc.sync.dma_start(out=zc[:], in_=Sc["UF"][0:C, :].rearrange("(a p) n -> p a n", p=128)),
                   reads=[k.B_UF], writes=[B_zc])
            sc_c = 1.0 / math.sqrt(C * 64.0)
            for cc in range(2):
                pr = psr[cc][:].rearrange("p a b -> p (a b)")[:, 0:256]
                pi_ = psi[cc][:].rearrange("p a b -> p (a b)")[:, 0:256]
                for a in range(2):
                    fw.op(pe, lambda a=a, cc=cc, pr=pr: nc.tensor.matmul(pr, lhsT=zc[:, a, cc * 128:(cc + 1) * 128], rhs=c256[:, a, :],
                                                                        start=(a == 0), stop=(a == 1)), reads=[B_zc, B_tab], writes=[B_psr[cc]])
                for a in range(2):
                    fw.op(pe, lambda a=a, cc=cc, pi_=pi_: nc.tensor.matmul(pi_, lhsT=zc[:, a, cc * 128:(cc + 1) * 128], rhs=s256[:, a, :],
                                                                          start=(a == 0), stop=(a == 1)), reads=[B_zc, B_tab], writes=[B_psi[cc]])
                fw.op(act, lambda pr=pr: nc.scalar.mul(out=wc[:, 0, :], in_=pr, mul=sc_c), reads=[B_psr[cc]], writes=[B_wc])
                fw.op(act, lambda pi_=pi_: nc.scalar.mul(out=wc[:, 1, :], in_=pi_, mul=-sc_c), reads=[B_psi[cc]], writes=[B_wc])
                for ri in range(2):
                    fw.dma(sp, st[f"stB{ri}"], lambda ri=ri, cc=cc: nc.sync.dma_start(
                        out=Sc["MIXT"][ri * 256 + cc * 128:ri * 256 + (cc + 1) * 128, 0:C], in_=wc[:, ri, :]),
                        reads=[B_wc], writes=[k.B_MIXT])
        fw.barrier()


def phase_attn(k, l):
    nc, fw, I, Sc = k.nc, k.fw, k.I, k.Sc
    pe, act, dve, pool, sp = fw.pe, fw.act, fw.dve, fw.pool, fw.sp
    st = k.st
    last = (l == DEPTH - 1)
    lam_init = 0.8 - 0.6 * math.exp(-0.3 * l)
    fw.barrier()
    with contextlib.ExitStack() as es:
        sb = lambda n_, shp, dt=F32: es.enter_context(nc.sbuf_tensor(f"{n_}_L{l}", list(shp), dt))
        pst = lambda n_, shp, dt=F32: es.enter_context(nc.psum_tensor(f"{n_}_L{l}", list(shp), dt))
        lv = sb("lv", [128, 4, 64]); lp = sb("lp", [128, 2, 64]); ls = sb("ls", [128, 2]); nlam = sb("nlam", [128, 1])
        gq = sb("agq", [128, 2, 64]); gm = sb("agm", [128, 2]); negM = sb("negM", [128, 1])
        sg = sb("sg", [128, 1])
        B_s = Buf("attn_setup")
        for j, nm in enumerate(("lambda_q1", "lambda_k1", "lambda_q2", "lambda_k2")):
            fw.dma(sp, st["misc"], lambda j=j, nm=nm: nc.sync.dma_start(out=lv[:, j, :], in_=I[nm][l, :].partition_broadcast(128)), writes=[B_s])
        fw.dma(sp, st["misc"], lambda: nc.sync.dma_start(out=gq[:, 0, :], in_=I["q_norm_g"][l, :].partition_broadcast(128)), writes=[B_s])
        fw.dma(sp, st["misc"], lambda: nc.sync.dma_start(out=gq[:, 1, :], in_=I["k_norm_g"][l, :].partition_broadcast(128)), writes=[B_s])
        fw.dma(sp, st["misc"], lambda: nc.sync.dma_start(out=sg[:], in_=I["subln_g"][l, :].rearrange("(p o) -> p o", o=1)), writes=[B_s])
        lvv = lv[:].rearrange("p (a b) d -> p a b d", b=2)
        fw.op(dve, lambda: nc.vector.tensor_tensor(out=lp[:], in0=lvv[:, :, 0, :], in1=lvv[:, :, 1, :], op=ALU.mult), writes=[B_s])
        fw.op(dve, lambda: nc.vector.tensor_reduce(out=ls[:], in_=lp[:], axis=AX.X, op=ALU.add), writes=[B_s])
        fw.op(act, lambda: nc.scalar.activation(out=ls[:], in_=ls[:], func=AF.Exp), writes=[B_s])
        fw.op(dve, lambda: nc.vector.tensor_tensor(out=nlam[:], in0=ls[:, 1:2], in1=ls[:, 0:1], op=ALU.subtract), writes=[B_s])
        fw.op(dve, lambda: nc.vector.tensor_scalar(out=nlam[:], in0=nlam[:], scalar1=-lam_init, scalar2=None, op0=ALU.add), writes=[B_s])
        fw.op(dve, lambda: nc.vector.tensor_reduce(out=gm[:], in_=gq[:], axis=AX.X, op=ALU.max, apply_absolute_value=True), writes=[B_s])
        fw.op(dve, lambda: nc.vector.tensor_tensor(out=negM[:], in0=gm[:, 0:1], in1=gm[:, 1:2], op=ALU.mult), writes=[B_s])
        fw.op(dve, lambda: nc.vector.tensor_scalar(out=negM[:], in0=negM[:], scalar1=-8.0, scalar2=None, op0=ALU.mult), writes=[B_s])
        fw.op(dve, lambda: nc.vector.tensor_scalar(out=sg[:], in0=sg[:], scalar1=1.0 - lam_init, scalar2=None, op0=ALU.mult), writes=[B_s])

        NKC = T // 128
        kt = [sb(f"kt{i}", [128, T], BF16) for i in range(2)]
        vh = [sb(f"vh{i}", [128, NKC, 128], BF16) for i in range(2)]
        qt = [sb(f"qt{i}", [128, 512], BF16) for i in range(2)]
        P = [[sb(f"P{m}{i}", [128, 512], BF16) for i in range(3)] for m in range(2)]
        rz = [sb(f"rz{m}", [128, 512]) for m in range(2)]
        o0 = sb("o0", [128, 512]); att = sb("att", [128, 512]); sqa = sb("sqa", [128, 512]); rs = sb("ars", [128, 512])
        ob = [sb(f"ob{i}", [128, 512], BF16) for i in range(2)]
        Sps = [[pst(f"S{m}{i}", [128, 512]) for i in range(2)] for m in range(2)]
        Ops = [pst(f"O{m}", [128, 512]) for m in range(2)]
        Zps = [pst(f"Z{m}", [128, 512]) for m in range(2)]
        B_kt = [Buf("kt0"), Buf("kt1")]; B_vh = [Buf("vh0"), Buf("vh1")]; B_qt = [Buf("qt0"), Buf("qt1")]
        B_P = [[Buf(f"P{m}{i}") for i in range(3)] for m in range(2)]
        B_rz = [Buf("rz0"), Buf("rz1")]; B_o0, B_att, B_sqa, B_rs = Buf("o0"), Buf("att"), Buf("sqa"), Buf("rs")
        B_ob = [Buf("ob0"), Buf("ob1")]
        B_S = [[Buf(f"S{m}{i}") for i in range(2)] for m in range(2)]
        B_O = [Buf("O0"), Buf("O1")]; B_Z = [Buf("Z0"), Buf("Z1")]

        def load_head(h):
            s_ = h % 2
            fw.dma(sp, st[f"ldA{s_}"], lambda: nc.sync.dma_start(out=kt[s_][:], in_=Sc["KT"][h, :, :]), reads=[k.B_KT], writes=[B_kt[s_]])
            fw.dma(sp, st[f"ldB{s_}"], lambda: nc.sync.dma_start(
                out=vh[s_][:], in_=Sc["V"][:, h * 128:(h + 1) * 128].rearrange("(c p) d -> p c d", p=128)), reads=[k.B_V], writes=[B_vh[s_]])

        qtiles = [(C + 512 * j, 512, NKC) for j in range(16)]
        if not last:
            qtiles = [(0, C, C // 128)] + qtiles
        qi = 0
        pcount = 0
        load_head(0)
        for h in range(4):
            hs = h % 2
            if h + 1 < 4:
                load_head(h + 1)
            for (q0, nq, nkc) in qtiles:
                qs = qi % 2
                qi += 1
                fw.dma(sp, st[f"ldC{qs}"], lambda qs=qs, q0=q0, nq=nq, h=h: nc.sync.dma_start(out=qt[qs][:, 0:nq], in_=Sc["QT"][h, :, q0:q0 + nq]),
                       reads=[k.B_QT], writes=[B_qt[qs]])

                def scores(c):
                    sbuf_ = c % 2
                    for m in range(2):
                        fw.op(pe, lambda m=m, c=c, sbuf_=sbuf_: nc.tensor.matmul(
                            Sps[m][sbuf_][:, 0:nq], lhsT=kt[hs][m * 64:(m + 1) * 64, c * 128:(c + 1) * 128],
                            rhs=qt[qs][m * 64:(m + 1) * 64, 0:nq], start=True, stop=True),
                            reads=[B_kt[hs], B_qt[qs]], writes=[B_S[m][sbuf_]])

                scores(0)
                for c in range(nkc):
                    if c + 1 < nkc:
                        scores(c + 1)
                    sbuf_ = c % 2
                    pb = pcount % 3
                    pcount += 1
                    for m in range(2):
                        fw.op(act, lambda m=m, sbuf_=sbuf_, pb=pb: nc.scalar.activation(
                            out=P[m][pb][:, 0:nq], in_=Sps[m][sbuf_][:, 0:nq], func=AF.Exp, bias=negM[:, 0:1], scale=0.125),
                            reads=[B_S[m][sbuf_], B_s], writes=[B_P[m][pb]])
                    for m in range(2):
                        fw.op(pe, lambda m=m, c=c, pb=pb: nc.tensor.matmul(
                            Ops[m][:, 0:nq], lhsT=vh[hs][:, c, :], rhs=P[m][pb][:, 0:nq], start=(c == 0), stop=(c == nkc - 1)),
                            reads=[B_vh[hs], B_P[m][pb]], writes=[B_O[m]])
                        fw.op(pe, lambda m=m, c=c, pb=pb: nc.tensor.matmul(
                            Zps[m][:, 0:nq], lhsT=k.ones_b[:], rhs=P[m][pb][:, 0:nq], start=(c == 0), stop=(c == nkc - 1)),
                            reads=[k.B_const, B_P[m][pb]], writes=[B_Z[m]])
                for m in range(2):
                    fw.op(dve, lambda m=m: nc.vector.reciprocal(out=rz[m][:, 0:nq], in_=Zps[m][:, 0:nq]), reads=[B_Z[m]], writes=[B_rz[m]])
                fw.op(dve, lambda: nc.vector.tensor_tensor(out=o0[:, 0:nq], in0=Ops[0][:, 0:nq], in1=rz[0][:, 0:nq], op=ALU.mult),
                      reads=[B_O[0], B_rz[0]], writes=[B_o0])
                fw.op(dve, lambda: nc.vector.tensor_tensor(out=att[:, 0:nq], in0=Ops[1][:, 0:nq], in1=rz[1][:, 0:nq], op=ALU.mult),
                      reads=[B_O[1], B_rz[1]], writes=[B_att])
                fw.op(dve, lambda: nc.vector.scalar_tensor_tensor(out=att[:, 0:nq], in0=att[:, 0:nq], scalar=nlam[:, 0:1], in1=o0[:, 0:nq],
                                                                  op0=ALU.mult, op1=ALU.add), reads=[B_o0, B_s], writes=[B_att])
                fw.op(pool, lambda: nc.gpsimd.tensor_tensor(out=sqa[:, 0:nq], in0=att[:, 0:nq], in1=att[:, 0:nq], op=ALU.mult),
                      reads=[B_att], writes=[B_sqa])
                fw.op(pe, lambda: nc.tensor.matmul(Sps[0][0][:, 0:nq], lhsT=k.ones_f[:], rhs=sqa[:, 0:nq], start=True, stop=True),
                      reads=[k.B_const, B_sqa], writes=[B_S[0][0]])
                fw.op(act, lambda: nc.scalar.activation(out=rs[:, 0:nq], in_=Sps[0][0][:, 0:nq], func=AF.Sqrt, scale=1.0 / 128, bias=EPS),
                      reads=[B_S[0][0]], writes=[B_rs])
                fw.op(dve, lambda: nc.vector.reciprocal(out=rs[:, 0:nq], in_=rs[:, 0:nq]), writes=[B_rs])
                fw.op(dve, lambda qs=qs: nc.vector.scalar_tensor_tensor(out=ob[qs][:, 0:nq], in0=att[:, 0:nq], scalar=sg[:, 0:1], in1=rs[:, 0:nq],
                                                                        op0=ALU.mult, op1=ALU.mult), reads=[B_att, B_rs, B_s], writes=[B_ob[qs]])
                fw.dma(pool, st[f"stA{qs}"], lambda qs=qs, q0=q0, nq=nq, h=h: nc.gpsimd.dma_start(
                    out=Sc["MIXT"][512 + h * 128:512 + (h + 1) * 128, q0:q0 + nq], in_=ob[qs][:, 0:nq]),
                    reads=[B_ob[qs]], writes=[k.B_MIXT])
        fw.barrier()


def phase_wout(k, l):
    nc, fw, I, Sc = k.nc, k.fw, k.I, k.Sc
    pe, act, dve, pool, sp = fw.pe, fw.act, fw.dve, fw.pool, fw.sp
    st = k.st
    last = (l == DEPTH - 1)
    fw.barrier()
    with contextlib.ExitStack() as es:
        sb = lambda n_, shp, dt=F32: es.enter_context(nc.sbuf_tensor(f"{n_}_L{l}", list(shp), dt))
        pst = lambda n_, shp, dt=F32: es.enter_context(nc.psum_tensor(f"{n_}_L{l}", list(shp), dt))
        woutb = sb("woutb", [128, 10, D], BF16)
        wof = sb("wof", [128, 2, D], BF16)
        bdcs = sb("bdcs", [128, 2, 128], BF16)
        B_w = Buf("woutw")
        wv = I["w_out"][l].rearrange("(c p) n -> p c n", p=128)
        for c_ in range(2):
            fw.dma(pool, st["w0"], lambda c_=c_: nc.gpsimd.dma_start(out=wof[:, c_, :], in_=wv[:, c_, :]), writes=[B_w])
        for c_ in range(2, 8):
            fw.dma(pool, st["w0"], lambda c_=c_: nc.gpsimd.dma_start(out=woutb[:, c_ + 2, :], in_=wv[:, c_, :]), writes=[B_w])
        fw.dma(pool, st["w0"], lambda: nc.gpsimd.dma_start(out=bdcs[:, 0, :], in_=I["f_bdc"][:, :]), writes=[B_w])
        fw.dma(pool, st["w0"], lambda: nc.gpsimd.dma_start(out=bdcs[:, 1, :], in_=I["f_bds"][:, :]), writes=[B_w])
        yps = [pst(f"yps{i}", [128, D]) for i in range(2)]
        trp = pst("trp", [128, 8, 128], BF16)
        trl = pst("trl", [128, 8, 128], BF16)
        lgp = pst("lgp", [128, 64])
        B_yps = [Buf("yps0"), Buf("yps1")]; B_trp = Buf("trp"); B_trl = Buf("trl"); B_lgp = Buf("lgp")
        for ri in range(2):
            for c_ in range(2):
                for hf_ in range(2):
                    fw.op(pe, lambda ri=ri, c_=c_, hf_=hf_: nc.tensor.matmul(
                        yps[0][:, hf_ * 512:(hf_ + 1) * 512], lhsT=bdcs[:, ri, :], rhs=wof[:, c_, hf_ * 512:(hf_ + 1) * 512], start=True, stop=True),
                        reads=[B_w], writes=[B_yps[0]])
                fw.op(act, lambda ri=ri, c_=c_: nc.scalar.copy(out=woutb[:, ri * 2 + c_, :], in_=yps[0][:]), reads=[B_yps[0]], writes=[B_w])
        rows = sb("rows5", [128, 2, 3, D])
        n2 = sb("n2row", [128, D])
        B_rows = Buf("rows5")
        fw.dma(sp, st["misc"], lambda: nc.sync.dma_start(out=n2[:], in_=I["norm2_g"][l, :].partition_broadcast(128)), writes=[B_rows])
        for r in range(2):
            for j, mj in enumerate((2, 4, 3)):
                fw.dma(sp, st["misc"], lambda r=r, j=j, mj=mj: nc.sync.dma_start(
                    out=rows[:, r, j, :], in_=Sc["MOD"][l, r, mj * D:(mj + 1) * D].partition_broadcast(128)), reads=[k.B_MOD], writes=[B_rows])
            fw.op(dve, lambda r=r: nc.vector.scalar_tensor_tensor(out=rows[:, r, 1, :], in0=rows[:, r, 1, :], scalar=1.0, in1=n2[:],
                                                                 op0=ALU.add, op1=ALU.mult), writes=[B_rows])
        wr = sb("wr", [128, 8, 36]); brow = sb("brow5", [128, 36])
        fw.dma(sp, st["misc"], lambda: nc.sync.dma_start(out=wr[:, :, 0:4], in_=I["w_group"][l].rearrange("(c p) g -> p c g", p=128),
                                                         allow_slow_non_contiguous=True), writes=[B_rows])
        fw.dma(sp, st["misc"], lambda: nc.sync.dma_start(out=wr[:, :, 4:36], in_=I["w_router"][l].rearrange("(c p) g -> p c g", p=128),
                                                         allow_slow_non_contiguous=True), writes=[B_rows])
        wrh = sb("wrh", [128, 8, 36], BF16); wrl = sb("wrl", [128, 8, 36], BF16)
        fw.op(dve, lambda: nc.vector.tensor_copy(out=wrh[:], in_=wr[:]), writes=[B_rows])
        fw.op(dve, lambda: nc.vector.tensor_tensor(out=wrl[:], in0=wr[:], in1=wrh[:], op=ALU.subtract), writes=[B_rows])
        fw.dma(sp, st["misc"], lambda: nc.sync.dma_start(out=brow[:, 0:4], in_=I["b_group"][l, :].partition_broadcast(128)), writes=[B_rows])
        fw.dma(sp, st["misc"], lambda: nc.sync.dma_start(out=brow[:, 4:36], in_=I["b_router"][l, :].partition_broadcast(128)), writes=[B_rows])

        mix = [sb(f"mix{i}", [128, 10, 512], BF16) for i in range(2)]
        xt = [sb(f"x5_{i}", [128, D]) for i in range(2)]
        xnw = [sb(f"xnw{i}", [128, D]) for i in range(2)]
        junk = sb("junk5", [128, D], BF16)
        ss = sb("ss5", [128, 1]); rstd = sb("rstd5", [128, 1])
        h2 = sb("h2", [128, D])
        hib = sb("hib", [128, D], BF16); lob = sb("lob", [128, D], BF16)
        loT = sb("loT", [128, 8, 128], BF16)
        h2Tb = [sb(f"h2Tb{i}", [128, 8, 512], BF16) for i in range(2)]
        lg = sb("lg", [128, 36]); sm = sb("sm5", [128, 16]); t4 = sb("t4", [128, 4]); ohg = sb("ohg", [128, 4])
        em = sb("em", [128, 32]); em2 = sb("em2", [128, 32]); oh1 = sb("oh1", [128, 32]); oh2 = sb("oh2", [128, 32])
        B_mix = [Buf("mix0"), Buf("mix1")]; B_xt = [Buf("x50"), Buf("x51")]; B_xnw = [Buf("xnw0"), Buf("xnw1")]
        B_junk, B_ss, B_h2, B_hl, B_loT, B_rt = Buf("junk5"), Buf("ss5"), Buf("h2"), Buf("hilo"), Buf("loT"), Buf("route")
        B_h2Tb = [Buf("h2Tb0"), Buf("h2Tb1")]
        k.B_H2T = Buf("H2T")
        G = k.G_all
        groups = ([] if last else [(0, 2)]) + [(2 + 4 * g, 4) for g in range(16)]
        if "w5_a" in k.debug:
            groups = []
        ti = 0
        for gi, (t0, ng) in enumerate(groups):
            gs = gi % 2
            r = 1 if t0 == 0 else 0
            tok0, ntok = t0 * 128, ng * 128
            fw.dma(sp, st[f"ldA{gs}"], lambda gs=gs, tok0=tok0, ntok=ntok: nc.sync.dma_start(
                out=mix[gs][:, :, 0:ntok], in_=Sc["MIXT"][:, tok0:tok0 + ntok].rearrange("(c p) t -> p c t", p=128)),
                reads=[k.B_MIXT], writes=[B_mix[gs]])
            for t in range(ng):
                s_ = ti % 2
                ti += 1
                gt = t0 + t
                fw.dma(sp, st[f"ldB{s_}"], lambda s_=s_, gt=gt: nc.sync.dma_start(out=xt[s_][:], in_=Sc["XR"][gt * 128:(gt + 1) * 128, :]),
                       reads=[k.B_XR[gt]], writes=[B_xt[s_]])
                for hf_ in range(2):
                    for c_ in range(10):
                        fw.op(pe, lambda s_=s_, hf_=hf_, c_=c_, t=t, gs=gs: nc.tensor.matmul(
                            yps[s_][:, hf_ * 512:(hf_ + 1) * 512], lhsT=mix[gs][:, c_, t * 128:(t + 1) * 128],
                            rhs=woutb[:, c_, hf_ * 512:(hf_ + 1) * 512], start=(c_ == 0), stop=(c_ == 9)),
                            reads=[B_mix[gs], B_w], writes=[B_yps[s_]])
                fw.op(dve, lambda s_=s_, r=r: nc.vector.tensor_tensor(out=xnw[s_][:], in0=yps[s_][:], in1=rows[:, r, 0, :], op=ALU.mult),
                      reads=[B_yps[s_], B_rows], writes=[B_xnw[s_]])
                fw.op(pool, lambda s_=s_: nc.gpsimd.tensor_tensor(out=xnw[s_][:], in0=xnw[s_][:], in1=xt[s_][:], op=ALU.add),
                      reads=[B_xt[s_]], writes=[B_xnw[s_]])
                fw.dma(pool, st[f"stA{s_}"], lambda s_=s_, gt=gt: nc.gpsimd.dma_start(out=Sc["XR"][gt * 128:(gt + 1) * 128, :], in_=xnw[s_][:]),
                       reads=[B_xnw[s_]], writes=[k.B_XR[gt]])
                if "w5_b" in k.debug:
                    continue
                fw.op(act, lambda s_=s_: nc.scalar.activation(out=junk[:], in_=xnw[s_][:], func=AF.Square, accum_out=ss[:]),
                      reads=[B_xnw[s_]], writes=[B_junk, B_ss])
                fw.op(act, lambda: nc.scalar.activation(out=rstd[:], in_=ss[:], func=AF.Sqrt, scale=1.0 / D, bias=EPS), writes=[B_ss])
                fw.op(dve, lambda: nc.vector.reciprocal(out=rstd[:], in_=rstd[:]), writes=[B_ss])
                fw.op(dve, lambda s_=s_, r=r: nc.vector.scalar_tensor_tensor(out=h2[:], in0=xnw[s_][:], scalar=rstd[:, 0:1], in1=rows[:, r, 1, :],
                                                                           op0=ALU.mult, op1=ALU.mult), reads=[B_xnw[s_], B_ss, B_rows], writes=[B_h2])
                fw.op(pool, lambda r=r: nc.gpsimd.tensor_tensor(out=h2[:], in0=h2[:], in1=rows[:, r, 2, :], op=ALU.add), reads=[B_rows], writes=[B_h2])
                fw.op(act, lambda: nc.scalar.copy(out=hib[:], in_=h2[:]), reads=[B_h2], writes=[B_hl])
                fw.op(dve, lambda: nc.vector.tensor_tensor(out=lob[:], in0=h2[:], in1=hib[:], op=ALU.subtract), reads=[B_h2], writes=[B_hl])
                for kk in range(8):
                    fw.op(pe, lambda kk=kk: nc.tensor.transpose(trp[:, kk, :], hib[:, kk * 128:(kk + 1) * 128], k.ident_b[:]),
                          reads=[B_hl, k.B_const], writes=[B_trp])
                for kk in range(8):
                    fw.op(pe, lambda kk=kk: nc.tensor.transpose(trl[:, kk, :], lob[:, kk * 128:(kk + 1) * 128], k.ident_b[:]),
                          reads=[B_hl, k.B_const], writes=[B_trl])
                fw.op(act, lambda t=t, gs=gs: nc.scalar.copy(out=h2Tb[gs][:, :, t * 128:(t + 1) * 128], in_=trp[:]),
                      reads=[B_trp], writes=[B_h2Tb[gs]])
                fw.op(dve, lambda: nc.vector.tensor_copy(out=loT[:], in_=trl[:]), reads=[B_trl], writes=[B_loT])
                nmm = 0
                for (lh, wv_) in (("hi", wrh), ("lo", wrh), ("hi", wrl)):
                    for kk in range(8):
                        lhs = h2Tb[gs][:, kk, t * 128:(t + 1) * 128] if lh == "hi" else loT[:, kk, :]
                        fw.op(pe, lambda lhs=lhs, wv_=wv_, kk=kk, nmm=nmm: nc.tensor.matmul(
                            lgp[:, 0:36], lhsT=lhs, rhs=wv_[:, kk, :], start=(nmm == 0), stop=(nmm == 23)),
                            reads=[B_h2Tb[gs], B_loT, B_rows], writes=[B_lgp])
                        nmm += 1
                if "w5_c" in k.debug:
                    continue
                R_ = [B_rt]
                V = nc.vector
                fw.op(dve, lambda: V.tensor_tensor(out=lg[:], in0=lgp[:, 0:36], in1=brow[:], op=ALU.add), reads=[B_lgp, B_rows], writes=R_)
                fw.op(dve, lambda: V.tensor_reduce(out=sm[:, 0:1], in_=lg[:, 0:4], axis=AX.X, op=ALU.max), writes=R_)
                fw.op(dve, lambda: V.tensor_scalar(out=ohg[:], in0=lg[:, 0:4], scalar1=sm[:, 0:1], scalar2=None, op0=ALU.is_ge), writes=R_)
                fw.op(dve, lambda: V.tensor_scalar(out=t4[:], in0=lg[:, 0:4], scalar1=sm[:, 0:1], scalar2=None, op0=ALU.subtract), writes=R_)
                fw.op(act, lambda: nc.scalar.activation(out=t4[:], in_=t4[:], func=AF.Exp), writes=R_)
                fw.op(dve, lambda: V.tensor_reduce(out=sm[:, 1:2], in_=t4[:], axis=AX.X, op=ALU.add), writes=R_)
                fw.op(dve, lambda: V.reciprocal(out=sm[:, 2:3], in_=sm[:, 1:2]), writes=R_)
                fw.op(dve, lambda: V.tensor_scalar(out=t4[:], in0=ohg[:], scalar1=-1.0, scalar2=1e30, op0=ALU.add, op1=ALU.mult), writes=R_)
                fw.op(dve, lambda: V.tensor_tensor(out=em[:].rearrange("p (g e) -> p g e", e=8), in0=lg[:, 4:36].rearrange("p (g e) -> p g e", e=8),
                                                   in1=bc(t4[:, :].unsqueeze(2), [128, 4, 8]), op=ALU.add), writes=R_)
                fw.op(dve, lambda: V.tensor_reduce(out=sm[:, 3:4], in_=em[:], axis=AX.X, op=ALU.max), writes=R_)
                fw.op(dve, lambda: V.tensor_scalar(out=oh1[:], in0=em[:], scalar1=sm[:, 3:4], scalar2=None, op0=ALU.is_ge), writes=R_)
                fw.op(dve, lambda: V.scalar_tensor_tensor(out=em2[:], in0=oh1[:], scalar=-1e30, in1=em[:], op0=ALU.mult, op1=ALU.add), writes=R_)
                fw.op(dve, lambda: V.tensor_reduce(out=sm[:, 4:5], in_=em2[:], axis=AX.X, op=ALU.max), writes=R_)
                fw.op(dve, lambda: V.tensor_scalar(out=oh2[:], in0=em2[:], scalar1=sm[:, 4:5], scalar2=None, op0=ALU.is_ge), writes=R_)
                fw.op(dve, lambda: V.tensor_tensor(out=sm[:, 5:6], in0=sm[:, 4:5], in1=sm[:, 3:4], op=ALU.subtract), writes=R_)
                fw.op(act, lambda: nc.scalar.activation(out=sm[:, 5:6], in_=sm[:, 5:6], func=AF.Exp), writes=R_)
                fw.op(dve, lambda: V.tensor_scalar(out=sm[:, 6:7], in0=sm[:, 5:6], scalar1=1.0, scalar2=None, op0=ALU.add), writes=R_)
                fw.op(dve, lambda: V.reciprocal(out=sm[:, 6:7], in_=sm[:, 6:7]), writes=R_)
                fw.op(dve, lambda: V.tensor_tensor(out=sm[:, 7:8], in0=sm[:, 6:7], in1=sm[:, 2:3], op=ALU.mult), writes=R_)
                fw.op(dve, lambda: V.tensor_tensor(out=sm[:, 8:9], in0=sm[:, 2:3], in1=sm[:, 7:8], op=ALU.subtract), writes=R_)
                fw.op(dve, lambda: V.tensor_scalar(out=oh1[:], in0=oh1[:], scalar1=sm[:, 7:8], scalar2=None, op0=ALU.mult), writes=R_)
                fw.op(dve, lambda gt=gt: V.scalar_tensor_tensor(out=G[:, gt, :], in0=oh2[:], scalar=sm[:, 8:9], in1=oh1[:], op0=ALU.mult, op1=ALU.add),
                      reads=R_, writes=[k.B_G])
            fw.dma(pool, st[f"stB{gs}"], lambda gs=gs, tok0=tok0, ntok=ntok: nc.gpsimd.dma_start(
                out=Sc["H2T"][:, tok0:tok0 + ntok].rearrange("(c p) t -> p c t", p=128), in_=h2Tb[gs][:, :, 0:ntok]),
                reads=[B_h2Tb[gs]], writes=[k.B_H2T])
        fw.barrier()


def phase_moe(k, l):
    nc, fw, I, Sc = k.nc, k.fw, k.I, k.Sc
    pe, act, dve, pool, sp = fw.pe, fw.act, fw.dve, fw.pool, fw.sp
    st = k.st
    last = (l == DEPTH - 1)
    fw.barrier()
    with contextlib.ExitStack() as es:
        sb = lambda n_, shp, dt=F32: es.enter_context(nc.sbuf_tensor(f"{n_}_L{l}", list(shp), dt))
        pst = lambda n_, shp, dt=F32: es.enter_context(nc.psum_tensor(f"{n_}_L{l}", list(shp), dt))
        GT = 2048
        xT = sb("xT6", [128, 8, GT], BF16)
        acc = sb("acc6", [128, GT // 128, D])
        w1b = [sb(f"w1b{i}", [128, 8, DE], BF16) for i in range(2)]
        w3b = [sb(f"w3b{i}", [128, 8, DE], BF16) for i in range(2)]
        w2b = [sb(f"w2b{i}", [128, 4, D], BF16) for i in range(2)]
        sl = [sb(f"sl{i}", [128, 512]) for i in range(2)]
        hT = [sb(f"hT6_{i}", [128, 4, 512], BF16) for i in range(2)]
        g2r = sb("g2r", [128, 2, D])
        xt = [sb(f"x6_{i}", [128, D]) for i in range(2)]
        h1p = [pst(f"h1p{i}", [128, 512]) for i in range(2)]
        h3p = [pst(f"h3p{i}", [128, 512]) for i in range(2)]
        yp = [pst(f"yp{i}", [128, D]) for i in range(2)]
        B_xT, B_acc = Buf("xT6"), Buf("acc6")
        B_wt = [Buf("wt0"), Buf("wt1")]
        B_sl = [Buf("sl0"), Buf("sl1")]; B_hT = [Buf("hT0"), Buf("hT1")]; B_g2r = Buf("g2r"); B_xt = [Buf("x60"), Buf("x61")]
        B_h1p = [Buf("h1p0"), Buf("h1p1")]; B_h3p = [Buf("h3p0"), Buf("h3p1")]; B_yp = [Buf("yp0"), Buf("yp1")]
        G = k.G_all
        for r in range(2):
            fw.dma(sp, st["misc"], lambda r=r: nc.sync.dma_start(out=g2r[:, r, :], in_=Sc["MOD"][l, r, 5 * D:6 * D].partition_broadcast(128)),
                   reads=[k.B_MOD], writes=[B_g2r])
        groups = ([] if last else [(0, C)]) + [(C + GT * j, GT) for j in range(4)]
        wcount = 0

        def load_w(e, ws):
            fw.dma(pool, st[f"w{ws}"], lambda: nc.gpsimd.dma_start(out=w1b[ws][:], in_=I["w1"][l, e].rearrange("(c p) f -> p c f", p=128)),
                   writes=[B_wt[ws]])
            fw.dma(pool, st[f"w{ws}"], lambda: nc.gpsimd.dma_start(out=w3b[ws][:], in_=I["w3"][l, e].rearrange("(c p) f -> p c f", p=128)),
                   writes=[B_wt[ws]])
            fw.dma(pool, st[f"w{ws}"], lambda: nc.gpsimd.dma_start(out=w2b[ws][:], in_=I["w2"][l, e].rearrange("(c p) n -> p c n", p=128)),
                   writes=[B_wt[ws]])

        hcount = 0
        ycount = 0
        for (tok0, ntok) in groups:
            r = 1 if tok0 == 0 else 0
            ntile = ntok // 128
            fw.dma(sp, st["ldA0"], lambda tok0=tok0, ntok=ntok: nc.sync.dma_start(
                out=xT[:, :, 0:ntok], in_=Sc["H2T"][:, tok0:tok0 + ntok].rearrange("(c p) t -> p c t", p=128)),
                reads=[k.B_H2T], writes=[B_xT])
            load_w(0, wcount % 2)
            for e in range(NE):
                ws = wcount % 2
                wcount += 1
                if e + 1 < NE:
                    load_w(e + 1, wcount % 2)
                for b0 in range(0, ntok, 512):
                    nb_ = min(512, ntok - b0)
                    hs = hcount % 2
                    hcount += 1
                    for fc in range(4):
                        ps_ = fc % 2
                        for kk in range(8):
                            fw.op(pe, lambda ps_=ps_, kk=kk, fc=fc, ws=ws, b0=b0, nb_=nb_: nc.tensor.matmul(
                                h1p[ps_][:, 0:nb_], lhsT=w1b[ws][:, kk, fc * 128:(fc + 1) * 128], rhs=xT[:, kk, b0:b0 + nb_],
                                start=(kk == 0), stop=(kk == 7)), reads=[B_wt[ws], B_xT], writes=[B_h1p[ps_]])
                        for kk in range(8):
                            fw.op(pe, lambda ps_=ps_, kk=kk, fc=fc, ws=ws, b0=b0, nb_=nb_: nc.tensor.matmul(
                                h3p[ps_][:, 0:nb_], lhsT=w3b[ws][:, kk, fc * 128:(fc + 1) * 128], rhs=xT[:, kk, b0:b0 + nb_],
                                start=(kk == 0), stop=(kk == 7)), reads=[B_wt[ws], B_xT], writes=[B_h3p[ps_]])
                        fw.op(act, lambda ps_=ps_, nb_=nb_: nc.scalar.activation(out=sl[ps_][:, 0:nb_], in_=h1p[ps_][:, 0:nb_], func=AF.Silu),
                              reads=[B_h1p[ps_]], writes=[B_sl[ps_]])
                        fw.op(dve, lambda ps_=ps_, nb_=nb_, hs=hs, fc=fc: nc.vector.tensor_tensor(
                            out=hT[hs][:, fc, 0:nb_], in0=h3p[ps_][:, 0:nb_], in1=sl[ps_][:, 0:nb_], op=ALU.mult),
                            reads=[B_h3p[ps_], B_sl[ps_]], writes=[B_hT[hs]])
                    for t in range(nb_ // 128):
                        ys = ycount % 2
                        ycount += 1
                        lt = b0 // 128 + t
                        gt = tok0 // 128 + lt
                        for hf_ in range(2):
                            for fc in range(4):
                                fw.op(pe, lambda ys=ys, hf_=hf_, fc=fc, hs=hs, t=t, ws=ws: nc.tensor.matmul(
                                    yp[ys][:, hf_ * 512:(hf_ + 1) * 512], lhsT=hT[hs][:, fc, t * 128:(t + 1) * 128],
                                    rhs=w2b[ws][:, fc, hf_ * 512:(hf_ + 1) * 512], start=(fc == 0), stop=(fc == 3)),
                                    reads=[B_hT[hs], B_wt[ws]], writes=[B_yp[ys]])
                        if e == 0:
                            fw.op(dve, lambda ys=ys, lt=lt, gt=gt, e=e: nc.vector.tensor_scalar(
                                out=acc[:, lt, :], in0=yp[ys][:], scalar1=G[:, gt, e:e + 1], scalar2=None, op0=ALU.mult),
                                reads=[B_yp[ys], k.B_G], writes=[B_acc])
                        else:
                            fw.op(dve, lambda ys=ys, lt=lt, gt=gt, e=e: nc.vector.scalar_tensor_tensor(
                                out=acc[:, lt, :], in0=yp[ys][:], scalar=G[:, gt, e:e + 1], in1=acc[:, lt, :], op0=ALU.mult, op1=ALU.add),
                                reads=[B_yp[ys], k.B_G], writes=[B_acc])
            for lt in range(ntile):
                gt = tok0 // 128 + lt
                s_ = lt % 2
                fw.dma(sp, st[f"ldB{s_}"], lambda s_=s_, gt=gt: nc.sync.dma_start(out=xt[s_][:], in_=Sc["XR"][gt * 128:(gt + 1) * 128, :]),
                       reads=[k.B_XR[gt]], writes=[B_xt[s_]])
                fw.op(pool, lambda lt=lt, r=r: nc.gpsimd.tensor_tensor(out=acc[:, lt, :], in0=acc[:, lt, :], in1=g2r[:, r, :], op=ALU.mult),
                      reads=[B_g2r], writes=[B_acc])
                fw.op(dve, lambda lt=lt, s_=s_: nc.vector.tensor_tensor(out=xt[s_][:], in0=xt[s_][:], in1=acc[:, lt, :], op=ALU.add),
                      reads=[B_acc], writes=[B_xt[s_]])
                if last:
                    xrow = gt * 128 - C
                    fw.dma(sp, st[f"stA{s_}"], lambda s_=s_, xrow=xrow: nc.sync.dma_start(out=k.out[xrow:xrow + 128, :], in_=xt[s_][:]),
                           reads=[B_xt[s_]], writes=[k.B_OUT])
                else:
                    fw.dma(sp, st[f"stA{s_}"], lambda s_=s_, gt=gt: nc.sync.dma_start(out=Sc["XR"][gt * 128:(gt + 1) * 128, :], in_=xt[s_][:]),
                           reads=[B_xt[s_]], writes=[k.B_XR[gt]])
        fw.barrier()
```

```python
import math
import contextlib
import numpy as np
import concourse.bass as bass
import concourse.mybir as mybir
from concourse.bass_utils import run_bass_kernel_spmd
from concourse.alu_op_type import AluOpType as ALU

F32 = mybir.dt.float32
BF16 = mybir.dt.bfloat16
I32 = mybir.dt.int32
U32 = mybir.dt.uint32
AF = mybir.ActivationFunctionType
AX = mybir.AxisListType

D = 1024
S = 8192
C = 256
T = S + C
NT = T // 128
DEPTH = 2
INW = 2304
EPS = 1e-6
NE = 32
DE = 512
BLK = 512
NB = (2 * T + BLK - 1) // BLK + NE
NSLOT = NB * BLK


class Eng:
    def __init__(self, fw, name, h):
        self.fw, self.name, self.h = fw, name, h
        self.sem = fw.new_sem("e_" + name)
        self.n = 0
        self.seen = {}


class Stream:
    def __init__(self, fw, name):
        self.sem = fw.new_sem("s_" + name)
        self.n = 0
        self.name = name
        self.twin = None


class Buf:
    __slots__ = ("name", "w", "r")

    def __init__(self, name):
        self.name = name
        self.w = None
        self.r = {}


class FW:
    def __init__(self, nc):
        self.nc = nc
        self.es = contextlib.ExitStack()
        self.nsem = 0
        self.pe = Eng(self, "pe", nc.tensor)
        self.act = Eng(self, "act", nc.scalar)
        self.dve = Eng(self, "dve", nc.vector)
        self.pool = Eng(self, "pool", nc.gpsimd)
        self.sp = Eng(self, "sp", nc.sync)
        self.engs = [self.pe, self.act, self.dve, self.pool, self.sp]
        self.streams = []

    def new_sem(self, name):
        self.nsem += 1
        return self.es.enter_context(self.nc.semaphore(name))

    def stream(self, name):
        s = Stream(self, name)
        self.streams.append(s)
        return s

    def _need(self, E, dep):
        if dep is None:
            return
        if dep[0] == "E":
            _, Fe, idx = dep
            if Fe is E and E is self.pe:
                return
            if E.seen.get(Fe, 0) >= idx:
                return
            E.h.wait_ge(Fe.sem, idx)
            E.seen[Fe] = idx
        else:
            _, St, idx = dep
            if E.seen.get(St, 0) >= idx:
                return
            E.h.wait_ge(St.sem, 16 * St.n)
            E.seen[St] = St.n

    def _deps(self, E, reads, writes):
        for b in reads:
            self._need(E, b.w)
        for b in writes:
            self._need(E, b.w)
            for d in b.r.values():
                self._need(E, d)

    def op(self, E, fn, reads=(), writes=()):
        self._deps(E, reads, writes)
        ins = fn()
        E.n += 1
        ins.then_inc(E.sem, 1)
        tok = ("E", E, E.n)
        for b in reads:
            b.r[E] = tok
        for b in writes:
            b.w = tok
            b.r = {}
        return ins

    def dma(self, Q, St, fn, reads=(), writes=()):
        if Q is self.pool:
            if St.twin is None:
                St.twin = Stream(self, St.name + "_sw")
                self.streams.append(St.twin)
            St = St.twin
        self._deps(Q, reads, writes)
        ins = fn()
        St.n += 1
        ins.then_inc(St.sem, 16)
        tok = ("D", St, St.n)
        for b in reads:
            b.r[St] = tok
        for b in writes:
            b.w = tok
            b.r = {}
        return ins

    def barrier(self):
        for E in self.engs:
            for Fe in self.engs:
                if Fe is not E and Fe.n > 0:
                    self._need(E, ("E", Fe, Fe.n))
            for St in self.streams:
                if St.n > 0:
                    self._need(E, ("D", St, St.n))


def bc(ap, shape):
    return ap.broadcast_to(list(shape))


class K:
    pass


def build(debug=()):
    nc = bass.Bass("TRN2", target_bir_lowering=False)
    fw = FW(nc)
    k = K()
    k.nc, k.fw, k.debug = nc, fw, set(debug)
    pe, act, dve, pool, sp = fw.pe, fw.act, fw.dve, fw.pool, fw.sp

    def din(name, shape, dt=F32):
        return nc.dram_tensor(name, list(shape), dt, kind="ExternalInput").ap()

    def dscr(name, shape, dt=F32):
        kind = "ExternalOutput" if name in k.debug else "Internal"
        return nc.dram_tensor(name, list(shape), dt, kind=kind).ap()

    I = {}
    I["xin"] = din("xin", [T, D])
    I["cvec"] = din("cvec", [2, D])
    I["w_mod"] = din("w_mod", [DEPTH, D, 6 * D])
    I["b_mod"] = din("b_mod", [DEPTH, 6 * D])
    I["norm1_g"] = din("norm1_g", [DEPTH, D])
    I["norm2_g"] = din("norm2_g", [DEPTH, D])
    I["w_in"] = din("w_in", [DEPTH, D, INW])
    for n_ in ("q_norm_g", "k_norm_g", "lambda_q1", "lambda_k1", "lambda_q2", "lambda_k2"):
        I[n_] = din(n_, [DEPTH, 64])
    I["subln_g"] = din("subln_g", [DEPTH, 128])
    I["conv_w"] = din("conv_w", [DEPTH, 4, 256])
    I["conv_b"] = din("conv_b", [DEPTH, 256])
    I["gate_a_w"] = din("gate_a_w", [DEPTH, 2, 4, 64, 64])
    I["gate_a_b"] = din("gate_a_b", [DEPTH, 2, 256])
    I["gate_x_w"] = din("gate_x_w", [DEPTH, 2, 4, 64, 64])
    I["gate_x_b"] = din("gate_x_b", [DEPTH, 2, 256])
    I["lru_lambda"] = din("lru_lambda", [DEPTH, 2, 256])
    I["w_out"] = din("w_out", [DEPTH, D, D])
    I["w_group"] = din("w_group", [DEPTH, D, 4])
    I["b_group"] = din("b_group", [DEPTH, 4])
    I["w_router"] = din("w_router", [DEPTH, D, NE])
    I["b_router"] = din("b_router", [DEPTH, NE])
    I["w1"] = din("w1", [DEPTH, NE, D, DE])
    I["w3"] = din("w3", [DEPTH, NE, D, DE])
    I["w2"] = din("w2", [DEPTH, NE, DE, D])
    I["ident"] = din("ident", [128, 128])
    I["rope_cos"] = din("rope_cos", [S, 64])
    I["rope_sin"] = din("rope_sin", [S, 64])
    I["f_cs64"] = din("f_cs64", [64, 128])
    I["f_c128"] = din("f_c128", [128, 64, 128])
    I["f_s128"] = din("f_s128", [128, 64, 128])
    I["f_n128"] = din("f_n128", [128, 64, 128])
    I["f_c256"] = din("f_c256", [256, 256])
    I["f_s256"] = din("f_s256", [256, 256])
    I["f_bdc"] = din("f_bdc", [128, 128])
    I["f_bds"] = din("f_bds", [128, 128])
    I["m_thr"] = din("m_thr", [128, 40])
    I["m_bst"] = din("m_bst", [128, NB])
    I["m_pidx"] = din("m_pidx", [128, 1])
    I["m_utri"] = din("m_utri", [128, 128])
    k.I = I
    out = nc.dram_tensor("out", [S, D], F32, kind="ExternalOutput").ap()
    k.out = out

    Sc = {}
    Sc["XR"] = dscr("XR", [T, D])
    Sc["MOD"] = dscr("MOD", [DEPTH, 2, 6 * D])
    Sc["UF"] = dscr("UF", [T, 256], BF16)
    Sc["V"] = dscr("V", [T, 512], BF16)
    Sc["QT"] = dscr("QT", [4, 128, T], BF16)
    Sc["KT"] = dscr("KT", [4, 128, T], BF16)
    Sc["YRT"] = dscr("YRT", [512, T])
    Sc["MIXT"] = dscr("MIXT", [1280, T], BF16)
    Sc["HFD"] = dscr("HFD", [256, T])
    Sc["H2"] = dscr("H2", [T, D], BF16)
    Sc["XS"] = dscr("XS", [NSLOT, D], BF16)
    Sc["YS"] = dscr("YS", [NSLOT, D])
    k.B_MIXT = Buf("MIXT")
    k.Sc = Sc

    es = fw.es
    ident_f = es.enter_context(nc.sbuf_tensor("ident_f", [128, 128], F32))
    ident_b = es.enter_context(nc.sbuf_tensor("ident_b", [128, 128], BF16))
    ones_b = es.enter_context(nc.sbuf_tensor("ones_b", [128, 128], BF16))
    ones_f = es.enter_context(nc.sbuf_tensor("ones_f", [128, 128], F32))
    k.ident_f, k.ident_b, k.ones_b, k.ones_f = ident_f, ident_b, ones_b, ones_f
    k.OH1 = es.enter_context(nc.sbuf_tensor("OH1", [128, NT, NE], BF16))
    k.OH2 = es.enter_context(nc.sbuf_tensor("OH2", [128, NT, NE], BF16))
    k.OH12 = es.enter_context(nc.sbuf_tensor("OH12", [128, NT, NE], BF16))
    k.W12 = es.enter_context(nc.sbuf_tensor("W12", [128, NT, 2], F32))
    k.DEST = es.enter_context(nc.sbuf_tensor("DEST", [128, NT, 2], I32))
    k.IDXW = es.enter_context(nc.sbuf_tensor("IDXW", [128, NB], I32))
    k.B_G = Buf("routing")
    k.reg_slot = nc.gpsimd.to_reg(NSLOT - 1)
    k.reg_w = nc.gpsimd.to_reg(DEPTH * NE * 128 - 1)
    k.B_DEST = Buf("dest")
    k.B_IDXW = Buf("idxw")
    k.B_OUT = Buf("OUT")
    k.B_const = Buf("const")
    st0 = fw.stream("const")
    k.st_const = st0
    fw.dma(sp, st0, lambda: nc.sync.dma_start(out=ident_f[:], in_=I["ident"][:, :]), writes=[k.B_const])
    fw.op(dve, lambda: nc.vector.tensor_copy(out=ident_b[:], in_=ident_f[:]), writes=[k.B_const])
    fw.op(dve, lambda: nc.vector.memset(ones_b[:], 1.0), writes=[k.B_const])
    fw.op(dve, lambda: nc.vector.memset(ones_f[:], 1.0), writes=[k.B_const])

    k.st = {n_: fw.stream(n_) for n_ in ("ldA0", "ldA1", "ldB0", "ldB1", "ldC0", "ldC1", "stA0", "stA1", "stB0", "stB1",
                                         "stC0", "stC1", "stD0", "stD1", "w0", "w1", "w2", "misc")}

    k.B_XR = [Buf(f"XR{i}") for i in range(NT)]
    for i in range(0, NT, 11):
        fw.dma(sp, k.st["misc"], lambda i=i: nc.sync.dma_start(out=Sc["XR"][i * 128:(i + 11) * 128, :],
                                                               in_=I["xin"][i * 128:(i + 11) * 128, :]),
               writes=k.B_XR[i:i + 11])

    for l in range(DEPTH):
        if "stop_before_mod" in k.debug:
            break
        phase_mod(k, l)
        if f"stop_mod{l}" in k.debug:
            break
        phase_proj(k, l)
        if f"stop_proj{l}" in k.debug:
            break
        phase_lru(k, l)
        if f"stop_lru{l}" in k.debug:
            break
        phase_fourier(k, l)
        if f"stop_fourier{l}" in k.debug:
            break
        phase_attn(k, l)
        if f"stop_attn{l}" in k.debug:
            break
        phase_wout(k, l)
        if f"stop_wout{l}" in k.debug:
            break
        phase_moe(k, l)
        if f"stop_moe{l}" in k.debug:
            break

    fw.barrier()
    fw.es.close()
    return nc


def phase_mod(k, l):
    nc, fw, I, Sc = k.nc, k.fw, k.I, k.Sc
    pe, act, dve, pool, sp = fw.pe, fw.act, fw.dve, fw.pool, fw.sp
    fw.barrier()
    with contextlib.ExitStack() as es:
        ccol = es.enter_context(nc.sbuf_tensor(f"ccol{l}", [128, 2, 8], F32))
        scol = es.enter_context(nc.sbuf_tensor(f"scol{l}", [128, 2, 8], F32))
        brow = es.enter_context(nc.sbuf_tensor(f"brow{l}", [1, 6 * D], F32))
        mrow = es.enter_context(nc.sbuf_tensor(f"mrow{l}", [1, 2, 6 * D], F32))
        wbuf = [es.enter_context(nc.sbuf_tensor(f"wmod{i}_{l}", [128, 8, 512], F32)) for i in range(2)]
        ps = [es.enter_context(nc.psum_tensor(f"psmod{i}_{l}", [1, 512], F32)) for i in range(2)]
        B_c, B_s, B_b, B_m = Buf("ccol"), Buf("scol"), Buf("brow"), Buf("mrow")
        B_w = [Buf("wmod0"), Buf("wmod1")]
        B_ps = [Buf("psmod0"), Buf("psmod1")]
        st = k.st
        for r in range(2):
            fw.dma(sp, st["misc"], lambda r=r: nc.sync.dma_start(
                out=ccol[:, r, :], in_=I["cvec"][r, :].rearrange("(k p) -> p k", p=128),
                allow_slow_non_contiguous=True), writes=[B_c])
        fw.dma(sp, st["misc"], lambda: nc.sync.dma_start(out=brow[:], in_=I["b_mod"][l:l + 1, :]), writes=[B_b])
        fw.op(act, lambda: nc.scalar.activation(out=scol[:], in_=ccol[:], func=AF.Silu), reads=[B_c], writes=[B_s])
        wv = I["w_mod"][l].rearrange("(k p) n -> p k n", p=128)
        for nb in range(12):
            sl = nb % 2
            fw.dma(sp, st[f"w{sl}"], lambda nb=nb, sl=sl: nc.sync.dma_start(
                out=wbuf[sl][:], in_=wv[:, :, nb * 512:(nb + 1) * 512]), writes=[B_w[sl]])
            for r in range(2):
                for kk in range(8):
                    fw.op(pe, lambda r=r, kk=kk, sl=sl: nc.tensor.matmul(
                        ps[r][:], lhsT=scol[:, r, kk:kk + 1], rhs=wbuf[sl][:, kk, :], start=(kk == 0), stop=(kk == 7)),
                        reads=[B_s, B_w[sl]], writes=[B_ps[r]])
                fw.op(dve, lambda r=r, nb=nb: nc.vector.tensor_tensor(
                    out=mrow[:, r, nb * 512:(nb + 1) * 512], in0=ps[r][:], in1=brow[:, nb * 512:(nb + 1) * 512], op=ALU.add),
                    reads=[B_ps[r], B_b], writes=[B_m])
        k.B_MOD = Buf("MOD")
        for r in range(2):
            fw.dma(sp, st["misc"], lambda r=r: nc.sync.dma_start(out=Sc["MOD"][l, r:r + 1, :], in_=mrow[:, r, :]),
                   reads=[B_m], writes=[k.B_MOD])
        fw.barrier()


def phase_proj(k, l):
    nc, fw, I, Sc = k.nc, k.fw, k.I, k.Sc
    pe, act, dve, pool, sp = fw.pe, fw.act, fw.dve, fw.pool, fw.sp
    st = k.st
    fw.barrier()
    with contextlib.ExitStack() as es:
        sb = lambda n_, shp, dt=F32: es.enter_context(nc.sbuf_tensor(f"{n_}_L{l}", list(shp), dt))
        pst = lambda n_, shp, dt=F32: es.enter_context(nc.psum_tensor(f"{n_}_L{l}", list(shp), dt))
        winb = sb("winb", [128, 8, INW], BF16)
        B_win = Buf("winb")
        wv = I["w_in"][l].rearrange("(k p) n -> p k n", p=128)
        for kk in range(8):
            fw.dma(pool, st["w0"], lambda kk=kk: nc.gpsimd.dma_start(out=winb[:, kk, :], in_=wv[:, kk, :],
                                                                      max_dma_last_dim=2048),
                   writes=[B_win])
        cols = sb("p1cols", [128, 2, 3, 8])
        A1 = sb("p1A", [128, 2, 8])
        B_cols, B_A1 = Buf("cols"), Buf("A1")
        for r in range(2):
            for j in range(2):
                fw.dma(sp, st["misc"], lambda r=r, j=j: nc.sync.dma_start(
                    out=cols[:, r, j, :], in_=Sc["MOD"][l, r, j * D:(j + 1) * D].rearrange("(k p) -> p k", p=128),
                    allow_slow_non_contiguous=True), reads=[k.B_MOD], writes=[B_cols])
            fw.dma(sp, st["misc"], lambda r=r: nc.sync.dma_start(
                out=cols[:, r, 2, :], in_=I["norm1_g"][l, :].rearrange("(k p) -> p k", p=128),
                allow_slow_non_contiguous=True), writes=[B_cols])
        for r in range(2):
            fw.op(dve, lambda r=r: nc.vector.scalar_tensor_tensor(
                out=A1[:, r, :], in0=cols[:, r, 1, :], scalar=1.0, in1=cols[:, r, 2, :], op0=ALU.add, op1=ALU.mult),
                reads=[B_cols], writes=[B_A1])
        gqk = sb("gqk", [128, 2, 64])
        B_gqk = Buf("gqk")
        fw.dma(sp, st["misc"], lambda: nc.sync.dma_start(out=gqk[:, 0, :], in_=I["q_norm_g"][l, :].partition_broadcast(128)),
               writes=[B_gqk])
        fw.dma(sp, st["misc"], lambda: nc.sync.dma_start(out=gqk[:, 1, :], in_=I["k_norm_g"][l, :].partition_broadcast(128)),
               writes=[B_gqk])

        NS = 2
        xt = [sb(f"xt{i}", [128, D]) for i in range(NS)]
        junk = sb("junk", [128, D], BF16)
        xn = [sb(f"xn{i}", [128, D], BF16) for i in range(NS)]
        ss = [sb(f"ss{i}", [128, 1]) for i in range(NS)]
        rstd = [sb(f"rstd{i}", [128, 1]) for i in range(NS)]
        hT = [sb(f"hT{i}", [128, 8, 512], BF16) for i in range(2)]
        ufs = [sb(f"ufs{i}", [128, 4, 256], BF16) for i in range(2)]
        vs = [sb(f"vs{i}", [128, 4, 512], BF16) for i in range(2)]
        yr = [sb(f"yr{i}", [128, 4, 512]) for i in range(2)]
        qk = [sb(f"qk{i}", [128, 16, 64]) for i in range(NS)]
        sq = sb("sq", [128, 16, 64])
        ss16 = sb("ss16", [128, 16])
        rs16 = sb("rs16", [128, 16])
        t1 = sb("t1", [128, 16, 64])
        t2 = sb("t2", [128, 16, 64])
        qr = [sb(f"qr{i}", [128, 16, 64], BF16) for i in range(NS)]
        qkT = [sb(f"qkT{i}", [128, 8, 512], BF16) for i in range(2)]
        cs = [sb(f"cs{i}", [128, 4, 2, 64]) for i in range(2)]
        tp = pst("tp", [128, 8, 128], BF16)
        tm = pst("tm", [128, 2048])
        fm = [pst(f"fm{i}", [128, 512]) for i in range(2)]
        tq = pst("tq", [128, 8, 128], BF16)
        B = lambda n_: Buf(n_)
        B_xt = [B("xt0"), B("xt1")]; B_junk = B("junk"); B_xn = [B("xn0"), B("xn1")]
        B_ss = [B("ss0"), B("ss1")]; B_rstd = [B("r0"), B("r1")]
        B_hT = [B("hT0"), B("hT1")]; B_ufs = [B("u0"), B("u1")]; B_vs = [B("v0"), B("v1")]; B_yr = [B("y0"), B("y1")]
        B_qk = [B("qk0"), B("qk1")]; B_sq = B("sq"); B_ss16 = B("ss16"); B_rs16 = B("rs16"); B_t1 = B("t1"); B_t2 = B("t2")
        B_qr = [B("qr0"), B("qr1")]; B_qkT = [B("qkT0"), B("qkT1")]; B_cs = [B("cs0"), B("cs1")]
        B_tp = B("tp"); B_tm = [B(f"tm{i}") for i in range(4)]; B_fm = [B("fm0"), B("fm1")]; B_tq = B("tq")
        k.B_UF = Buf("UF"); k.B_V = Buf("V"); k.B_QT = Buf("QT"); k.B_KT = Buf("KT"); k.B_YRT = Buf("YRT")

        groups = [(0, 2)] + [(2 + 4 * g, 4) for g in range(16)]
        ti = 0
        for gi, (t0, ng) in enumerate(groups):
            gs = gi % 2
            r = 1 if gi == 0 else 0
            ntok = ng * 128
            tok0 = t0 * 128
            if gi > 0:
                xr0 = tok0 - C
                fw.dma(sp, st[f"ldB{gs}"], lambda xr0=xr0, gs=gs: nc.sync.dma_start(
                    out=cs[gs][:, :, 0, :], in_=I["rope_cos"][xr0:xr0 + 512, :].rearrange("(t p) d -> p t d", p=128)),
                    writes=[B_cs[gs]])
                fw.dma(sp, st[f"ldB{gs}"], lambda xr0=xr0, gs=gs: nc.sync.dma_start(
                    out=cs[gs][:, :, 1, :], in_=I["rope_sin"][xr0:xr0 + 512, :].rearrange("(t p) d -> p t d", p=128)),
                    writes=[B_cs[gs]])
            for t in range(ng):
                s_ = ti % NS
                ti += 1
                gt = t0 + t
                fw.dma(sp, st[f"ldA{s_}"], lambda s_=s_, gt=gt: nc.sync.dma_start(
                    out=xt[s_][:], in_=Sc["XR"][gt * 128:(gt + 1) * 128, :]), reads=[k.B_XR[gt]], writes=[B_xt[s_]])
                fw.op(act, lambda s_=s_: nc.scalar.activation(out=junk[:], in_=xt[s_][:], func=AF.Square, accum_out=ss[s_][:]),
                      reads=[B_xt[s_]], writes=[B_junk, B_ss[s_]])
                fw.op(act, lambda s_=s_: nc.scalar.activation(out=rstd[s_][:], in_=ss[s_][:], func=AF.Sqrt, scale=1.0 / D, bias=EPS),
                      reads=[B_ss[s_]], writes=[B_rstd[s_]])
                fw.op(dve, lambda s_=s_: nc.vector.reciprocal(out=rstd[s_][:], in_=rstd[s_][:]), writes=[B_rstd[s_]])
                fw.op(dve, lambda s_=s_: nc.vector.tensor_scalar(out=xn[s_][:], in0=xt[s_][:], scalar1=rstd[s_][:, 0:1],
                                                                  scalar2=None, op0=ALU.mult),
                      reads=[B_xt[s_], B_rstd[s_]], writes=[B_xn[s_]])
                for kk in range(8):
                    fw.op(pe, lambda s_=s_, kk=kk: nc.tensor.transpose(tp[:, kk, :], xn[s_][:, kk * 128:(kk + 1) * 128], ident_b_ap(k)),
                          reads=[B_xn[s_], k.B_const], writes=[B_tp])
                for kk in range(8):
                    fw.op(act, lambda kk=kk, t=t, gs=gs, r=r: nc.scalar.activation(
                        out=hT[gs][:, kk, t * 128:(t + 1) * 128], in_=tp[:, kk, :], func=AF.Identity,
                        scale=A1[:, r, kk:kk + 1], bias=cols[:, r, 0, kk:kk + 1]),
                        reads=[B_tp, B_A1, B_cols], writes=[B_hT[gs]])
                for nb in range(4):
                    c0, c1 = nb * 512, min(1792, (nb + 1) * 512)
                    for kk in range(8):
                        fw.op(pe, lambda kk=kk, t=t, gs=gs, c0=c0, c1=c1: nc.tensor.matmul(
                            tm[:, c0:c1], lhsT=hT[gs][:, kk, t * 128:(t + 1) * 128], rhs=winb[:, kk, c0:c1],
                            start=(kk == 0), stop=(kk == 7)), reads=[B_hT[gs], B_win], writes=[B_tm[nb]])
                fw.op(act, lambda t=t, gs=gs: nc.scalar.copy(out=ufs[gs][:, t, :], in_=tm[:, 0:256]),
                      reads=[B_tm[0]], writes=[B_ufs[gs]])
                fw.op(act, lambda s_=s_: nc.scalar.copy(out=qk[s_][:, 0:4, :], in_=tm[:, 256:512].rearrange("p (g d) -> p g d", d=64)),
                      reads=[B_tm[0]], writes=[B_qk[s_]])
                fw.op(dve, lambda s_=s_: nc.vector.tensor_copy(out=qk[s_][:, 4:12, :], in_=tm[:, 512:1024].rearrange("p (g d) -> p g d", d=64)),
                      reads=[B_tm[1]], writes=[B_qk[s_]])
                fw.op(act, lambda s_=s_: nc.scalar.copy(out=qk[s_][:, 12:16, :], in_=tm[:, 1024:1280].rearrange("p (g d) -> p g d", d=64)),
                      reads=[B_tm[2]], writes=[B_qk[s_]])
                fw.op(act, lambda t=t, gs=gs: nc.scalar.copy(out=vs[gs][:, t, 0:256], in_=tm[:, 1280:1536]),
                      reads=[B_tm[2]], writes=[B_vs[gs]])
                fw.op(dve, lambda t=t, gs=gs: nc.vector.tensor_copy(out=vs[gs][:, t, 256:512], in_=tm[:, 1536:1792]),
                      reads=[B_tm[3]], writes=[B_vs[gs]])
                fw.op(pool, lambda s_=s_: nc.gpsimd.tensor_tensor(out=sq[:], in0=qk[s_][:], in1=qk[s_][:], op=ALU.mult),
                      reads=[B_qk[s_]], writes=[B_sq])
                fw.op(dve, lambda: nc.vector.tensor_reduce(out=ss16[:], in_=sq[:], axis=AX.X, op=ALU.add),
                      reads=[B_sq], writes=[B_ss16])
                fw.op(act, lambda: nc.scalar.activation(out=rs16[:], in_=ss16[:], func=AF.Sqrt, scale=1.0 / 64, bias=EPS),
                      reads=[B_ss16], writes=[B_rs16])
                fw.op(dve, lambda: nc.vector.reciprocal(out=rs16[:], in_=rs16[:]), writes=[B_rs16])
                fw.op(dve, lambda s_=s_: nc.vector.tensor_tensor(out=t1[:], in0=qk[s_][:], in1=bc(rs16[:, :].unsqueeze(2), [128, 16, 64]),
                                                                  op=ALU.mult), reads=[B_qk[s_], B_rs16], writes=[B_t1])
                gq_b = lambda j: bc(gqk[:, j:j + 1, :], [128, 8, 64])
                if gi == 0:
                    for j in range(2):
                        fw.op(dve, lambda j=j, s_=s_: nc.vector.tensor_tensor(out=qr[s_][:, j * 8:(j + 1) * 8, :], in0=t1[:, j * 8:(j + 1) * 8, :],
                                                                             in1=gq_b(j), op=ALU.mult),
                              reads=[B_t1, B_gqk], writes=[B_qr[s_]])
                else:
                    for j in range(2):
                        fw.op(pool, lambda j=j: nc.gpsimd.tensor_tensor(out=t1[:, j * 8:(j + 1) * 8, :], in0=t1[:, j * 8:(j + 1) * 8, :],
                                                                       in1=gq_b(j), op=ALU.mult),
                              reads=[B_gqk], writes=[B_t1])
                    cosb = bc(cs[gs][:, t, 0:1, :], [128, 16, 64])
                    t1v = t1[:].rearrange("p g (a h f) -> p g a h f", a=2, h=2)
                    t2v = t2[:].rearrange("p g (a h f) -> p g a h f", a=2, h=2)
                    sinv = cs[gs][:, t, 1, :].rearrange("p (a h f) -> p a h f", a=2, h=2)
                    for g4 in range(2):
                        pass
                    for h in range(2):
                        for a in range(2):
                            fw.op(pool, lambda h=h, a=a: nc.gpsimd.tensor_tensor(
                                out=t2v[:, :, a, h, :], in0=t1v[:, :, a, 1 - h, :],
                                in1=bc(sinv[:, a, h, :].unsqueeze(1), [128, 16, 16]), op=ALU.mult),
                                reads=[B_t1, B_cs[gs]], writes=[B_t2])
                    fw.op(dve, lambda: nc.vector.tensor_tensor(out=t1[:], in0=t1[:], in1=cosb, op=ALU.mult),
                          reads=[B_cs[gs], B_t2], writes=[B_t1])
                    fw.op(dve, lambda s_=s_: nc.vector.tensor_tensor(out=qr[s_][:], in0=t1[:], in1=t2[:], op=ALU.add),
                          reads=[B_t1, B_t2], writes=[B_qr[s_]])
                for j in range(8):
                    fw.op(pe, lambda j=j, s_=s_: nc.tensor.transpose(tq[:, j, :], qr[s_][:, 2 * j:2 * j + 2, :].rearrange("p g d -> p (g d)"), ident_b_ap(k)),
                          reads=[B_qr[s_], k.B_const], writes=[B_tq])
                fw.op(act, lambda t=t, gs=gs: nc.scalar.copy(out=qkT[gs][:, :, t * 128:(t + 1) * 128], in_=tq[:]),
                      reads=[B_tq], writes=[B_qkT[gs]])
            for cc in range(4):
                fs = cc % 2
                for kk in range(8):
                    fw.op(pe, lambda kk=kk, cc=cc, fs=fs, gs=gs, ntok=ntok: nc.tensor.matmul(
                        fm[fs][:, 0:ntok], lhsT=winb[:, kk, 1792 + cc * 128:1792 + (cc + 1) * 128], rhs=hT[gs][:, kk, 0:ntok],
                        start=(kk == 0), stop=(kk == 7)), reads=[B_hT[gs], B_win], writes=[B_fm[fs]])
                fw.op(dve, lambda cc=cc, fs=fs, gs=gs, ntok=ntok: nc.vector.tensor_copy(out=yr[gs][:, cc, 0:ntok], in_=fm[fs][:, 0:ntok]),
                      reads=[B_fm[fs]], writes=[B_yr[gs]])
            fw.dma(pool, st[f"stA{gs}"], lambda gs=gs, tok0=tok0, ntok=ntok, ng=ng: nc.gpsimd.dma_start(
                out=Sc["UF"][tok0:tok0 + ntok, :].rearrange("(t p) d -> p t d", p=128), in_=ufs[gs][:, 0:ng, :]),
                reads=[B_ufs[gs]], writes=[k.B_UF])
            fw.dma(pool, st[f"stB{gs}"], lambda gs=gs, tok0=tok0, ntok=ntok, ng=ng: nc.gpsimd.dma_start(
                out=Sc["V"][tok0:tok0 + ntok, :].rearrange("(t p) d -> p t d", p=128), in_=vs[gs][:, 0:ng, :]),
                reads=[B_vs[gs]], writes=[k.B_V])
            fw.dma(pool, st[f"stC{gs}"], lambda gs=gs, tok0=tok0, ntok=ntok: nc.gpsimd.dma_start(
                out=Sc["YRT"][:, tok0:tok0 + ntok].rearrange("(c p) t -> p c t", p=128), in_=yr[gs][:, :, 0:ntok]),
                reads=[B_yr[gs]], writes=[k.B_YRT])
            fw.dma(pool, st[f"stD{gs}"], lambda gs=gs, tok0=tok0, ntok=ntok: nc.gpsimd.dma_start(
                out=Sc["QT"][:, :, tok0:tok0 + ntok].rearrange("h p t -> p h t"), in_=qkT[gs][:, 0:4, 0:ntok]),
                reads=[B_qkT[gs]], writes=[k.B_QT])
            fw.dma(pool, st[f"stD{gs}"], lambda gs=gs, tok0=tok0, ntok=ntok: nc.gpsimd.dma_start(
                out=Sc["KT"][:, :, tok0:tok0 + ntok].rearrange("h p t -> p h t"), in_=qkT[gs][:, 4:8, 0:ntok]),
                reads=[B_qkT[gs]], writes=[k.B_KT])
        fw.barrier()


def ident_b_ap(k):
    return k.ident_b[:]


def _tables():
    ident = np.eye(128, dtype=np.float32)
    t = np.arange(S)
    row = (t // 64).astype(np.float32)
    col = (t % 64).astype(np.float32)
    freqs = (10000.0 ** (-np.arange(16, dtype=np.float32) / 16)).astype(np.float32)
    ar = (row[:, None] * freqs[None, :]).astype(np.float32)
    ac = (col[:, None] * freqs[None, :]).astype(np.float32)
    cos = np.concatenate([np.cos(ar), np.cos(ar), np.cos(ac), np.cos(ac)], axis=1).astype(np.float32)
    sin = np.concatenate([-np.sin(ar), np.sin(ar), -np.sin(ac), np.sin(ac)], axis=1).astype(np.float32)
    tabs = {"ident": ident, "rope_cos": cos, "rope_sin": sin}
    j64 = np.arange(64, dtype=np.float64)
    a64 = 2 * np.pi * np.outer(j64, j64) / 64.0
    tabs["f_cs64"] = np.concatenate([np.cos(a64), np.sin(a64)], axis=1).astype(np.float32)
    t2 = np.arange(128, dtype=np.float64)[:, None, None]
    k1 = np.arange(64, dtype=np.float64)[None, :, None]
    k2 = np.arange(128, dtype=np.float64)[None, None, :]
    ang = 2 * np.pi * ((t2 * (k1 + 64 * k2)) % 8192) / 8192.0
    tabs["f_c128"] = np.cos(ang).astype(np.float32)
    tabs["f_s128"] = np.sin(ang).astype(np.float32)
    tabs["f_n128"] = (-np.sin(ang)).astype(np.float32)
    j256 = np.arange(256, dtype=np.float64)
    a256 = 2 * np.pi * (np.outer(j256, j256) % 256) / 256.0
    tabs["f_c256"] = np.cos(a256).astype(np.float32)
    tabs["f_s256"] = np.sin(a256).astype(np.float32)
    bdc = np.zeros((128, 128), np.float32); bds = np.zeros((128, 128), np.float32)
    for hb in range(2):
        bdc[hb * 64:(hb + 1) * 64, hb * 64:(hb + 1) * 64] = np.cos(a64)
        bds[hb * 64:(hb + 1) * 64, hb * 64:(hb + 1) * 64] = np.sin(a64)
    tabs["f_bdc"] = bdc; tabs["f_bds"] = bds
    tabs["m_thr"] = np.tile((np.arange(40, dtype=np.float32) * BLK)[None, :], (128, 1)).astype(np.float32)
    tabs["m_bst"] = np.tile((np.arange(NB, dtype=np.float32) * BLK)[None, :], (128, 1)).astype(np.float32)
    tabs["m_pidx"] = np.arange(128, dtype=np.float32).reshape(128, 1)
    tabs["m_utri"] = np.triu(np.ones((128, 128), np.float32), 1)
    return tabs


_PARAMS = ["w_mod", "b_mod", "norm1_g", "norm2_g", "w_in", "q_norm_g", "k_norm_g", "lambda_q1", "lambda_k1",
           "lambda_q2", "lambda_k2", "subln_g", "conv_w", "conv_b", "gate_a_w", "gate_a_b", "gate_x_w", "gate_x_b",
           "lru_lambda", "w_out", "w_group", "b_group", "w_router", "b_router", "w1", "w3", "w2"]


def make_in_maps(inputs, ncores=8):
    tabs = _tables()
    shared = {n_: np.ascontiguousarray(np.asarray(inputs[n_], dtype=np.float32)) for n_ in _PARAMS}
    maps = []
    for b in range(ncores):
        m = dict(shared)
        m.update(tabs)
        m["xin"] = np.ascontiguousarray(np.concatenate([inputs["ctx"][b], inputs["x"][b]], axis=0).astype(np.float32))
        m["cvec"] = np.ascontiguousarray(np.stack([inputs["c"][b], inputs["c_ctx"]], axis=0).astype(np.float32))
        maps.append(m)
    return maps


def kernel(**inputs):
    nc = build()
    maps = make_in_maps(inputs, 8)
    res = run_bass_kernel_spmd(nc, maps, core_ids=list(range(8)))
    return np.stack([np.asarray(r["out"]) for r in res.results], axis=0).astype(np.float32)


def phase_lru(k, l):
    nc, fw, I, Sc = k.nc, k.fw, k.I, k.Sc
    pe, act, dve, pool, sp = fw.pe, fw.act, fw.dve, fw.pool, fw.sp
    st = k.st
    last = (l == DEPTH - 1)
    fw.barrier()
    with contextlib.ExitStack() as es:
        sb = lambda n_, shp, dt=F32: es.enter_context(nc.sbuf_tensor(f"{n_}_L{l}", list(shp), dt))
        pst = lambda n_, shp, dt=F32: es.enter_context(nc.psum_tensor(f"{n_}_L{l}", list(shp), dt))
        NCH = 2048
        bdf = sb("bdf", [128, 8, 128])
        bdb = sb("bdb", [128, 8, 128], BF16)
        cw = sb("cw", [128, 2, 4])
        cb = sb("cb", [128, 2])
        gb = sb("gb", [128, 2, 2, 2])
        lam = sb("lam", [128, 2, 2])
        e_ = sb("lru_e", [128, 4]); ln_ = sb("lru_ln", [128, 4]); se_ = sb("lru_se", [128, 4]); mk_ = sb("lru_mk", [128, 4])
        m8 = sb("m8", [128, 4])
        B_w = Buf("lruw")
        fw.op(dve, lambda: nc.vector.memset(bdf[:], 0.0), writes=[B_w])
        for g_, nm in enumerate(("gate_a_w", "gate_x_w")):
            for d in range(2):
                for cc in range(2):
                    idx = g_ * 4 + d * 2 + cc
                    for hb in range(2):
                        fw.dma(sp, st["misc"], lambda idx=idx, nm=nm, d=d, cc=cc, hb=hb: nc.sync.dma_start(
                            out=bdf[hb * 64:(hb + 1) * 64, idx, hb * 64:(hb + 1) * 64], in_=I[nm][l, d, 2 * cc + hb, :, :]),
                            writes=[B_w])
        fw.op(dve, lambda: nc.vector.tensor_copy(out=bdb[:], in_=bdf[:]), writes=[B_w])
        col = lambda ap1: ap1.rearrange("(c p) -> p c", p=128)
        for kk in range(4):
            fw.dma(sp, st["misc"], lambda kk=kk: nc.sync.dma_start(out=cw[:, :, kk], in_=col(I["conv_w"][l, kk, :]),
                                                                  allow_slow_non_contiguous=True), writes=[B_w])
        fw.dma(sp, st["misc"], lambda: nc.sync.dma_start(out=cb[:, :], in_=col(I["conv_b"][l, :]), allow_slow_non_contiguous=True),
               writes=[B_w])
        for g_, nm in enumerate(("gate_a_b", "gate_x_b")):
            for d in range(2):
                fw.dma(sp, st["misc"], lambda g_=g_, nm=nm, d=d: nc.sync.dma_start(
                    out=gb[:, g_, d, :], in_=col(I[nm][l, d, :]), allow_slow_non_contiguous=True), writes=[B_w])
        for d in range(2):
            fw.dma(sp, st["misc"], lambda d=d: nc.sync.dma_start(out=lam[:, d, :], in_=col(I["lru_lambda"][l, d, :]),
                                                                allow_slow_non_contiguous=True), writes=[B_w])
        lamf = lam[:].rearrange("p d c -> p (d c)")
        fw.op(act, lambda: nc.scalar.activation(out=e_[:], in_=lamf, func=AF.Exp, scale=-1.0), writes=[B_w])
        fw.op(act, lambda: nc.scalar.activation(out=ln_[:], in_=e_[:], func=AF.Ln, bias=1.0, scale=1.0), writes=[B_w])
        fw.op(dve, lambda: nc.vector.tensor_scalar(out=se_[:], in0=e_[:], scalar1=1.0 / 3, scalar2=-0.5, op0=ALU.mult, op1=ALU.add), writes=[B_w])
        fw.op(dve, lambda: nc.vector.tensor_tensor(out=se_[:], in0=se_[:], in1=e_[:], op=ALU.mult), writes=[B_w])
        fw.op(dve, lambda: nc.vector.tensor_scalar(out=se_[:], in0=se_[:], scalar1=1.0, scalar2=None, op0=ALU.add), writes=[B_w])
        fw.op(dve, lambda: nc.vector.tensor_tensor(out=se_[:], in0=se_[:], in1=e_[:], op=ALU.mult), writes=[B_w])
        fw.op(dve, lambda: nc.vector.tensor_scalar(out=mk_[:], in0=e_[:], scalar1=0.02, scalar2=None, op0=ALU.is_lt), writes=[B_w])
        fw.op(dve, lambda: nc.vector.tensor_tensor(out=se_[:], in0=se_[:], in1=ln_[:], op=ALU.subtract), writes=[B_w])
        fw.op(dve, lambda: nc.vector.tensor_tensor(out=se_[:], in0=se_[:], in1=mk_[:], op=ALU.mult), writes=[B_w])
        fw.op(dve, lambda: nc.vector.tensor_tensor(out=se_[:], in0=se_[:], in1=ln_[:], op=ALU.add), writes=[B_w])
        fw.op(dve, lambda: nc.vector.tensor_scalar(out=m8[:], in0=se_[:], scalar1=-8.0, scalar2=None, op0=ALU.mult), writes=[B_w])

        raw = sb("raw", [128, NCH + 3]); xr = sb("xr", [128, NCH]); xrb = sb("xrb", [128, NCH], BF16)
        gr = sb("gr", [128, NCH]); gi = sb("gi", [128, NCH]); tmp = sb("ltmp", [128, NCH]); H = sb("H", [128, NCH])
        hf = sb("hf", [128, NCH]); yb = sb("yb", [128, NCH]); rec = sb("rec", [128, NCH], BF16)
        carry = sb("carry", [128, 1])
        psA = [pst(f"psA{i}", [128, 512]) for i in range(2)]
        psX = [pst(f"psX{i}", [128, 512]) for i in range(2)]
        Bn = lambda n_: Buf(n_)
        B_raw, B_xr, B_xrb, B_gr, B_gi, B_tmp, B_H, B_hf, B_yb, B_rec, B_carry = [Bn(n_) for n_ in
            ("raw", "xr", "xrb", "gr", "gi", "tmp", "H", "hf", "yb", "rec", "carry")]
        B_psA = [Bn("psA0"), Bn("psA1")]; B_psX = [Bn("psX0"), Bn("psX1")]
        HFD = Sc["HFD"]
        k.B_HFD = Buf("HFD")
        chunks = [(0, C, 0, C)] + [(C + j * NCH, C + (j + 1) * NCH, C, T) for j in range(4)]
        for cc in range(2):
            for d in range(2):
                order = chunks if d == 0 else [chunks[0]] + chunks[:0:-1]
                for ci, (lo, hi, slo, shi) in enumerate(order):
                    n = hi - lo
                    a0 = max(lo - 1, slo); a1 = min(hi + 2, shi)
                    fw.op(pool, lambda: nc.gpsimd.memset(raw[:], 0.0), writes=[B_raw])
                    fw.dma(sp, st["ldA0"], lambda a0=a0, a1=a1, lo=lo, cc=cc: nc.sync.dma_start(
                        out=raw[:, a0 - (lo - 1):a1 - (lo - 1)], in_=Sc["YRT"][256 + cc * 128:256 + (cc + 1) * 128, a0:a1]),
                        reads=[k.B_YRT], writes=[B_raw])
                    fw.op(dve, lambda n=n, cc=cc: nc.vector.tensor_scalar(out=xr[:, 0:n], in0=raw[:, 1:1 + n], scalar1=cw[:, cc, 1:2],
                                                                        scalar2=cb[:, cc:cc + 1], op0=ALU.mult, op1=ALU.add),
                          reads=[B_raw, B_w], writes=[B_xr])
                    for kk in (0, 2, 3):
                        fw.op(dve, lambda n=n, cc=cc, kk=kk: nc.vector.scalar_tensor_tensor(
                            out=xr[:, 0:n], in0=raw[:, kk:kk + n], scalar=cw[:, cc, kk:kk + 1], in1=xr[:, 0:n], op0=ALU.mult, op1=ALU.add),
                            reads=[B_raw, B_w], writes=[B_xr])
                    fw.op(act, lambda n=n: nc.scalar.copy(out=xrb[:, 0:n], in_=xr[:, 0:n]), reads=[B_xr], writes=[B_xrb])
                    nblk = (n + 511) // 512
                    for bi in range(nblk):
                        b0, b1 = bi * 512, min(n, (bi + 1) * 512)
                        s_ = bi % 2
                        fw.op(pe, lambda s_=s_, b0=b0, b1=b1, d=d, cc=cc: nc.tensor.matmul(
                            psA[s_][:, 0:b1 - b0], lhsT=bdb[:, 0 * 4 + d * 2 + cc, :], rhs=xrb[:, b0:b1], start=True, stop=True),
                            reads=[B_xrb, B_w], writes=[B_psA[s_]])
                        fw.op(pe, lambda s_=s_, b0=b0, b1=b1, d=d, cc=cc: nc.tensor.matmul(
                            psX[s_][:, 0:b1 - b0], lhsT=bdb[:, 1 * 4 + d * 2 + cc, :], rhs=xrb[:, b0:b1], start=True, stop=True),
                            reads=[B_xrb, B_w], writes=[B_psX[s_]])
                        fw.op(act, lambda s_=s_, b0=b0, b1=b1, d=d, cc=cc: nc.scalar.activation(
                            out=gr[:, b0:b1], in_=psA[s_][:, 0:b1 - b0], func=AF.Sigmoid, bias=gb[:, 0, d, cc:cc + 1], scale=1.0),
                            reads=[B_psA[s_], B_w], writes=[B_gr])
                        fw.op(act, lambda s_=s_, b0=b0, b1=b1, d=d, cc=cc: nc.scalar.activation(
                            out=gi[:, b0:b1], in_=psX[s_][:, 0:b1 - b0], func=AF.Sigmoid, bias=gb[:, 1, d, cc:cc + 1], scale=1.0),
                            reads=[B_psX[s_], B_w], writes=[B_gi])
                    fw.op(act, lambda n=n, d=d, cc=cc: nc.scalar.activation(out=gr[:, 0:n], in_=gr[:, 0:n], func=AF.Exp,
                                                                           scale=m8[:, d * 2 + cc:d * 2 + cc + 1]),
                          reads=[B_w], writes=[B_gr])
                    fw.op(pool, lambda n=n: nc.gpsimd.tensor_tensor(out=tmp[:, 0:n], in0=gr[:, 0:n], in1=gr[:, 0:n], op=ALU.mult),
                          reads=[B_gr], writes=[B_tmp])
                    fw.op(dve, lambda n=n: nc.vector.tensor_scalar(out=tmp[:, 0:n], in0=tmp[:, 0:n], scalar1=-1.0, scalar2=1.0,
                                                                   op0=ALU.mult, op1=ALU.add), writes=[B_tmp])
                    fw.op(act, lambda n=n: nc.scalar.activation(out=tmp[:, 0:n], in_=tmp[:, 0:n], func=AF.Sqrt), writes=[B_tmp])
                    fw.op(dve, lambda n=n: nc.vector.tensor_tensor(out=gi[:, 0:n], in0=gi[:, 0:n], in1=xr[:, 0:n], op=ALU.mult),
                          reads=[B_xr], writes=[B_gi])
                    fw.op(dve, lambda n=n: nc.vector.tensor_tensor(out=gi[:, 0:n], in0=gi[:, 0:n], in1=tmp[:, 0:n], op=ALU.mult),
                          reads=[B_tmp], writes=[B_gi])
                    init = 0.0 if ci == 0 else carry[:, 0:1]
                    if d == 0:
                        fw.op(dve, lambda n=n, init=init: nc.vector.tensor_tensor_scan(
                            out=H[:, 0:n], data0=gr[:, 0:n], data1=gi[:, 0:n], initial=init, op0=ALU.mult, op1=ALU.add),
                            reads=[B_gr, B_gi, B_carry], writes=[B_H])
                        fw.op(dve, lambda n=n: nc.vector.tensor_copy(out=carry[:], in_=H[:, n - 1:n]), reads=[B_H], writes=[B_carry])
                        fw.dma(sp, st["stA0"], lambda n=n, lo=lo, hi=hi, cc=cc: nc.sync.dma_start(
                            out=HFD[cc * 128:(cc + 1) * 128, lo:hi], in_=H[:, 0:n]), reads=[B_H], writes=[k.B_HFD])
                    else:
                        fw.op(dve, lambda n=n, init=init: nc.vector.tensor_tensor_scan(
                            out=H[:, 0:n][:, ::-1], data0=gr[:, 0:n][:, ::-1], data1=gi[:, 0:n][:, ::-1],
                            initial=init, op0=ALU.mult, op1=ALU.add),
                            reads=[B_gr, B_gi, B_carry], writes=[B_H])
                        fw.op(dve, lambda: nc.vector.tensor_copy(out=carry[:], in_=H[:, 0:1]), reads=[B_H], writes=[B_carry])
                        if last and lo == 0:
                            continue
                        fw.dma(sp, st["ldB0"], lambda n=n, lo=lo, hi=hi, cc=cc: nc.sync.dma_start(
                            out=hf[:, 0:n], in_=HFD[cc * 128:(cc + 1) * 128, lo:hi]), reads=[k.B_HFD], writes=[B_hf])
                        fw.dma(sp, st["ldC0"], lambda n=n, lo=lo, hi=hi, cc=cc: nc.sync.dma_start(
                            out=yb[:, 0:n], in_=Sc["YRT"][cc * 128:(cc + 1) * 128, lo:hi]), reads=[k.B_YRT], writes=[B_yb])
                        fw.op(pool, lambda n=n: nc.gpsimd.tensor_tensor(out=tmp[:, 0:n], in0=yb[:, 0:n], in1=yb[:, 0:n], op=ALU.mult),
                              reads=[B_yb], writes=[B_tmp])
                        fw.op(pool, lambda n=n: nc.gpsimd.tensor_scalar(out=tmp[:, 0:n], in0=tmp[:, 0:n], scalar1=0.044715, scalar2=1.0,
                                                                       op0=ALU.mult, op1=ALU.add), writes=[B_tmp])
                        fw.op(pool, lambda n=n: nc.gpsimd.tensor_tensor(out=tmp[:, 0:n], in0=tmp[:, 0:n], in1=yb[:, 0:n], op=ALU.mult),
                              reads=[B_yb], writes=[B_tmp])
                        fw.op(act, lambda n=n: nc.scalar.activation(out=tmp[:, 0:n], in_=tmp[:, 0:n], func=AF.Sigmoid,
                                                                    scale=2.0 * math.sqrt(2.0 / math.pi)), writes=[B_tmp])
                        fw.op(dve, lambda n=n: nc.vector.tensor_tensor(out=tmp[:, 0:n], in0=tmp[:, 0:n], in1=yb[:, 0:n], op=ALU.mult),
                              reads=[B_yb], writes=[B_tmp])
                        fw.op(dve, lambda n=n: nc.vector.tensor_tensor(out=hf[:, 0:n], in0=hf[:, 0:n], in1=H[:, 0:n], op=ALU.add),
                              reads=[B_H], writes=[B_hf])
                        fw.op(dve, lambda n=n: nc.vector.tensor_tensor(out=rec[:, 0:n], in0=tmp[:, 0:n], in1=hf[:, 0:n], op=ALU.mult),
                              reads=[B_tmp, B_hf], writes=[B_rec])
                        fw.dma(sp, st["stB0"], lambda n=n, lo=lo, hi=hi, cc=cc: nc.sync.dma_start(
                            out=Sc["MIXT"][1024 + cc * 128:1024 + (cc + 1) * 128, lo:hi], in_=rec[:, 0:n]),
                            reads=[B_rec], writes=[k.B_MIXT])
        fw.barrier()


def phase_fourier(k, l):
    nc, fw, I, Sc = k.nc, k.fw, k.I, k.Sc
    pe, act, dve, pool, sp = fw.pe, fw.act, fw.dve, fw.pool, fw.sp
    st = k.st
    last = (l == DEPTH - 1)
    fw.barrier()
    with contextlib.ExitStack() as es:
        sb = lambda n_, shp, dt=F32: es.enter_context(nc.sbuf_tensor(f"{n_}_L{l}", list(shp), dt))
        pst = lambda n_, shp, dt=F32: es.enter_context(nc.psum_tensor(f"{n_}_L{l}", list(shp), dt))
        cs64 = sb("cs64", [64, 128], BF16)
        tC = sb("tC", [128, 64, 128], BF16); tS = sb("tS", [128, 64, 128], BF16); tN = sb("tN", [128, 64, 128], BF16)
        B_tab = Buf("ftab")
        fw.dma(pool, st["w0"], lambda: nc.gpsimd.dma_start(out=cs64[:], in_=I["f_cs64"][:, :]), writes=[B_tab])
        for tt, nm in ((tC, "f_c128"), (tS, "f_s128"), (tN, "f_n128")):
            for q4 in range(4):
                fw.dma(pool, st["w0"], lambda tt=tt, nm=nm, q4=q4: nc.gpsimd.dma_start(
                    out=tt[:, q4 * 16:(q4 + 1) * 16, :], in_=I[nm][:, q4 * 16:(q4 + 1) * 16, :]), writes=[B_tab])
        zs = sb("zs", [64, 128, 128], BF16)
        As = sb("As", [128, 64, 2, 128], BF16)
        WT = [sb(f"WT{i}", [128, S], BF16) for i in range(2)]
        ps1 = [pst(f"ps1_{i}", [128, 4, 2, 64]) for i in range(2)]
        psr = [pst(f"psr{i}", [128, 4, 128]) for i in range(2)]
        psi = [pst(f"psi{i}", [128, 4, 128]) for i in range(2)]
        B_zs, B_As = Buf("zs"), Buf("As")
        B_WT = [Buf("WT0"), Buf("WT1")]
        B_ps1 = [Buf("ps1_0"), Buf("ps1_1")]; B_psr = [Buf("psr0"), Buf("psr1")]; B_psi = [Buf("psi0"), Buf("psi1")]
        sc_x = 1.0 / math.sqrt(S * 64.0)
        for cc in range(2):
            fw.dma(sp, st["ldA0"], lambda cc=cc: nc.sync.dma_start(
                out=zs[:], in_=Sc["UF"][C:T, cc * 128:(cc + 1) * 128].rearrange("(a b) c -> a b c", b=128)),
                reads=[k.B_UF], writes=[B_zs])
            for c4 in range(32):
                s_ = c4 % 2
                for j in range(4):
                    ch = c4 * 4 + j
                    fw.op(pe, lambda s_=s_, j=j, ch=ch: nc.tensor.matmul(
                        ps1[s_][:, j, :, :].rearrange("p r k -> p (r k)"), lhsT=zs[:, :, ch], rhs=cs64[:], start=True, stop=True),
                        reads=[B_zs, B_tab], writes=[B_ps1[s_]])
                dst = As[:, :, :, c4 * 4:(c4 + 1) * 4].rearrange("p k r c -> p c r k")
                if c4 % 2 == 0:
                    fw.op(act, lambda s_=s_, dst=dst: nc.scalar.copy(out=dst, in_=ps1[s_][:]), reads=[B_ps1[s_]], writes=[B_As])
                else:
                    fw.op(dve, lambda s_=s_, dst=dst: nc.vector.tensor_copy(out=dst, in_=ps1[s_][:]), reads=[B_ps1[s_]], writes=[B_As])
            for kg in range(16):
                s_ = kg % 2
                for j in range(4):
                    k1 = kg * 4 + j
                    fw.op(pe, lambda s_=s_, j=j, k1=k1: nc.tensor.matmul(psr[s_][:, j, :], lhsT=As[:, k1, 0, :], rhs=tC[:, k1, :], start=True, stop=False),
                          reads=[B_As, B_tab], writes=[B_psr[s_]])
                    fw.op(pe, lambda s_=s_, j=j, k1=k1: nc.tensor.matmul(psr[s_][:, j, :], lhsT=As[:, k1, 1, :], rhs=tN[:, k1, :], start=False, stop=True),
                          reads=[B_As, B_tab], writes=[B_psr[s_]])
                for j in range(4):
                    k1 = kg * 4 + j
                    fw.op(pe, lambda s_=s_, j=j, k1=k1: nc.tensor.matmul(psi[s_][:, j, :], lhsT=As[:, k1, 1, :], rhs=tC[:, k1, :], start=True, stop=False),
                          reads=[B_As, B_tab], writes=[B_psi[s_]])
                    fw.op(pe, lambda s_=s_, j=j, k1=k1: nc.tensor.matmul(psi[s_][:, j, :], lhsT=As[:, k1, 0, :], rhs=tS[:, k1, :], start=False, stop=True),
                          reads=[B_As, B_tab], writes=[B_psi[s_]])
                o_r = WT[0][:].rearrange("p (b a) -> p a b", a=64)[:, kg * 4:(kg + 1) * 4, :]
                o_i = WT[1][:].rearrange("p (b a) -> p a b", a=64)[:, kg * 4:(kg + 1) * 4, :]
                fw.op(act, lambda s_=s_, o_r=o_r: nc.scalar.mul(out=o_r, in_=psr[s_][:], mul=sc_x), reads=[B_psr[s_]], writes=[B_WT[0]])
                fw.op(dve, lambda s_=s_, o_i=o_i: nc.vector.tensor_scalar(out=o_i, in0=psi[s_][:], scalar1=-sc_x, scalar2=None, op0=ALU.mult),
                      reads=[B_psi[s_]], writes=[B_WT[1]])
            for ri in range(2):
                fw.dma(sp, st[f"stA{ri}"], lambda ri=ri, cc=cc: nc.sync.dma_start(
                    out=Sc["MIXT"][ri * 256 + cc * 128:ri * 256 + (cc + 1) * 128, C:T], in_=WT[ri][:]),
                    reads=[B_WT[ri]], writes=[k.B_MIXT])
        if not last:
            c256 = sb("c256", [128, 2, 256], BF16); s256 = sb("s256", [128, 2, 256], BF16)
            zc = sb("zc", [128, 2, 256], BF16)
            wc = sb("wc", [128, 2, 256], BF16)
            B_zc, B_wc = Buf("zc"), Buf("wc")
            fw.dma(pool, st["w0"], lambda: nc.gpsimd.dma_start(out=c256[:], in_=I["f_c256"].rearrange("(a p) n -> p a n", p=128)), writes=[B_tab])
            fw.dma(pool, st["w0"], lambda: nc.gpsimd.dma_start(out=s256[:], in_=I["f_s256"].rearrange("(a p) n -> p a n", p=128)), writes=[B_tab])
            fw.dma(sp, st["ldA0"], lambda: nc.sync.dma_start(out=zc[:], in_=Sc["UF"][0:C, :].rearrange("(a p) n -> p a n", p=128)),
                   reads=[k.B_UF], writes=[B_zc])
            sc_c = 1.0 / math.sqrt(C * 64.0)
            for cc in range(2):
                pr = psr[cc][:].rearrange("p a b -> p (a b)")[:, 0:256]
                pi_ = psi[cc][:].rearrange("p a b -> p (a b)")[:, 0:256]
                for a in range(2):
                    fw.op(pe, lambda a=a, cc=cc, pr=pr: nc.tensor.matmul(pr, lhsT=zc[:, a, cc * 128:(cc + 1) * 128], rhs=c256[:, a, :],
                                                                        start=(a == 0), stop=(a == 1)), reads=[B_zc, B_tab], writes=[B_psr[cc]])
                for a in range(2):
                    fw.op(pe, lambda a=a, cc=cc, pi_=pi_: nc.tensor.matmul(pi_, lhsT=zc[:, a, cc * 128:(cc + 1) * 128], rhs=s256[:, a, :],
                                                                          start=(a == 0), stop=(a == 1)), reads=[B_zc, B_tab], writes=[B_psi[cc]])
                fw.op(act, lambda pr=pr: nc.scalar.mul(out=wc[:, 0, :], in_=pr, mul=sc_c), reads=[B_psr[cc]], writes=[B_wc])
                fw.op(act, lambda pi_=pi_: nc.scalar.mul(out=wc[:, 1, :], in_=pi_, mul=-sc_c), reads=[B_psi[cc]], writes=[B_wc])
                for ri in range(2):
                    fw.dma(sp, st[f"stB{ri}"], lambda ri=ri, cc=cc: nc.sync.dma_start(
                        out=Sc["MIXT"][ri * 256 + cc * 128:ri * 256 + (cc + 1) * 128, 0:C], in_=wc[:, ri, :]),
                        reads=[B_wc], writes=[k.B_MIXT])
        fw.barrier()


def phase_attn(k, l):
    nc, fw, I, Sc = k.nc, k.fw, k.I, k.Sc
    pe, act, dve, pool, sp = fw.pe, fw.act, fw.dve, fw.pool, fw.sp
    st = k.st
    last = (l == DEPTH - 1)
    lam_init = 0.8 - 0.6 * math.exp(-0.3 * l)
    fw.barrier()
    with contextlib.ExitStack() as es:
        sb = lambda n_, shp, dt=F32: es.enter_context(nc.sbuf_tensor(f"{n_}_L{l}", list(shp), dt))
        pst = lambda n_, shp, dt=F32: es.enter_context(nc.psum_tensor(f"{n_}_L{l}", list(shp), dt))
        lv = sb("lv", [128, 4, 64]); lp = sb("lp", [128, 2, 64]); ls = sb("ls", [128, 2]); nlam = sb("nlam", [128, 1])
        gq = sb("agq", [128, 2, 64]); gm = sb("agm", [128, 2]); negM = sb("negM", [128, 1])
        sg = sb("sg", [128, 1])
        B_s = Buf("attn_setup")
        for j, nm in enumerate(("lambda_q1", "lambda_k1", "lambda_q2", "lambda_k2")):
            fw.dma(sp, st["misc"], lambda j=j, nm=nm: nc.sync.dma_start(out=lv[:, j, :], in_=I[nm][l, :].partition_broadcast(128)), writes=[B_s])
        fw.dma(sp, st["misc"], lambda: nc.sync.dma_start(out=gq[:, 0, :], in_=I["q_norm_g"][l, :].partition_broadcast(128)), writes=[B_s])
        fw.dma(sp, st["misc"], lambda: nc.sync.dma_start(out=gq[:, 1, :], in_=I["k_norm_g"][l, :].partition_broadcast(128)), writes=[B_s])
        fw.dma(sp, st["misc"], lambda: nc.sync.dma_start(out=sg[:], in_=I["subln_g"][l, :].rearrange("(p o) -> p o", o=1)), writes=[B_s])
        lvv = lv[:].rearrange("p (a b) d -> p a b d", b=2)
        fw.op(dve, lambda: nc.vector.tensor_tensor(out=lp[:], in0=lvv[:, :, 0, :], in1=lvv[:, :, 1, :], op=ALU.mult), writes=[B_s])
        fw.op(dve, lambda: nc.vector.tensor_reduce(out=ls[:], in_=lp[:], axis=AX.X, op=ALU.add), writes=[B_s])
        fw.op(act, lambda: nc.scalar.activation(out=ls[:], in_=ls[:], func=AF.Exp), writes=[B_s])
        fw.op(dve, lambda: nc.vector.tensor_tensor(out=nlam[:], in0=ls[:, 1:2], in1=ls[:, 0:1], op=ALU.subtract), writes=[B_s])
        fw.op(dve, lambda: nc.vector.tensor_scalar(out=nlam[:], in0=nlam[:], scalar1=-lam_init, scalar2=None, op0=ALU.add), writes=[B_s])
        fw.op(dve, lambda: nc.vector.tensor_reduce(out=gm[:], in_=gq[:], axis=AX.X, op=ALU.max, apply_absolute_value=True), writes=[B_s])
        fw.op(dve, lambda: nc.vector.tensor_tensor(out=negM[:], in0=gm[:, 0:1], in1=gm[:, 1:2], op=ALU.mult), writes=[B_s])
        fw.op(dve, lambda: nc.vector.tensor_scalar(out=negM[:], in0=negM[:], scalar1=-8.0, scalar2=None, op0=ALU.mult), writes=[B_s])
        fw.op(dve, lambda: nc.vector.tensor_scalar(out=sg[:], in0=sg[:], scalar1=1.0 - lam_init, scalar2=None, op0=ALU.mult), writes=[B_s])

        NKC = T // 128
        kt = [sb(f"kt{i}", [128, T], BF16) for i in range(2)]
        vh = [sb(f"vh{i}", [128, NKC, 128], BF16) for i in range(2)]
        qt = [sb(f"qt{i}", [128, 512], BF16) for i in range(2)]
        P = [[sb(f"P{m}{i}", [128, 512], BF16) for i in range(3)] for m in range(2)]
        rz = [sb(f"rz{m}", [128, 512]) for m in range(2)]
        o0 = sb("o0", [128, 512]); att = sb("att", [128, 512]); sqa = sb("sqa", [128, 512]); rs = sb("ars", [128, 512])
        ob = [sb(f"ob{i}", [128, 512], BF16) for i in range(2)]
        Sps = [[pst(f"S{m}{i}", [128, 512]) for i in range(2)] for m in range(2)]
        Ops = [pst(f"O{m}", [128, 512]) for m in range(2)]
        Zps = [pst(f"Z{m}", [128, 512]) for m in range(2)]
        B_kt = [Buf("kt0"), Buf("kt1")]; B_vh = [Buf("vh0"), Buf("vh1")]; B_qt = [Buf("qt0"), Buf("qt1")]
        B_P = [[Buf(f"P{m}{i}") for i in range(3)] for m in range(2)]
        B_rz = [Buf("rz0"), Buf("rz1")]; B_o0, B_att, B_sqa, B_rs = Buf("o0"), Buf("att"), Buf("sqa"), Buf("rs")
        B_ob = [Buf("ob0"), Buf("ob1")]
        B_S = [[Buf(f"S{m}{i}") for i in range(2)] for m in range(2)]
        B_O = [Buf("O0"), Buf("O1")]; B_Z = [Buf("Z0"), Buf("Z1")]

        def load_head(h):
            s_ = h % 2
            fw.dma(sp, st[f"ldA{s_}"], lambda: nc.sync.dma_start(out=kt[s_][:], in_=Sc["KT"][h, :, :]), reads=[k.B_KT], writes=[B_kt[s_]])
            fw.dma(sp, st[f"ldB{s_}"], lambda: nc.sync.dma_start(
                out=vh[s_][:], in_=Sc["V"][:, h * 128:(h + 1) * 128].rearrange("(c p) d -> p c d", p=128)), reads=[k.B_V], writes=[B_vh[s_]])

        qtiles = [(C + 512 * j, 512, NKC) for j in range(16)]
        if not last:
            qtiles = [(0, C, C // 128)] + qtiles
        qi = 0
        pcount = 0
        load_head(0)
        for h in range(4):
            hs = h % 2
            if h + 1 < 4:
                load_head(h + 1)
            for (q0, nq, nkc) in qtiles:
                qs = qi % 2
                qi += 1
                fw.dma(sp, st[f"ldC{qs}"], lambda qs=qs, q0=q0, nq=nq, h=h: nc.sync.dma_start(out=qt[qs][:, 0:nq], in_=Sc["QT"][h, :, q0:q0 + nq]),
                       reads=[k.B_QT], writes=[B_qt[qs]])

                def scores(c):
                    sbuf_ = c % 2
                    for m in range(2):
                        fw.op(pe, lambda m=m, c=c, sbuf_=sbuf_: nc.tensor.matmul(
                            Sps[m][sbuf_][:, 0:nq], lhsT=kt[hs][m * 64:(m + 1) * 64, c * 128:(c + 1) * 128],
                            rhs=qt[qs][m * 64:(m + 1) * 64, 0:nq], start=True, stop=True),
                            reads=[B_kt[hs], B_qt[qs]], writes=[B_S[m][sbuf_]])

                scores(0)
                for c in range(nkc):
                    if c + 1 < nkc:
                        scores(c + 1)
                    sbuf_ = c % 2
                    pb = pcount % 3
                    pcount += 1
                    for m in range(2):
                        fw.op(act, lambda m=m, sbuf_=sbuf_, pb=pb: nc.scalar.activation(
                            out=P[m][pb][:, 0:nq], in_=Sps[m][sbuf_][:, 0:nq], func=AF.Exp, bias=negM[:, 0:1], scale=0.125),
                            reads=[B_S[m][sbuf_], B_s], writes=[B_P[m][pb]])
                    for m in range(2):
                        fw.op(pe, lambda m=m, c=c, pb=pb: nc.tensor.matmul(
                            Ops[m][:, 0:nq], lhsT=vh[hs][:, c, :], rhs=P[m][pb][:, 0:nq], start=(c == 0), stop=(c == nkc - 1)),
                            reads=[B_vh[hs], B_P[m][pb]], writes=[B_O[m]])
                        fw.op(pe, lambda m=m, c=c, pb=pb: nc.tensor.matmul(
                            Zps[m][:, 0:nq], lhsT=k.ones_b[:], rhs=P[m][pb][:, 0:nq], start=(c == 0), stop=(c == nkc - 1)),
                            reads=[k.B_const, B_P[m][pb]], writes=[B_Z[m]])
                for m in range(2):
                    fw.op(dve, lambda m=m: nc.vector.reciprocal(out=rz[m][:, 0:nq], in_=Zps[m][:, 0:nq]), reads=[B_Z[m]], writes=[B_rz[m]])
                fw.op(dve, lambda: nc.vector.tensor_tensor(out=o0[:, 0:nq], in0=Ops[0][:, 0:nq], in1=rz[0][:, 0:nq], op=ALU.mult),
                      reads=[B_O[0], B_rz[0]], writes=[B_o0])
                fw.op(dve, lambda: nc.vector.tensor_tensor(out=att[:, 0:nq], in0=Ops[1][:, 0:nq], in1=rz[1][:, 0:nq], op=ALU.mult),
                      reads=[B_O[1], B_rz[1]], writes=[B_att])
                fw.op(dve, lambda: nc.vector.scalar_tensor_tensor(out=att[:, 0:nq], in0=att[:, 0:nq], scalar=nlam[:, 0:1], in1=o0[:, 0:nq],
                                                                  op0=ALU.mult, op1=ALU.add), reads=[B_o0, B_s], writes=[B_att])
                fw.op(pool, lambda: nc.gpsimd.tensor_tensor(out=sqa[:, 0:nq], in0=att[:, 0:nq], in1=att[:, 0:nq], op=ALU.mult),
                      reads=[B_att], writes=[B_sqa])
                fw.op(pe, lambda: nc.tensor.matmul(Sps[0][0][:, 0:nq], lhsT=k.ones_f[:], rhs=sqa[:, 0:nq], start=True, stop=True),
                      reads=[k.B_const, B_sqa], writes=[B_S[0][0]])
                fw.op(act, lambda: nc.scalar.activation(out=rs[:, 0:nq], in_=Sps[0][0][:, 0:nq], func=AF.Sqrt, scale=1.0 / 128, bias=EPS),
                      reads=[B_S[0][0]], writes=[B_rs])
                fw.op(dve, lambda: nc.vector.reciprocal(out=rs[:, 0:nq], in_=rs[:, 0:nq]), writes=[B_rs])
                fw.op(dve, lambda qs=qs: nc.vector.scalar_tensor_tensor(out=ob[qs][:, 0:nq], in0=att[:, 0:nq], scalar=sg[:, 0:1], in1=rs[:, 0:nq],
                                                                        op0=ALU.mult, op1=ALU.mult), reads=[B_att, B_rs, B_s], writes=[B_ob[qs]])
                fw.dma(pool, st[f"stA{qs}"], lambda qs=qs, q0=q0, nq=nq, h=h: nc.gpsimd.dma_start(
                    out=Sc["MIXT"][512 + h * 128:512 + (h + 1) * 128, q0:q0 + nq], in_=ob[qs][:, 0:nq]),
                    reads=[B_ob[qs]], writes=[k.B_MIXT])
        fw.barrier()


def phase_wout(k, l):
    nc, fw, I, Sc = k.nc, k.fw, k.I, k.Sc
    pe, act, dve, pool, sp = fw.pe, fw.act, fw.dve, fw.pool, fw.sp
    st = k.st
    last = (l == DEPTH - 1)
    fw.barrier()
    with contextlib.ExitStack() as es:
        sb = lambda n_, shp, dt=F32: es.enter_context(nc.sbuf_tensor(f"{n_}_L{l}", list(shp), dt))
        pst = lambda n_, shp, dt=F32: es.enter_context(nc.psum_tensor(f"{n_}_L{l}", list(shp), dt))
        woutb = sb("woutb", [128, 10, D], BF16)
        wof = sb("wof", [128, 2, D], BF16)
        bdcs = sb("bdcs", [128, 2, 128], BF16)
        B_w = Buf("woutw")
        wv = I["w_out"][l].rearrange("(c p) n -> p c n", p=128)
        for c_ in range(2):
            fw.dma(pool, st["w0"], lambda c_=c_: nc.gpsimd.dma_start(out=wof[:, c_, :], in_=wv[:, c_, :]), writes=[B_w])
        for c_ in range(2, 8):
            fw.dma(pool, st["w0"], lambda c_=c_: nc.gpsimd.dma_start(out=woutb[:, c_ + 2, :], in_=wv[:, c_, :]), writes=[B_w])
        fw.dma(pool, st["w0"], lambda: nc.gpsimd.dma_start(out=bdcs[:, 0, :], in_=I["f_bdc"][:, :]), writes=[B_w])
        fw.dma(pool, st["w0"], lambda: nc.gpsimd.dma_start(out=bdcs[:, 1, :], in_=I["f_bds"][:, :]), writes=[B_w])
        yps = [pst(f"yps{i}", [128, D]) for i in range(2)]
        trp = pst("trp", [128, 8, 128], BF16)
        trl = pst("trl", [128, 8, 128], BF16)
        lgp = pst("lgp", [128, 64])
        B_yps = [Buf("yps0"), Buf("yps1")]; B_trp = Buf("trp"); B_trl = Buf("trl"); B_lgp = Buf("lgp")
        for ri in range(2):
            for c_ in range(2):
                for hf_ in range(2):
                    fw.op(pe, lambda ri=ri, c_=c_, hf_=hf_: nc.tensor.matmul(
                        yps[0][:, hf_ * 512:(hf_ + 1) * 512], lhsT=bdcs[:, ri, :], rhs=wof[:, c_, hf_ * 512:(hf_ + 1) * 512], start=True, stop=True),
                        reads=[B_w], writes=[B_yps[0]])
                fw.op(act, lambda ri=ri, c_=c_: nc.scalar.copy(out=woutb[:, ri * 2 + c_, :], in_=yps[0][:]), reads=[B_yps[0]], writes=[B_w])
        rows = sb("rows5", [128, 2, 3, D])
        n2 = sb("n2row", [128, D])
        B_rows = Buf("rows5")
        fw.dma(sp, st["misc"], lambda: nc.sync.dma_start(out=n2[:], in_=I["norm2_g"][l, :].partition_broadcast(128)), writes=[B_rows])
        for r in range(2):
            for j, mj in enumerate((2, 4, 3)):
                fw.dma(sp, st["misc"], lambda r=r, j=j, mj=mj: nc.sync.dma_start(
                    out=rows[:, r, j, :], in_=Sc["MOD"][l, r, mj * D:(mj + 1) * D].partition_broadcast(128)), reads=[k.B_MOD], writes=[B_rows])
            fw.op(dve, lambda r=r: nc.vector.scalar_tensor_tensor(out=rows[:, r, 1, :], in0=rows[:, r, 1, :], scalar=1.0, in1=n2[:],
                                                                 op0=ALU.add, op1=ALU.mult), writes=[B_rows])
        wr = sb("wr", [128, 8, 36]); brow = sb("brow5", [128, 36])
        fw.dma(sp, st["misc"], lambda: nc.sync.dma_start(out=wr[:, :, 0:4], in_=I["w_group"][l].rearrange("(c p) g -> p c g", p=128),
                                                         allow_slow_non_contiguous=True), writes=[B_rows])
        fw.dma(sp, st["misc"], lambda: nc.sync.dma_start(out=wr[:, :, 4:36], in_=I["w_router"][l].rearrange("(c p) g -> p c g", p=128),
                                                         allow_slow_non_contiguous=True), writes=[B_rows])
        wrh = sb("wrh", [128, 8, 36], BF16); wrl = sb("wrl", [128, 8, 36], BF16)
        fw.op(dve, lambda: nc.vector.tensor_copy(out=wrh[:], in_=wr[:]), writes=[B_rows])
        fw.op(dve, lambda: nc.vector.tensor_tensor(out=wrl[:], in0=wr[:], in1=wrh[:], op=ALU.subtract), writes=[B_rows])
        fw.dma(sp, st["misc"], lambda: nc.sync.dma_start(out=brow[:, 0:4], in_=I["b_group"][l, :].partition_broadcast(128)), writes=[B_rows])
        fw.dma(sp, st["misc"], lambda: nc.sync.dma_start(out=brow[:, 4:36], in_=I["b_router"][l, :].partition_broadcast(128)), writes=[B_rows])

        mix = [sb(f"mix{i}", [128, 10, 512], BF16) for i in range(2)]
        xt = [sb(f"x5_{i}", [128, D]) for i in range(2)]
        xnw = [sb(f"xnw{i}", [128, D]) for i in range(2)]
        junk = sb("junk5", [128, D], BF16)
        ss = sb("ss5", [128, 1]); rstd = sb("rstd5", [128, 1])
        h2 = sb("h2", [128, D])
        hib = sb("hib", [128, D], BF16); lob = sb("lob", [128, D], BF16)
        loT = sb("loT", [128, 8, 128], BF16)
        hiT = sb("hiT", [128, 8, 128], BF16)
        hib2 = [sb(f"hib2_{i}", [128, D], BF16) for i in range(2)]
        lg = sb("lg", [128, 36]); sm = sb("sm5", [128, 16]); t4 = sb("t4", [128, 4]); ohg = sb("ohg", [128, 4])
        em = sb("em", [128, 32]); em2 = sb("em2", [128, 32]); oh1 = sb("oh1", [128, 32]); oh2 = sb("oh2", [128, 32])
        B_mix = [Buf("mix0"), Buf("mix1")]; B_xt = [Buf("x50"), Buf("x51")]; B_xnw = [Buf("xnw0"), Buf("xnw1")]
        B_junk, B_ss, B_h2, B_hl, B_loT, B_rt = Buf("junk5"), Buf("ss5"), Buf("h2"), Buf("hilo"), Buf("loT"), Buf("route")
        B_hiT = Buf("hiT")
        B_hib2 = [Buf("hib2_0"), Buf("hib2_1")]
        k.B_H2 = Buf("H2")
        groups = ([] if last else [(0, 2)]) + [(2 + 4 * g, 4) for g in range(16)]
        if "w5_a" in k.debug:
            groups = []
        ti = 0
        for gi, (t0, ng) in enumerate(groups):
            gs = gi % 2
            r = 1 if t0 == 0 else 0
            tok0, ntok = t0 * 128, ng * 128
            fw.dma(sp, st[f"ldA{gs}"], lambda gs=gs, tok0=tok0, ntok=ntok: nc.sync.dma_start(
                out=mix[gs][:, :, 0:ntok], in_=Sc["MIXT"][:, tok0:tok0 + ntok].rearrange("(c p) t -> p c t", p=128)),
                reads=[k.B_MIXT], writes=[B_mix[gs]])
            for t in range(ng):
                s_ = ti % 2
                ti += 1
                gt = t0 + t
                fw.dma(sp, st[f"ldB{s_}"], lambda s_=s_, gt=gt: nc.sync.dma_start(out=xt[s_][:], in_=Sc["XR"][gt * 128:(gt + 1) * 128, :]),
                       reads=[k.B_XR[gt]], writes=[B_xt[s_]])
                for hf_ in range(2):
                    for c_ in range(10):
                        fw.op(pe, lambda s_=s_, hf_=hf_, c_=c_, t=t, gs=gs: nc.tensor.matmul(
                            yps[s_][:, hf_ * 512:(hf_ + 1) * 512], lhsT=mix[gs][:, c_, t * 128:(t + 1) * 128],
                            rhs=woutb[:, c_, hf_ * 512:(hf_ + 1) * 512], start=(c_ == 0), stop=(c_ == 9)),
                            reads=[B_mix[gs], B_w], writes=[B_yps[s_]])
                fw.op(dve, lambda s_=s_, r=r: nc.vector.tensor_tensor(out=xnw[s_][:], in0=yps[s_][:], in1=rows[:, r, 0, :], op=ALU.mult),
                      reads=[B_yps[s_], B_rows], writes=[B_xnw[s_]])
                fw.op(pool, lambda s_=s_: nc.gpsimd.tensor_tensor(out=xnw[s_][:], in0=xnw[s_][:], in1=xt[s_][:], op=ALU.add),
                      reads=[B_xt[s_]], writes=[B_xnw[s_]])
                fw.dma(pool, st[f"stA{s_}"], lambda s_=s_, gt=gt: nc.gpsimd.dma_start(out=Sc["XR"][gt * 128:(gt + 1) * 128, :], in_=xnw[s_][:]),
                       reads=[B_xnw[s_]], writes=[k.B_XR[gt]])
                if "w5_b" in k.debug:
                    continue
                fw.op(act, lambda s_=s_: nc.scalar.activation(out=junk[:], in_=xnw[s_][:], func=AF.Square, accum_out=ss[:]),
                      reads=[B_xnw[s_]], writes=[B_junk, B_ss])
                fw.op(act, lambda: nc.scalar.activation(out=rstd[:], in_=ss[:], func=AF.Sqrt, scale=1.0 / D, bias=EPS), writes=[B_ss])
                fw.op(dve, lambda: nc.vector.reciprocal(out=rstd[:], in_=rstd[:]), writes=[B_ss])
                fw.op(dve, lambda s_=s_, r=r: nc.vector.scalar_tensor_tensor(out=h2[:], in0=xnw[s_][:], scalar=rstd[:, 0:1], in1=rows[:, r, 1, :],
                                                                           op0=ALU.mult, op1=ALU.mult), reads=[B_xnw[s_], B_ss, B_rows], writes=[B_h2])
                fw.op(pool, lambda r=r: nc.gpsimd.tensor_tensor(out=h2[:], in0=h2[:], in1=rows[:, r, 2, :], op=ALU.add), reads=[B_rows], writes=[B_h2])
                fw.op(act, lambda s_=s_: nc.scalar.copy(out=hib2[s_][:], in_=h2[:]), reads=[B_h2], writes=[B_hib2[s_]])
                fw.op(dve, lambda s_=s_: nc.vector.tensor_tensor(out=lob[:], in0=h2[:], in1=hib2[s_][:], op=ALU.subtract),
                      reads=[B_h2, B_hib2[s_]], writes=[B_hl])
                fw.dma(pool, st[f"stB{s_}"], lambda s_=s_, gt=gt: nc.gpsimd.dma_start(out=Sc["H2"][gt * 128:(gt + 1) * 128, :], in_=hib2[s_][:]),
                       reads=[B_hib2[s_]], writes=[k.B_H2])
                for kk in range(8):
                    fw.op(pe, lambda kk=kk, s_=s_: nc.tensor.transpose(trp[:, kk, :], hib2[s_][:, kk * 128:(kk + 1) * 128], k.ident_b[:]),
                          reads=[B_hib2[s_], k.B_const], writes=[B_trp])
                for kk in range(8):
                    fw.op(pe, lambda kk=kk: nc.tensor.transpose(trl[:, kk, :], lob[:, kk * 128:(kk + 1) * 128], k.ident_b[:]),
                          reads=[B_hl, k.B_const], writes=[B_trl])
                fw.op(act, lambda: nc.scalar.copy(out=hiT[:], in_=trp[:]), reads=[B_trp], writes=[B_hiT])
                fw.op(dve, lambda: nc.vector.tensor_copy(out=loT[:], in_=trl[:]), reads=[B_trl], writes=[B_loT])
                nmm = 0
                for (lh, wv_) in (("hi", wrh), ("lo", wrh), ("hi", wrl)):
                    for kk in range(8):
                        lhs = hiT[:, kk, :] if lh == "hi" else loT[:, kk, :]
                        fw.op(pe, lambda lhs=lhs, wv_=wv_, kk=kk, nmm=nmm: nc.tensor.matmul(
                            lgp[:, 0:36], lhsT=lhs, rhs=wv_[:, kk, :], start=(nmm == 0), stop=(nmm == 23)),
                            reads=[B_hiT, B_loT, B_rows], writes=[B_lgp])
                        nmm += 1
                if "w5_c" in k.debug:
                    continue
                R_ = [B_rt]
                V = nc.vector
                fw.op(dve, lambda: V.tensor_tensor(out=lg[:], in0=lgp[:, 0:36], in1=brow[:], op=ALU.add), reads=[B_lgp, B_rows], writes=R_)
                fw.op(dve, lambda: V.tensor_reduce(out=sm[:, 0:1], in_=lg[:, 0:4], axis=AX.X, op=ALU.max), writes=R_)
                fw.op(dve, lambda: V.tensor_scalar(out=ohg[:], in0=lg[:, 0:4], scalar1=sm[:, 0:1], scalar2=None, op0=ALU.is_ge), writes=R_)
                fw.op(dve, lambda: V.tensor_scalar(out=t4[:], in0=lg[:, 0:4], scalar1=sm[:, 0:1], scalar2=None, op0=ALU.subtract), writes=R_)
                fw.op(act, lambda: nc.scalar.activation(out=t4[:], in_=t4[:], func=AF.Exp), writes=R_)
                fw.op(dve, lambda: V.tensor_reduce(out=sm[:, 1:2], in_=t4[:], axis=AX.X, op=ALU.add), writes=R_)
                fw.op(dve, lambda: V.reciprocal(out=sm[:, 2:3], in_=sm[:, 1:2]), writes=R_)
                fw.op(dve, lambda: V.tensor_scalar(out=t4[:], in0=ohg[:], scalar1=-1.0, scalar2=1e30, op0=ALU.add, op1=ALU.mult), writes=R_)
                fw.op(dve, lambda: V.tensor_tensor(out=em[:].rearrange("p (g e) -> p g e", e=8), in0=lg[:, 4:36].rearrange("p (g e) -> p g e", e=8),
                                                   in1=bc(t4[:, :].unsqueeze(2), [128, 4, 8]), op=ALU.add), writes=R_)
                fw.op(dve, lambda: V.tensor_reduce(out=sm[:, 3:4], in_=em[:], axis=AX.X, op=ALU.max), writes=R_)
                fw.op(dve, lambda: V.tensor_scalar(out=oh1[:], in0=em[:], scalar1=sm[:, 3:4], scalar2=None, op0=ALU.is_ge), writes=R_)
                fw.op(dve, lambda: V.scalar_tensor_tensor(out=em2[:], in0=oh1[:], scalar=-1e30, in1=em[:], op0=ALU.mult, op1=ALU.add), writes=R_)
                fw.op(dve, lambda: V.tensor_reduce(out=sm[:, 4:5], in_=em2[:], axis=AX.X, op=ALU.max), writes=R_)
                fw.op(dve, lambda: V.tensor_scalar(out=oh2[:], in0=em2[:], scalar1=sm[:, 4:5], scalar2=None, op0=ALU.is_ge), writes=R_)
                fw.op(dve, lambda: V.tensor_tensor(out=sm[:, 5:6], in0=sm[:, 4:5], in1=sm[:, 3:4], op=ALU.subtract), writes=R_)
                fw.op(act, lambda: nc.scalar.activation(out=sm[:, 5:6], in_=sm[:, 5:6], func=AF.Exp), writes=R_)
                fw.op(dve, lambda: V.tensor_scalar(out=sm[:, 6:7], in0=sm[:, 5:6], scalar1=1.0, scalar2=None, op0=ALU.add), writes=R_)
                fw.op(dve, lambda: V.reciprocal(out=sm[:, 6:7], in_=sm[:, 6:7]), writes=R_)
                fw.op(dve, lambda: V.tensor_tensor(out=sm[:, 7:8], in0=sm[:, 6:7], in1=sm[:, 2:3], op=ALU.mult), writes=R_)
                fw.op(dve, lambda: V.tensor_tensor(out=sm[:, 8:9], in0=sm[:, 2:3], in1=sm[:, 7:8], op=ALU.subtract), writes=R_)
                fw.op(dve, lambda gt=gt: V.tensor_copy(out=k.OH1[:, gt, :], in_=oh1[:]), reads=R_, writes=[k.B_G])
                fw.op(dve, lambda gt=gt: V.tensor_copy(out=k.OH2[:, gt, :], in_=oh2[:]), reads=R_, writes=[k.B_G])
                fw.op(dve, lambda gt=gt: V.tensor_tensor(out=k.OH12[:, gt, :], in0=oh1[:], in1=oh2[:], op=ALU.add), reads=R_, writes=[k.B_G])
                fw.op(dve, lambda gt=gt: V.tensor_copy(out=k.W12[:, gt, :], in_=sm[:, 7:9]), reads=R_, writes=[k.B_G])
        fw.barrier()


def phase_moe(k, l):
    nc, fw, I, Sc = k.nc, k.fw, k.I, k.Sc
    pe, act, dve, pool, sp = fw.pe, fw.act, fw.dve, fw.pool, fw.sp
    st = k.st
    last = (l == DEPTH - 1)
    tiles = list(range(2 if last else 0, NT))
    V = nc.vector
    IOA = bass.IndirectOffsetOnAxis
    fw.barrier()
    with contextlib.ExitStack() as es:
        sb = lambda n_, shp, dt=F32: es.enter_context(nc.sbuf_tensor(f"{n_}_L{l}", list(shp), dt))
        pst = lambda n_, shp, dt=F32: es.enter_context(nc.psum_tensor(f"{n_}_L{l}", list(shp), dt))
        thr = sb("thr", [128, 40]); bst = sb("bst", [128, NB]); pidx = sb("pidx", [128, 1]); utf = sb("utf", [128, 128])
        utri = sb("utri", [128, 128], BF16)
        B_t = Buf("moetab")
        fw.dma(sp, st["misc"], lambda: nc.sync.dma_start(out=thr[:], in_=I["m_thr"][:, :]), writes=[B_t])
        fw.dma(sp, st["misc"], lambda: nc.sync.dma_start(out=bst[:], in_=I["m_bst"][:, :]), writes=[B_t])
        fw.dma(sp, st["misc"], lambda: nc.sync.dma_start(out=pidx[:], in_=I["m_pidx"][:, :]), writes=[B_t])
        fw.dma(sp, st["misc"], lambda: nc.sync.dma_start(out=utf[:], in_=I["m_utri"][:, :]), writes=[B_t])
        fw.op(dve, lambda: V.tensor_copy(out=utri[:], in_=utf[:]), writes=[B_t])
        cntp = pst("cntp", [128, 32]); rkp = [pst(f"rkp{i}", [128, 64]) for i in range(2)]
        cnt = sb("cnt", [128, 32]); cmp3 = sb("cmp3", [128, 32, 40]); nblk = sb("nblk", [128, 32]); pend = sb("pend", [128, 32])
        ones32 = sb("ones32", [128, 32]); cacc = sb("cacc", [128, 32]); cmp4 = sb("cmp4", [128, NB, 32]); bexp = sb("bexp", [128, NB])
        tmp = sb("rtmp", [128, 32]); prod = sb("rprod", [128, 32]); destf = sb("destf", [128, NT, 2])
        B_c = Buf("cnt"); B_cntp = Buf("cntp"); B_rkp = [Buf("rkp0"), Buf("rkp1")]; B_cacc = Buf("cacc"); B_tmp = Buf("rtmp"); B_df = Buf("destf")
        for i, gt in enumerate(tiles):
            fw.op(pe, lambda i=i, gt=gt: nc.tensor.matmul(cntp[:], lhsT=k.ones_b[:], rhs=k.OH12[:, gt, :], start=(i == 0), stop=(i == len(tiles) - 1)),
                  reads=[k.B_G, k.B_const], writes=[B_cntp])
        fw.op(dve, lambda: V.tensor_copy(out=cnt[:], in_=cntp[:]), reads=[B_cntp], writes=[B_c])
        fw.op(dve, lambda: V.tensor_tensor(out=cmp3[:], in0=bc(cnt[:, :].unsqueeze(2), [128, 32, 40]), in1=bc(thr[:, :].unsqueeze(1), [128, 32, 40]),
                                           op=ALU.is_gt), reads=[B_t], writes=[B_c])
        fw.op(dve, lambda: V.tensor_reduce(out=nblk[:], in_=cmp3[:], axis=AX.X, op=ALU.add), writes=[B_c])
        fw.op(dve, lambda: V.tensor_scalar(out=nblk[:], in0=nblk[:], scalar1=float(BLK), scalar2=None, op0=ALU.mult), writes=[B_c])
        fw.op(dve, lambda: V.memset(ones32[:], 1.0), writes=[B_c])
        fw.op(dve, lambda: V.tensor_tensor_scan(out=pend[:], data0=ones32[:], data1=nblk[:], initial=0.0, op0=ALU.mult, op1=ALU.add), writes=[B_c])
        fw.op(dve, lambda: V.tensor_tensor(out=cacc[:], in0=pend[:], in1=nblk[:], op=ALU.subtract), reads=[B_c], writes=[B_cacc])
        fw.op(dve, lambda: V.tensor_tensor(out=cmp4[:], in0=bc(pend[:, :].unsqueeze(1), [128, NB, 32]), in1=bc(bst[:, :].unsqueeze(2), [128, NB, 32]),
                                           op=ALU.is_le), reads=[B_t], writes=[B_c])
        fw.op(dve, lambda: V.tensor_reduce(out=bexp[:], in_=cmp4[:], axis=AX.X, op=ALU.add), writes=[B_c])
        fw.op(dve, lambda: V.tensor_scalar(out=bexp[:], in0=bexp[:], scalar1=float(NE - 1), scalar2=128.0, op0=ALU.min, op1=ALU.mult), writes=[B_c])
        fw.op(dve, lambda: V.tensor_scalar(out=bexp[:], in0=bexp[:], scalar1=pidx[:, 0:1], scalar2=float(l * NE * 128), op0=ALU.add, op1=ALU.add),
              reads=[B_t], writes=[B_c])
        fw.op(dve, lambda: V.tensor_copy(out=k.IDXW[:], in_=bexp[:]), reads=[B_c], writes=[k.B_IDXW])
        for i, gt in enumerate(tiles):
            s_ = i % 2
            fw.op(pe, lambda s_=s_, gt=gt: nc.tensor.matmul(rkp[s_][:, 0:32], lhsT=utri[:], rhs=k.OH12[:, gt, :], start=True, stop=True),
                  reads=[k.B_G, B_t], writes=[B_rkp[s_]])
            fw.op(pe, lambda s_=s_, gt=gt: nc.tensor.matmul(rkp[s_][:, 32:64], lhsT=k.ones_b[:], rhs=k.OH12[:, gt, :], start=True, stop=True),
                  reads=[k.B_G, k.B_const], writes=[B_rkp[s_]])
            fw.op(dve, lambda s_=s_: V.tensor_tensor(out=tmp[:], in0=rkp[s_][:, 0:32], in1=cacc[:], op=ALU.add), reads=[B_rkp[s_], B_cacc], writes=[B_tmp])
            for k_, OH in enumerate((k.OH1, k.OH2)):
                fw.op(dve, lambda OH=OH, gt=gt: V.tensor_tensor(out=prod[:], in0=tmp[:], in1=OH[:, gt, :], op=ALU.mult), reads=[k.B_G], writes=[B_tmp])
                fw.op(dve, lambda k_=k_, gt=gt: V.tensor_reduce(out=destf[:, gt, k_:k_ + 1], in_=prod[:], axis=AX.X, op=ALU.add), reads=[B_tmp], writes=[B_df])
            fw.op(dve, lambda s_=s_: V.tensor_tensor(out=cacc[:], in0=cacc[:], in1=rkp[s_][:, 32:64], op=ALU.add), reads=[B_rkp[s_], B_tmp], writes=[B_cacc])
        t0_ = tiles[0]
        fw.op(dve, lambda: V.tensor_copy(out=k.DEST[:, t0_:NT, :], in_=destf[:, t0_:NT, :]), reads=[B_df], writes=[k.B_DEST])
        if "DBG" in k.debug:
            dbg = nc.dram_tensor("DBG", [128, 512], F32, kind="ExternalOutput").ap()
            dbt = sb("dbt", [128, 512])
            B_dbt = Buf("dbt")
            fw.op(dve, lambda: V.memset(dbt[:], 0.0), writes=[B_dbt])
            fw.op(dve, lambda: V.tensor_copy(out=dbt[:, 0:32], in_=cnt[:]), reads=[B_c], writes=[B_dbt])
            fw.op(dve, lambda: V.tensor_copy(out=dbt[:, 32:64], in_=nblk[:]), reads=[B_c], writes=[B_dbt])
            fw.op(dve, lambda: V.tensor_copy(out=dbt[:, 64:96], in_=pend[:]), reads=[B_c], writes=[B_dbt])
            fw.op(dve, lambda: V.tensor_copy(out=dbt[:, 96:96 + NB], in_=k.IDXW[:]), reads=[k.B_IDXW], writes=[B_dbt])
            fw.op(dve, lambda: V.tensor_copy(out=dbt[:, 200:200 + 2 * NT], in_=k.DEST[:].rearrange("p a b -> p (a b)")), reads=[k.B_DEST], writes=[B_dbt])
            fw.op(dve, lambda: V.tensor_copy(out=dbt[:, 340:340 + 2 * NT], in_=destf[:].rearrange("p a b -> p (a b)")), reads=[B_df], writes=[B_dbt])
            fw.dma(sp, st["misc"], lambda: nc.sync.dma_start(out=dbg[:, :], in_=dbt[:]), reads=[B_dbt])
        if "moe_noscatter" in k.debug:
            tiles_sc = []
        else:
            tiles_sc = tiles
        h2t = [sb(f"h2t{i}", [128, D], BF16) for i in range(3)]
        B_h2t = [Buf(f"h2t{i}") for i in range(3)]
        k.B_XS = Buf("XS")
        for i, gt in enumerate(tiles_sc):
            s_ = i % 3
            fw.dma(sp, st[f"ld{'ABC'[s_]}0"], lambda s_=s_, gt=gt: nc.sync.dma_start(out=h2t[s_][:], in_=Sc["H2"][gt * 128:(gt + 1) * 128, :]),
                   reads=[k.B_H2], writes=[B_h2t[s_]])
            for k_ in range(2):
                fw.dma(pool, st[f"st{'ABC'[s_]}0"], lambda s_=s_, gt=gt, k_=k_: nc.gpsimd.indirect_dma_start(
                    out=Sc["XS"][:, :], out_offset=IOA(ap=k.DEST[:, gt, k_:k_ + 1], axis=0), in_=h2t[s_][:], in_offset=None,
                    bounds_check=k.reg_slot, oob_is_err=False),
                    reads=[B_h2t[s_], k.B_DEST], writes=[k.B_XS])
        fw.barrier()
    if f"stop_moeB{l}" in k.debug:
        return
    with contextlib.ExitStack() as es:
        sb = lambda n_, shp, dt=F32: es.enter_context(nc.sbuf_tensor(f"{n_}_L{l}", list(shp), dt))
        pst = lambda n_, shp, dt=F32: es.enter_context(nc.psum_tensor(f"{n_}_L{l}", list(shp), dt))
        w1b = [sb(f"w1b{i}", [128, 8, DE], BF16) for i in range(2)]
        w3b = [sb(f"w3b{i}", [128, 8, DE], BF16) for i in range(2)]
        w2b = [sb(f"w2b{i}", [128, 4, D], BF16) for i in range(2)]
        xs = [sb(f"xs{i}", [128, 4, D], BF16) for i in range(2)]
        xT = [sb(f"xT{i}", [128, 8, BLK], BF16) for i in range(2)]
        sl = [sb(f"sl{i}", [128, BLK]) for i in range(2)]
        hT = [sb(f"hT6_{i}", [128, 4, BLK], BF16) for i in range(2)]
        ys = [sb(f"ys{i}", [128, 4, D]) for i in range(2)]
        txp = [pst(f"txp{i}", [128, 8, 128], BF16) for i in range(2)]
        h1p = [pst(f"h1p{i}", [128, BLK]) for i in range(2)]
        h3p = [pst(f"h3p{i}", [128, BLK]) for i in range(2)]
        yp = pst("yp", [128, D])
        B_wt = [Buf("wt0"), Buf("wt1")]; B_xs = [Buf("xs0"), Buf("xs1")]; B_xT = [Buf("xT0"), Buf("xT1")]
        B_sl = [Buf("sl0"), Buf("sl1")]; B_hT = [Buf("hT0"), Buf("hT1")]; B_ys = [Buf("ys0"), Buf("ys1")]
        B_txp = [Buf("txp0"), Buf("txp1")]; B_h1p = [Buf("h1p0"), Buf("h1p1")]; B_h3p = [Buf("h3p0"), Buf("h3p1")]; B_yp = Buf("yp")
        k.B_YS = Buf("YS")
        w1v = I["w1"].rearrange("l e (p j) f -> (l e p) (j f)", j=8)
        w3v = I["w3"].rearrange("l e (p j) f -> (l e p) (j f)", j=8)
        w2v = I["w2"].rearrange("l e (p j) n -> (l e p) (j n)", j=4)
        wst = [sb(f"wst{i}", [128, 4096]) for i in range(3)]
        B_wst = [Buf(f"wst{i}") for i in range(3)]

        def load_blk(b):
            ws = b % 2
            for wi, (dst, src) in enumerate(((w1b[ws], w1v), (w3b[ws], w3v), (w2b[ws], w2v))):
                fw.dma(pool, st[f"w{wi}"], lambda wi=wi, src=src: nc.gpsimd.indirect_dma_start(
                    out=wst[wi][:], out_offset=None, in_=src, in_offset=IOA(ap=k.IDXW[:, b:b + 1], axis=0),
                    bounds_check=k.reg_w, oob_is_err=False),
                    reads=[k.B_IDXW], writes=[B_wst[wi]])
                dflat = dst[:].rearrange("p a b -> p (a b)")
                if wi == 0:
                    fw.op(pool, lambda dflat=dflat, wi=wi: nc.gpsimd.tensor_copy(out=dflat, in_=wst[wi][:]), reads=[B_wst[wi]], writes=[B_wt[ws]])
                elif wi == 1:
                    fw.op(act, lambda dflat=dflat, wi=wi: nc.scalar.copy(out=dflat, in_=wst[wi][:]), reads=[B_wst[wi]], writes=[B_wt[ws]])
                else:
                    fw.op(dve, lambda dflat=dflat, wi=wi: V.tensor_copy(out=dflat, in_=wst[wi][:]), reads=[B_wst[wi]], writes=[B_wt[ws]])
            fw.dma(sp, st[f"ldA{ws}"], lambda: nc.sync.dma_start(
                out=xs[ws][:], in_=Sc["XS"][b * BLK:(b + 1) * BLK, :].rearrange("(t p) d -> p t d", p=128)),
                reads=[k.B_XS], writes=[B_xs[ws]])

        nblocks = NB if "moe_nb" not in k.debug else 2
        load_blk(0)
        tcount = 0
        for b in range(nblocks):
            ws = b % 2
            if b + 1 < nblocks:
                load_blk(b + 1)
            for t in range(4):
                ts_ = tcount % 2
                tcount += 1
                for j in range(8):
                    fw.op(pe, lambda ts_=ts_, j=j, t=t, ws=ws: nc.tensor.transpose(
                        txp[ts_][:, j, :], xs[ws][:, t, :].rearrange("p (q j) -> p j q", j=8)[:, j, :], k.ident_b[:]),
                        reads=[B_xs[ws], k.B_const], writes=[B_txp[ts_]])
                if t % 2 == 0:
                    fw.op(act, lambda ts_=ts_, t=t, ws=ws: nc.scalar.copy(out=xT[ws][:, :, t * 128:(t + 1) * 128], in_=txp[ts_][:]),
                          reads=[B_txp[ts_]], writes=[B_xT[ws]])
                else:
                    fw.op(dve, lambda ts_=ts_, t=t, ws=ws: V.tensor_copy(out=xT[ws][:, :, t * 128:(t + 1) * 128], in_=txp[ts_][:]),
                          reads=[B_txp[ts_]], writes=[B_xT[ws]])
            for fc in range(4):
                ps_ = fc % 2
                for kk in range(8):
                    fw.op(pe, lambda ps_=ps_, kk=kk, fc=fc, ws=ws: nc.tensor.matmul(
                        h1p[ps_][:], lhsT=w1b[ws][:, kk, :].rearrange("p (q j) -> p j q", j=4)[:, fc, :], rhs=xT[ws][:, kk, :],
                        start=(kk == 0), stop=(kk == 7)), reads=[B_wt[ws], B_xT[ws]], writes=[B_h1p[ps_]])
                for kk in range(8):
                    fw.op(pe, lambda ps_=ps_, kk=kk, fc=fc, ws=ws: nc.tensor.matmul(
                        h3p[ps_][:], lhsT=w3b[ws][:, kk, :].rearrange("p (q j) -> p j q", j=4)[:, fc, :], rhs=xT[ws][:, kk, :],
                        start=(kk == 0), stop=(kk == 7)), reads=[B_wt[ws], B_xT[ws]], writes=[B_h3p[ps_]])
                fw.op(act, lambda ps_=ps_: nc.scalar.activation(out=sl[ps_][:], in_=h1p[ps_][:], func=AF.Silu), reads=[B_h1p[ps_]], writes=[B_sl[ps_]])
                fw.op(dve, lambda ps_=ps_, fc=fc, ws=ws: V.tensor_tensor(out=hT[ws][:, fc, :], in0=h3p[ps_][:], in1=sl[ps_][:], op=ALU.mult),
                      reads=[B_h3p[ps_], B_sl[ps_]], writes=[B_hT[ws]])
            for t in range(4):
                for hf_ in range(2):
                    for fc in range(4):
                        fw.op(pe, lambda hf_=hf_, fc=fc, t=t, ws=ws: nc.tensor.matmul(
                            yp[:, hf_ * 512:(hf_ + 1) * 512], lhsT=hT[ws][:, fc, t * 128:(t + 1) * 128],
                            rhs=w2b[ws][:, fc, hf_ * 512:(hf_ + 1) * 512], start=(fc == 0), stop=(fc == 3)),
                            reads=[B_hT[ws], B_wt[ws]], writes=[B_yp])
                if t % 2 == 0:
                    fw.op(act, lambda t=t, ws=ws: nc.scalar.copy(out=ys[ws][:, t, :], in_=yp[:]), reads=[B_yp], writes=[B_ys[ws]])
                else:
                    fw.op(dve, lambda t=t, ws=ws: V.tensor_copy(out=ys[ws][:, t, :], in_=yp[:]), reads=[B_yp], writes=[B_ys[ws]])
            fw.dma(sp, st[f"stA{ws}"], lambda b=b, ws=ws: nc.sync.dma_start(
                out=Sc["YS"][b * BLK:(b + 1) * BLK, :].rearrange("(t p) d -> p t d", p=128), in_=ys[ws][:]),
                reads=[B_ys[ws]], writes=[k.B_YS])
        fw.barrier()
    if f"stop_moeC{l}" in k.debug:
        return
    with contextlib.ExitStack() as es:
        sb = lambda n_, shp, dt=F32: es.enter_context(nc.sbuf_tensor(f"{n_}_L{l}", list(shp), dt))
        g2r = sb("g2r", [128, 2, D])
        ya = [sb(f"ya{i}", [128, D]) for i in range(2)]
        yb = [sb(f"yb{i}", [128, D]) for i in range(2)]
        xt = [sb(f"x6_{i}", [128, D]) for i in range(2)]
        B_g2r = Buf("g2r"); B_ya = [Buf("ya0"), Buf("ya1")]; B_yb = [Buf("yb0"), Buf("yb1")]; B_xt = [Buf("x60"), Buf("x61")]
        for r in range(2):
            fw.dma(sp, st["misc"], lambda r=r: nc.sync.dma_start(out=g2r[:, r, :], in_=Sc["MOD"][l, r, 5 * D:6 * D].partition_broadcast(128)),
                   reads=[k.B_MOD], writes=[B_g2r])
        for i, gt in enumerate(tiles):
            s_ = i % 2
            r = 1 if gt < 2 else 0
            fw.dma(pool, st[f"ldA{s_}"], lambda s_=s_, gt=gt: nc.gpsimd.indirect_dma_start(
                out=ya[s_][:], out_offset=None, in_=Sc["YS"][:, :], in_offset=IOA(ap=k.DEST[:, gt, 0:1], axis=0),
                bounds_check=k.reg_slot, oob_is_err=False),
                reads=[k.B_YS, k.B_DEST], writes=[B_ya[s_]])
            fw.dma(pool, st[f"ldB{s_}"], lambda s_=s_, gt=gt: nc.gpsimd.indirect_dma_start(
                out=yb[s_][:], out_offset=None, in_=Sc["YS"][:, :], in_offset=IOA(ap=k.DEST[:, gt, 1:2], axis=0),
                bounds_check=k.reg_slot, oob_is_err=False),
                reads=[k.B_YS, k.B_DEST], writes=[B_yb[s_]])
            fw.dma(sp, st[f"ldC{s_}"], lambda s_=s_, gt=gt: nc.sync.dma_start(out=xt[s_][:], in_=Sc["XR"][gt * 128:(gt + 1) * 128, :]),
                   reads=[k.B_XR[gt]], writes=[B_xt[s_]])
            fw.op(dve, lambda s_=s_, gt=gt: V.tensor_scalar(out=ya[s_][:], in0=ya[s_][:], scalar1=k.W12[:, gt, 0:1], scalar2=None, op0=ALU.mult),
                  reads=[k.B_G], writes=[B_ya[s_]])
            fw.op(dve, lambda s_=s_, gt=gt: V.scalar_tensor_tensor(out=ya[s_][:], in0=yb[s_][:], scalar=k.W12[:, gt, 1:2], in1=ya[s_][:],
                                                                   op0=ALU.mult, op1=ALU.add), reads=[B_yb[s_], k.B_G], writes=[B_ya[s_]])
            fw.op(act, lambda: nc.scalar.copy(out=yb[s_][:, 0:1], in_=yb[s_][:, 0:1]), writes=[B_yb[s_]]) if False else None
            fw.op(dve, lambda s_=s_, r=r: V.tensor_tensor(out=ya[s_][:], in0=ya[s_][:], in1=g2r[:, r, :], op=ALU.mult), reads=[B_g2r], writes=[B_ya[s_]])
            fw.op(dve, lambda s_=s_: V.tensor_tensor(out=xt[s_][:], in0=xt[s_][:], in1=ya[s_][:], op=ALU.add), reads=[B_ya[s_]], writes=[B_xt[s_]])
            if last:
                xrow = gt * 128 - C
                fw.dma(sp, st[f"stA{s_}"], lambda s_=s_, xrow=xrow: nc.sync.dma_start(out=k.out[xrow:xrow + 128, :], in_=xt[s_][:]),
                       reads=[B_xt[s_]], writes=[k.B_OUT])
            else:
                fw.dma(sp, st[f"stA{s_}"], lambda s_=s_, gt=gt: nc.sync.dma_start(out=Sc["XR"][gt * 128:(gt + 1) * 128, :], in_=xt[s_][:]),
                       reads=[B_xt[s_]], writes=[k.B_XR[gt]])
        fw.barrier()
```
